# Optimizing a Trainium2 kernel written in Bass

```python
import jax, jax.numpy as jnp
from jax import lax
import numpy as np

D_MODEL = 2048
BATCH = 2
SEQ = 4096
DEPTH = 2

GRID_W = 64
CTX_LEN = 256
HEAD_DIM = 128
N_HEADS = D_MODEL // HEAD_DIM
A_Q_HEADS = N_HEADS // 2
A_KV_HEADS = max(1, A_Q_HEADS // 4)
B_Q_HEADS = N_HEADS - A_Q_HEADS
B_KV_HEADS = max(1, B_Q_HEADS // 4)
C_HEADS = N_HEADS
Q_BLOCK = 128
WINDOW = 128
NA_KH = 8
NA_KW = 16
D_FF = 4 * D_MODEL
ROPE_THETA = 10000.0
ROPE_PAIRS = HEAD_DIM // 4
NORM_EPS = 1e-6
N_EVEN = (DEPTH + 1) // 2
N_ODD = DEPTH // 2
EVEN_IN = (A_Q_HEADS + 2 * A_KV_HEADS + B_Q_HEADS + 2 * B_KV_HEADS) * HEAD_DIM
ODD_IN = 3 * C_HEADS * HEAD_DIM
NEG = -1e30

kernel_name = 'hybrid_dit_gqa_swa_natten'


def _rmsnorm(x, w):
    xf = x.astype(jnp.float32)
    y = xf * lax.rsqrt(jnp.mean(xf * xf, axis=-1, keepdims=True) + NORM_EPS) * w.astype(jnp.float32)
    return y.astype(x.dtype)


def _modulate(h, shift, scale):
    return h * (1 + scale) + shift


def _rope_tables(S):
    t = jnp.arange(S)
    row = (t // GRID_W).astype(jnp.float32)
    col = (t % GRID_W).astype(jnp.float32)
    inv = ROPE_THETA ** (-jnp.arange(ROPE_PAIRS, dtype=jnp.float32) / ROPE_PAIRS)
    ang_r = row[:, None] * inv
    ang_c = col[:, None] * inv
    ang = jnp.concatenate([ang_r, ang_r, ang_c, ang_c], axis=-1)
    return jnp.cos(ang), jnp.sin(ang)


def _rot_half(u):
    u1, u2 = jnp.split(u, 2, axis=-1)
    return jnp.concatenate([-u2, u1], axis=-1)


def _apply_rope(x, cos, sin):
    xr, xc = jnp.split(x, 2, axis=-1)
    xrot = jnp.concatenate([_rot_half(xr), _rot_half(xc)], axis=-1)
    return (x.astype(jnp.float32) * cos + xrot.astype(jnp.float32) * sin).astype(x.dtype)


def _heads(x, n):
    B, N, _ = x.shape
    return x.reshape(B, N, n, HEAD_DIM).transpose(0, 2, 1, 3)


def _q_groups(x, n_kv):
    B, N, F = x.shape
    g = F // HEAD_DIM // n_kv
    return x.reshape(B, N, n_kv, g, HEAD_DIM).transpose(0, 2, 3, 1, 4)


def _merge(o):
    B, K, G, N, dh = o.shape
    return o.transpose(0, 3, 1, 2, 4).reshape(B, N, K * G * dh)


def _dense_attn(q, k, v, sink=None):
    B, K, G, N, dh = q.shape
    s = jnp.einsum('bkgnd,bkmd->bkgnm', q, k).astype(jnp.float32) * (dh ** -0.5)
    M = k.shape[2]
    if sink is not None:
        sk = jnp.broadcast_to(sink.astype(jnp.float32)[None, :, :, None, None], (B, K, G, N, 1))
        s = jnp.concatenate([s, sk], axis=-1)
    p = jax.nn.softmax(s, axis=-1)[..., :M].astype(v.dtype)
    return jnp.einsum('bkgnm,bkmd->bkgnd', p, v)


def _global_attn(q, k_all, v_all):
    B, K, G, S, dh = q.shape
    nblk = S // Q_BLOCK
    qb = jnp.moveaxis(q.reshape(B, K, G, nblk, Q_BLOCK, dh), 3, 0)

    def block(qi):
        s = jnp.einsum('bkgqd,bkmd->bkgqm', qi, k_all).astype(jnp.float32) * (dh ** -0.5)
        p = jax.nn.softmax(s, axis=-1).astype(v_all.dtype)
        return jnp.einsum('bkgqm,bkmd->bkgqd', p, v_all)

    o = lax.map(block, qb)
    return jnp.moveaxis(o, 0, 3).reshape(B, K, G, S, dh)


def _window_attn(q, k, v, k_ctx, v_ctx, sink):
    B, K, G, S, dh = q.shape
    nblk = S // Q_BLOCK
    pad = ((0, 0), (0, 0), (WINDOW, WINDOW), (0, 0))
    kp = jnp.pad(k, pad).reshape(B, K, nblk + 2, Q_BLOCK, dh)
    vp = jnp.pad(v, pad).reshape(B, K, nblk + 2, Q_BLOCK, dh)
    kw = jnp.concatenate([kp[:, :, 0:nblk], kp[:, :, 1:nblk + 1], kp[:, :, 2:nblk + 2]], axis=3)
    vw = jnp.concatenate([vp[:, :, 0:nblk], vp[:, :, 1:nblk + 1], vp[:, :, 2:nblk + 2]], axis=3)
    qb = q.reshape(B, K, G, nblk, Q_BLOCK, dh)
    scale = dh ** -0.5
    s_win = jnp.einsum('bkgnqd,bknjd->bkgnqj', qb, kw).astype(jnp.float32) * scale
    blk = jnp.arange(nblk)[:, None]
    qpos = blk * Q_BLOCK + jnp.arange(Q_BLOCK)[None, :]
    kpos = blk * Q_BLOCK - WINDOW + jnp.arange(3 * Q_BLOCK)[None, :]
    kk = kpos[:, None, :]
    valid = (kk >= 0) & (kk < S) & (jnp.abs(kk - qpos[:, :, None]) <= WINDOW)
    s_win = jnp.where(valid, s_win, NEG)
    s_ctx = jnp.einsum('bkgnqd,bkcd->bkgnqc', qb, k_ctx).astype(jnp.float32) * scale
    s_sink = jnp.broadcast_to(sink.astype(jnp.float32)[None, :, :, None, None, None],
                              (B, K, G, nblk, Q_BLOCK, 1))
    p = jax.nn.softmax(jnp.concatenate([s_ctx, s_win, s_sink], axis=-1), axis=-1)
    C = k_ctx.shape[2]
    p_ctx = p[..., :C].astype(v.dtype)
    p_win = p[..., C:C + 3 * Q_BLOCK].astype(v.dtype)
    o = (jnp.einsum('bkgnqc,bkcd->bkgnqd', p_ctx, v_ctx)
         + jnp.einsum('bkgnqj,bknjd->bkgnqd', p_win, vw))
    return o.reshape(B, K, G, S, dh)


def _even_mixer(h, hc, w_in, w_out, q_norm_w, k_norm_w, sink, cos, sin, need_ctx):
    sizes = [A_Q_HEADS, A_KV_HEADS, A_KV_HEADS, B_Q_HEADS, B_KV_HEADS, B_KV_HEADS]
    splits = np.cumsum([n * HEAD_DIM for n in sizes])[:-1].tolist()
    qa, ka, va, qb, kb, vb = jnp.split(h @ w_in, splits, axis=-1)
    qac, kac, vac, qbc, kbc, vbc = jnp.split(hc @ w_in, splits, axis=-1)
    qa = _apply_rope(_rmsnorm(_q_groups(qa, A_KV_HEADS), q_norm_w), cos, sin)
    ka = _apply_rope(_rmsnorm(_heads(ka, A_KV_HEADS), k_norm_w), cos, sin)
    va = _heads(va, A_KV_HEADS)
    kac = _rmsnorm(_heads(kac, A_KV_HEADS), k_norm_w)
    vac = _heads(vac, A_KV_HEADS)
    oa = _global_attn(qa, jnp.concatenate([kac, ka], axis=2), jnp.concatenate([vac, va], axis=2))
    sink_kg = sink.reshape(B_KV_HEADS, B_Q_HEADS // B_KV_HEADS)
    qb = _apply_rope(_q_groups(qb, B_KV_HEADS), cos, sin)
    kb = _apply_rope(_heads(kb, B_KV_HEADS), cos, sin)
    vb = _heads(vb, B_KV_HEADS)
    kbc = _heads(kbc, B_KV_HEADS)
    vbc = _heads(vbc, B_KV_HEADS)
    ob = _window_attn(qb, kb, vb, kbc, vbc, sink_kg)
    y = jnp.concatenate([_merge(oa), _merge(ob)], axis=-1) @ w_out
    if not need_ctx:
        return y, None
    qac = _rmsnorm(_q_groups(qac, A_KV_HEADS), q_norm_w)
    oac = _dense_attn(qac, kac, vac)
    obc = _dense_attn(_q_groups(qbc, B_KV_HEADS), kbc, vbc, sink_kg)
    yc = jnp.concatenate([_merge(oac), _merge(obc)], axis=-1) @ w_out
    return y, yc


def _odd_mixer(h, hc, w_in, w_out, rpb, need_ctx):
    B, S, _ = h.shape
    rows = S // GRID_W
    kh = min(NA_KH, rows)
    kw = NA_KW
    q, k, v = [_heads(t, C_HEADS) for t in jnp.split(h @ w_in, 3, axis=-1)]
    qc, kc, vc = [_heads(t, C_HEADS) for t in jnp.split(hc @ w_in, 3, axis=-1)]
    qg = q.reshape(B, C_HEADS, rows, GRID_W, HEAD_DIM)
    kg = k.reshape(B, C_HEADS, rows, GRID_W, HEAD_DIM)
    vg = v.reshape(B, C_HEADS, rows, GRID_W, HEAD_DIM)
    col = jnp.arange(GRID_W)
    cs = jnp.clip(col - kw // 2, 0, GRID_W - kw)
    col_valid = (col[None, :] >= cs[:, None]) & (col[None, :] < cs[:, None] + kw)
    ci = jnp.clip(col[None, :] - col[:, None] + NA_KW - 1, 0, 2 * NA_KW - 2)
    scale = HEAD_DIM ** -0.5
    C = kc.shape[2]

    def row_block(args):
        r, qr = args
        rs = jnp.clip(r - kh // 2, 0, rows - kh)
        kr = lax.dynamic_slice_in_dim(kg, rs, kh, axis=2)
        vr = lax.dynamic_slice_in_dim(vg, rs, kh, axis=2)
        ri = rs + jnp.arange(kh) - r + NA_KH - 1
        bias = rpb[:, ri[None, :, None], ci[:, None, :]]
        s_nb = jnp.einsum('bhqd,bhiwd->bhqiw', qr, kr).astype(jnp.float32) * scale
        s_nb = s_nb + bias[None].astype(jnp.float32)
        s_nb = jnp.where(col_valid[:, None, :], s_nb, NEG).reshape(B, C_HEADS, GRID_W, kh * GRID_W)
        s_ctx = jnp.einsum('bhqd,bhcd->bhqc', qr, kc).astype(jnp.float32) * scale
        p = jax.nn.softmax(jnp.concatenate([s_ctx, s_nb], axis=-1), axis=-1).astype(v.dtype)
        p_nb = p[..., C:].reshape(B, C_HEADS, GRID_W, kh, GRID_W)
        return (jnp.einsum('bhqc,bhcd->bhqd', p[..., :C], vc)
                + jnp.einsum('bhqiw,bhiwd->bhqd', p_nb, vr))

    o = lax.map(row_block, (jnp.arange(rows), jnp.moveaxis(qg, 2, 0)))
    o = jnp.moveaxis(o, 0, 2).reshape(B, C_HEADS, S, HEAD_DIM)
    y = _merge(o[:, :, None]) @ w_out
    if not need_ctx:
        return y, None
    oc = _dense_attn(qc[:, :, None], kc, vc)
    return y, _merge(oc) @ w_out


def _mlp(h, w1, w2):
    return jnp.square(jax.nn.relu(h @ w1)) @ w2


def setup_inputs(seed: int = 0) -> dict:
    key = jax.random.key(seed)
    ks = jax.random.split(key, 20)
    f32 = jnp.float32

    def nrm(k, shape, fan_in, gain=1.0):
        return gain * fan_in ** -0.5 * jax.random.normal(k, shape, f32)

    return {
        'x': jax.random.normal(ks[0], (BATCH, SEQ, D_MODEL), f32),
        'c': jax.random.normal(ks[1], (BATCH, D_MODEL), f32),
        'ctx': jax.random.normal(ks[2], (BATCH, CTX_LEN, D_MODEL), f32),
        'c_ctx': jax.random.normal(ks[3], (D_MODEL,), f32),
        'ada_w': nrm(ks[4], (DEPTH, D_MODEL, 6 * D_MODEL), D_MODEL, 0.5),
        'ada_b': 0.02 * jax.random.normal(ks[5], (DEPTH, 6 * D_MODEL), f32),
        'norm_w': 1.0 + 0.02 * jax.random.normal(ks[6], (DEPTH, 2, D_MODEL), f32),
        'mlp_w1': nrm(ks[7], (DEPTH, D_MODEL, D_FF), D_MODEL),
        'mlp_w2': nrm(ks[8], (DEPTH, D_FF, D_MODEL), D_FF),
        'ev_w_in': nrm(ks[9], (N_EVEN, D_MODEL, EVEN_IN), D_MODEL),
        'ev_w_out': nrm(ks[10], (N_EVEN, N_HEADS * HEAD_DIM, D_MODEL), N_HEADS * HEAD_DIM),
        'ev_q_norm': 1.0 + 0.02 * jax.random.normal(ks[11], (N_EVEN, HEAD_DIM), f32),
        'ev_k_norm': 1.0 + 0.02 * jax.random.normal(ks[12], (N_EVEN, HEAD_DIM), f32),
        'ev_sink': 0.5 * jax.random.normal(ks[13], (N_EVEN, B_Q_HEADS), f32),
        'od_w_in': nrm(ks[14], (N_ODD, D_MODEL, ODD_IN), D_MODEL),
        'od_w_out': nrm(ks[15], (N_ODD, C_HEADS * HEAD_DIM, D_MODEL), C_HEADS * HEAD_DIM),
        'od_rpb': 0.1 * jax.random.normal(ks[16], (N_ODD, C_HEADS, 2 * NA_KH - 1, 2 * NA_KW - 1), f32),
        'final_norm_w': 1.0 + 0.02 * jax.random.normal(ks[17], (D_MODEL,), f32),
    }


def reference(x, c, ctx, c_ctx, ada_w, ada_b, norm_w, mlp_w1, mlp_w2, ev_w_in, ev_w_out,
              ev_q_norm, ev_k_norm, ev_sink, od_w_in, od_w_out, od_rpb, final_norm_w):
    S = x.shape[1]
    cos, sin = _rope_tables(S)
    for i in range(DEPTH):
        need_ctx = i < DEPTH - 1
        mod = (jax.nn.silu(c) @ ada_w[i] + ada_b[i])[:, None, :]
        sh1, sc1, g1, sh2, sc2, g2 = jnp.split(mod, 6, axis=-1)
        modc = jax.nn.silu(c_ctx) @ ada_w[i] + ada_b[i]
        sh1c, sc1c, g1c, sh2c, sc2c, g2c = jnp.split(modc, 6, axis=-1)
        h = _modulate(_rmsnorm(x, norm_w[i, 0]), sh1, sc1)
        hc = _modulate(_rmsnorm(ctx, norm_w[i, 0]), sh1c, sc1c)
        if i % 2 == 0:
            j = i // 2
            y, yc = _even_mixer(h, hc, ev_w_in[j], ev_w_out[j], ev_q_norm[j], ev_k_norm[j],
                                ev_sink[j], cos, sin, need_ctx)
        else:
            j = i // 2
            y, yc = _odd_mixer(h, hc, od_w_in[j], od_w_out[j], od_rpb[j], need_ctx)
        x = x + g1 * y
        h = _modulate(_rmsnorm(x, norm_w[i, 1]), sh2, sc2)
        x = x + g2 * _mlp(h, mlp_w1[i], mlp_w2[i])
        if need_ctx:
            ctx = ctx + g1c * yc
            hc = _modulate(_rmsnorm(ctx, norm_w[i, 1]), sh2c, sc2c)
            ctx = ctx + g2c * _mlp(hc, mlp_w1[i], mlp_w2[i])
    return _rmsnorm(x, final_norm_w)
```

```python
import numpy as np
from contextlib import ExitStack
import concourse.bass as bass
import concourse.mybir as mybir
from concourse.bass_utils import run_bass_kernel_spmd

F32 = mybir.dt.float32
BF16 = mybir.dt.bfloat16
AF = mybir.ActivationFunctionType
ALU = mybir.AluOpType

P = 128
D = 2048
KC = 16
DFF = 8192
SEQ = 4096
CTXL = 256
NOWN = 1024
NEG = -1.0e30
EPS = 1e-6
SCALE = 128 ** -0.5


class Trk:
    def __init__(self, nc):
        self.nc = nc
        self.es = ExitStack()
        self.E = {'pe': nc.tensor, 'act': nc.scalar, 'dve': nc.vector, 'pool': nc.gpsimd, 'sp': nc.sync}
        self.sem = {}
        self.cnt = {}
        for k in self.E:
            self.sem[k] = self.es.enter_context(nc.semaphore('e_' + k))
            self.cnt[k] = 0
        self.seen = {k: {} for k in self.E}
        self.bw = {}
        self.br = {}
        self.dcount = {}

    def _need(self, eng, r, w):
        need = {}

        def add(ev):
            if ev is None:
                return
            s, v = ev
            if need.get(s, 0) < v:
                need[s] = v
        for b in r:
            add(self.bw.get(b))
        for b in w:
            add(self.bw.get(b))
            for s, v in self.br.get(b, {}).items():
                add((s, v))
        e = self.E[eng]
        for s, v in need.items():
            if eng == 'pe' and s == 'pe':
                continue
            if self.seen[eng].get(s, 0) >= v:
                continue
            e.wait_ge(self.sem[s], v)
            self.seen[eng][s] = v

    def _mark(self, ev, r, w):
        s, v = ev
        for b in r:
            d = self.br.setdefault(b, {})
            if d.get(s, 0) < v:
                d[s] = v
        for b in w:
            self.bw[b] = ev
            self.br[b] = {}

    def op(self, eng, fn, r=(), w=()):
        self._need(eng, r, w)
        inst = fn(self.E[eng])
        self.cnt[eng] += 1
        inst.then_inc(self.sem[eng], 1)
        self._mark((eng, self.cnt[eng]), r, w)

    def dma(self, eng, out, in_, dst, src=()):
        self._need(eng, src, (dst,))
        key = 'd_' + dst
        if key not in self.sem:
            self.sem[key] = self.es.enter_context(self.nc.semaphore(key.replace(':', '_')))
            self.dcount[key] = 0
        inst = self.E[eng].dma_start(out=out, in_=in_)
        self.dcount[key] += 16
        inst.then_inc(self.sem[key], 16)
        self._mark((key, self.dcount[key]), src, (dst,))

    def collective(self, fn, dst, src=()):
        self._need('pool', src, (dst,))
        key = 'c_' + dst
        if key not in self.sem:
            self.sem[key] = self.es.enter_context(self.nc.semaphore(key.replace(':', '_')))
            self.dcount[key] = 0
        inst = fn(self.E['pool'])
        self.dcount[key] += 1
        inst.then_inc(self.sem[key])
        self._mark((key, self.dcount[key]), src, (dst,))

    def barrier(self):
        evs = {}
        for k in self.E:
            if self.cnt[k] > 0:
                evs[k] = self.cnt[k]
        for k, v in self.dcount.items():
            if v > 0:
                evs[k] = v
        for eng in self.E:
            for s, v in evs.items():
                if s == eng and eng == 'pe':
                    continue
                if self.seen[eng].get(s, 0) >= v:
                    continue
                self.E[eng].wait_ge(self.sem[s], v)
                self.seen[eng][s] = v


def build(mode):
    nc = bass.Bass("TRN2", target_bir_lowering=False)
    t = Trk(nc)
    fused = (mode == 'fused')
    layers = [0, 1] if fused else [mode]

    keep = []

    def din0(name, shape):
        h = nc.dram_tensor(name, shape, F32, kind="ExternalInput")
        keep.append(h)
        return h.ap()

    cvec = din0("cvec", [D, 2]).rearrange("(c p) j -> p c j", p=P)
    carry = {}
    top = ExitStack()
    ARENA_W = 53000
    arena = top.enter_context(nc.sbuf_tensor("arena", [P, ARENA_W], F32))
    free_list = [(0, ARENA_W)]

    class Scope:
        def __init__(self):
            self.items = []

        def __enter__(self):
            return self

        def __exit__(self, *a):
            self.close()
            return False

        def close(self):
            for (o, n) in self.items:
                free_list.append((o, n))
            self.items = []
            free_list.sort()
            merged = []
            for (o, n) in free_list:
                if merged and merged[-1][0] + merged[-1][1] == o:
                    merged[-1] = (merged[-1][0], merged[-1][1] + n)
                else:
                    merged.append((o, n))
            free_list[:] = merged

    topS = Scope()

    def sb(es, name, shape, dt):
        nel = 1
        for d_ in shape[1:]:
            nel *= d_
        nw = (nel * (2 if dt == BF16 else 4) + 3) // 4
        nw = (nw + 15) // 16 * 16
        for i, (o, n) in enumerate(free_list):
            if n >= nw:
                free_list[i] = (o + nw, n - nw)
                es.items.append((o, nw))
                v = arena[:, o:o + nw]
                if dt == BF16:
                    v = v.bitcast(BF16)
                v = v[:, 0:nel]
                if len(shape) == 3:
                    v = v.rearrange("p (a b) -> p a b", a=shape[1])
                return v
        raise RuntimeError("arena full allocating %s %s; free=%s" % (name, shape, free_list))

    ps = [top.enter_context(nc.psum_tensor("ps%d" % i, [P, 512], F32)) for i in range(8)]
    ones = sb(topS, "ones", [P, P], BF16)
    epst = sb(topS, "epst", [P, 1], F32)
    modsL = {}
    for L_ in layers:
        modsL[L_] = dict(mod=sb(topS, "mod%d" % L_, [P, 96, 2], F32), se1=sb(topS, "se1_%d" % L_, [P, KC, 2], F32),
                         se2=sb(topS, "se2_%d" % L_, [P, KC, 2], F32))
    cur = {}
    t.op('dve', lambda e: e.memset(ones[:], 1.0), w=('ones',))
    t.op('dve', lambda e: e.memset(epst[:], EPS), w=('epst',))

    uid = {}

    def rr(n, key=None):
        key = key or ('k%d' % n)
        uid[key] = uid.get(key, -1) + 1
        return uid[key] % n

    sq = [sb(topS, "sq%d" % i, [P, 512], BF16) for i in range(4)]
    nr = [sb(topS, "nr%d" % i, [P, 512], F32) for i in range(2)]
    tmpf = [sb(topS, "tmpf%d" % i, [P, 512], F32) for i in range(3)]

    def rstd_from(ps_ap, n, inv_n, psbuf):
        i = rr(2, 'nr')
        t.op('act', lambda e: e.activation(out=nr[i][:, :n], in_=ps_ap, func=AF.Ln, bias=epst[:, 0:1], scale=inv_n),
             r=(psbuf, 'epst'), w=('nr%d' % i,))
        t.op('act', lambda e: e.activation(out=nr[i][:, :n], in_=nr[i][:, :n], func=AF.Exp, scale=-0.5),
             r=('nr%d' % i,), w=('nr%d' % i,))
        return nr[i][:, :n], 'nr%d' % i

    def norm_blk(x3, xbuf, n, j, se, shoff, h3, hbuf, psi=6):
        pb = 'ps%d' % psi
        for c in range(KC):
            i = c % 4
            if c % 3 != 2:
                t.op('act', lambda e, c=c, i=i: e.activation(out=sq[i][:, :n], in_=x3[:, c, :], func=AF.Square),
                     r=(xbuf,), w=('sq%d' % i,))
            else:
                t.op('dve', lambda e, c=c, i=i: e.tensor_tensor(out=sq[i][:, :n], in0=x3[:, c, :], in1=x3[:, c, :], op=ALU.mult),
                     r=(xbuf,), w=('sq%d' % i,))
            t.op('pe', lambda e, c=c, i=i: e.matmul(ps[psi][:, :n], lhsT=ones[:], rhs=sq[i][:, :n],
                                                     start=(c == 0), stop=(c == KC - 1)),
                 r=('sq%d' % i, 'ones'), w=(pb,))
        rs, rsb = rstd_from(ps[psi][:, :n], n, 1.0 / D, pb)
        for c in range(KC):
            i = rr(3, 'tmpf')
            t.op('dve', lambda e, c=c, i=i: e.scalar_tensor_tensor(out=tmpf[i][:, :n], in0=x3[:, c, :], scalar=se[:, c, j:j + 1],
                                                                   in1=rs, op0=ALU.mult, op1=ALU.mult),
                 r=(xbuf, rsb, 'se'), w=('tmpf%d' % i,))
            t.op('act', lambda e, c=c, i=i: e.activation(out=h3[:, c, :], in_=tmpf[i][:, :n], func=AF.Identity,
                                                         bias=cur['mod'][:, shoff + c, j:j + 1], scale=1.0),
                 r=('tmpf%d' % i, 'mod'), w=(hbuf,))

    pT = [sb(topS, "pT%d" % i, [P, 512], BF16) for i in range(5)]
    rdn = [sb(topS, "rdn%d" % i, [P, 512], F32) for i in range(2)]
    SB_ = [0, 1, 2, 7]
    ACC = [(3, 4), (5, 6)]
    jobn = [0]

    def attn_job(q_ap, qbuf, N, items, out_ap, outbuf, extra=None):
        assert items[0]['c0'] == 0 and items[0]['n'] == N
        for it in items:
            if 'subs' not in it:
                it['subs'] = [(it['kT'], it['kbuf'], it['v'], it['vbuf'])]
            assert len(it['subs']) * it['n'] <= 512
        ob, db = ACC[jobn[0] % 2]
        jobn[0] += 1
        nit = len(items)
        st = {}

        def emit_s(i):
            it = items[i]
            sbk = SB_[rr(4, 'sbk')]
            pi = rr(5, 'pT')
            n = it['n']
            c0 = it['c0']
            ns = len(it['subs'])
            w_ = ns * n
            for k_, (kT_, kb_, _, _) in enumerate(it['subs']):
                t.op('pe', lambda e: e.matmul(ps[sbk][:, k_ * n:(k_ + 1) * n], lhsT=kT_, rhs=q_ap[:, c0:c0 + n], start=True, stop=True),
                     r=(kb_, qbuf), w=('ps%d' % sbk,))
            if it.get('bias') is not None:
                k = rr(3, 'tmpf')
                t.op('dve', lambda e: e.scalar_tensor_tensor(out=tmpf[k][:, :w_], in0=ps[sbk][:, :w_], scalar=SCALE,
                                                             in1=it['bias'], op0=ALU.mult, op1=ALU.add),
                     r=('ps%d' % sbk, it['bbuf']), w=('tmpf%d' % k,))
                t.op('act', lambda e: e.activation(out=pT[pi][:, :w_], in_=tmpf[k][:, :w_], func=AF.Exp),
                     r=('tmpf%d' % k,), w=('pT%d' % pi,))
            else:
                t.op('act', lambda e: e.activation(out=pT[pi][:, :w_], in_=ps[sbk][:, :w_], func=AF.Exp, scale=SCALE),
                     r=('ps%d' % sbk,), w=('pT%d' % pi,))
            st[i] = pi

        def emit_pv(i):
            it = items[i]
            pi = st[i]
            n = it['n']
            c0 = it['c0']
            ns = len(it['subs'])
            for k_, (_, _, v_, vb_) in enumerate(it['subs']):
                first = (i == 0 and k_ == 0)
                last = (i == nit - 1 and k_ == ns - 1)
                t.op('pe', lambda e: e.matmul(ps[ob][:, c0:c0 + n], lhsT=v_, rhs=pT[pi][:, k_ * n:(k_ + 1) * n],
                                              start=first, stop=last, skip_group_check=True),
                     r=(vb_, 'pT%d' % pi), w=('ps%d' % ob,))
                t.op('pe', lambda e: e.matmul(ps[db][:, c0:c0 + n], lhsT=ones[:], rhs=pT[pi][:, k_ * n:(k_ + 1) * n],
                                              start=first, stop=last, skip_group_check=True),
                     r=('ones', 'pT%d' % pi), w=('ps%d' % db,))

        DEPTH = 3
        for i in range(min(DEPTH, nit)):
            emit_s(i)
        for i in range(nit):
            if i + DEPTH < nit:
                emit_s(i + DEPTH)
            emit_pv(i)
        k = rr(2, 'rdn')
        if extra is not None:
            t.op('act', lambda e: e.activation(out=rdn[k][:, :N], in_=ps[db][:, :N], func=AF.Ln, bias=extra, scale=1.0),
                 r=('ps%d' % db, 'sinke'), w=('rdn%d' % k,))
        else:
            t.op('act', lambda e: e.activation(out=rdn[k][:, :N], in_=ps[db][:, :N], func=AF.Ln),
                 r=('ps%d' % db,), w=('rdn%d' % k,))
        t.op('act', lambda e: e.activation(out=rdn[k][:, :N], in_=rdn[k][:, :N], func=AF.Exp, scale=-1.0),
             r=('rdn%d' % k,), w=('rdn%d' % k,))
        t.op('dve', lambda e: e.tensor_tensor(out=out_ap, in0=ps[ob][:, :N], in1=rdn[k][:, :N], op=ALU.mult),
             r=('ps%d' % ob, 'rdn%d' % k), w=(outbuf,))


    def emit_layer(layer):
        def din(name, shape):
            return din0(name + ("_%d" % layer if fused else ""), shape)
        L0 = (layer == 0)
        TEXT = SEQ if L0 else 1536
        OWN0 = 128 if L0 else 256
        NCQ = 64 if L0 else 0
        NTOK = NOWN + NCQ
        TBS = [(0, 512), (512, 512)] + ([(1024, 64)] if L0 else [])
        CIN = 3072 if L0 else 6144

        if fused and not L0:
            xT = carry['x1own'].ap().rearrange("(c p) t -> p c t", p=P)
            cT = None
            selv = din("selv", [P, 8])
        else:
            xT = din("xT", [D, TEXT]).rearrange("(c p) t -> p c t", p=P)
            cT = din("cT", [D, CTXL]).rearrange("(c p) t -> p c t", p=P)
        mod, se1, se2 = modsL[layer]['mod'], modsL[layer]['se1'], modsL[layer]['se2']
        cur['mod'] = mod
        if not fused:
            ada_w = din("ada_w", [D, 6 * D]).rearrange("(c p) n -> p c n", p=P)
            ada_b = din("ada_b", [P, 96])
            normw = din("normw", [P, 2, KC])
        w_in = din("w_in", [D, CIN]).rearrange("(c p) n -> p c n", p=P)
        w_out = din("w_out", [D, D]).rearrange("(c p) n -> p c n", p=P)
        w1 = din("w1", [D, DFF]).rearrange("(c p) n -> p c n", p=P)
        w2 = din("w2", [DFF, D]).rearrange("(m p) n -> p m n", p=P)
        if L0:
            qkw = din("qkw", [P, 2])
            sinkb = din("sinkb", [P, 8])
            cosT = din("cosT", [P, SEQ])
            sinT = din("sinT", [P, SEQ])
            rotm = din("rotm", [P, P])
            bandb = din("bandb", [P, 3, 384])
            if not fused:
                x1T = nc.dram_tensor("x1T", [D, NOWN], F32, kind="ExternalOutput").ap().rearrange("(c p) t -> p c t", p=P)
                c1T = nc.dram_tensor("c1T", [D, 64], F32, kind="ExternalOutput").ap().rearrange("(c p) t -> p c t", p=P)
            else:
                carry['x1own'] = nc.dram_tensor("x1own", [D, NOWN], F32)
        else:
            biasT = din("biasT", [16, P, 29, P])
            fnw = din("fnw", [P, KC])
            outT = nc.dram_tensor("outT", [D, NOWN], F32, kind="ExternalOutput").ap().rearrange("(c p) t -> p c t", p=P)

        if not fused:
            with Scope() as es:
                cv = sb(es, "cv", [P, KC, 2], F32)
                scv = sb(es, "scv", [P, KC, 2], BF16)
                adab = sb(es, "adab", [P, 96], F32)
                nw = sb(es, "nw", [P, 2, KC], F32)
                wa = [sb(es, "wa%d" % i, [P, KC, 1024], BF16) for i in range(2)]
                t.dma('sp', cv[:], cvec, 'cv')
                t.dma('sp', adab[:], ada_b, 'adab')
                t.dma('sp', nw[:], normw, 'nw')
                t.op('act', lambda e: e.activation(out=scv[:], in_=cv[:], func=AF.Silu), r=('cv',), w=('scv',))
                psm = ps[7]
                for g in range(12):
                    s = g % 2
                    for hlf in range(2):
                        t.dma('pool', wa[s][:, hlf * 8:(hlf + 1) * 8, :],
                              ada_w[:, hlf * 8:(hlf + 1) * 8, g * 1024:(g + 1) * 1024], 'wa%d' % s)
                    for cl in range(8):
                        cc = g * 8 + cl
                        for kc in range(KC):
                            t.op('pe', lambda e, s=s, cl=cl, kc=kc, cc=cc: e.matmul(
                                psm[:, 2 * cc:2 * cc + 2], lhsT=wa[s][:, kc, cl * P:(cl + 1) * P], rhs=scv[:, kc, :],
                                start=(kc == 0), stop=(kc == KC - 1)), r=('wa%d' % s, 'scv'), w=('ps7',))
                psm3 = psm[:, 0:192].rearrange("p (c j) -> p c j", j=2)
                for j in range(2):
                    t.op('dve', lambda e, j=j: e.tensor_tensor(out=mod[:, :, j], in0=psm3[:, :, j], in1=adab[:, :], op=ALU.add),
                         r=('ps7', 'adab'), w=('mod',))
                for j in range(2):
                    t.op('dve', lambda e, j=j: e.scalar_tensor_tensor(out=se1[:, :, j], in0=mod[:, 16:32, j], scalar=1.0,
                                                                       in1=nw[:, 0, :], op0=ALU.add, op1=ALU.mult),
                         r=('mod', 'nw'), w=('se1',))
                    t.op('dve', lambda e, j=j: e.scalar_tensor_tensor(out=se2[:, :, j], in0=mod[:, 64:80, j], scalar=1.0,
                                                                       in1=nw[:, 1, :], op0=ALU.add, op1=ALU.mult),
                         r=('mod', 'nw'), w=('se2',))
                t.barrier()
        SH1, G1, SH2, G2 = 0, 32, 48, 80

        att = Scope()
        ost = Scope()
        oT = None
        if L0:
            NKA = CTXL + SEQ
            NKB = CTXL + 1280
            kTA = sb(att, "kTA", [P, 2, NKA], BF16)
            VA = sb(att, "VA", [P, NKA // P, 256], BF16)
            kTB = sb(att, "kTB", [P, 2, NKB], BF16)
            VB = sb(att, "VB", [P, NKB // P, 256], BF16)
            hkeep = sb(att, "hkeep", [P, KC, NTOK], BF16)
            qkw_t = sb(att, "qkw_t", [P, 2], F32)
            sinke = sb(att, "sinke", [P, 8], F32)
            rot_t = sb(att, "rot_t", [P, P], BF16)
            band_t = sb(att, "band_t", [P, 3, 384], F32)
            qf = [sb(att, "qf%d" % i, [P, 512], F32) for i in range(2)]
            qn = [sb(att, "qn%d" % i, [P, 512], F32) for i in range(2)]
            qb = [sb(att, "qb%d" % i, [P, 512], BF16) for i in range(2)]
            t.dma('sp', qkw_t[:], qkw, 'qkw_t')
            t.dma('sp', sinke[:], sinkb, 'sinke')
            t.dma('pool', rot_t[:], rotm, 'rot_t')
            t.dma('sp', band_t[:], bandb, 'band_t')
            t.op('act', lambda e: e.activation(out=sinke[:], in_=sinke[:], func=AF.Exp), r=('sinke',), w=('sinke',))

            def qk_post(ps_ap, psbuf, n, nwcol, cs_ap, sn_ap, csbuf, out_ap, outbuf, pss=7):
                i = rr(2, 'qf')
                t.op('act', lambda e: e.activation(out=qf[i][:, :n], in_=ps_ap, func=AF.Copy), r=(psbuf,), w=('qf%d' % i,))
                src, srcb = qf[i], 'qf%d' % i
                if nwcol is not None:
                    s2 = rr(2, 'sq')
                    t.op('pool', lambda e: e.tensor_tensor(out=sq[s2][:, :n], in0=qf[i][:, :n], in1=qf[i][:, :n], op=ALU.mult),
                         r=('qf%d' % i,), w=('sq%d' % s2,))
                    t.op('pe', lambda e: e.matmul(ps[pss][:, :n], lhsT=ones[:], rhs=sq[s2][:, :n], start=True, stop=True),
                         r=('sq%d' % s2, 'ones'), w=('ps%d' % pss,))
                    rs, rsb = rstd_from(ps[pss][:, :n], n, 1.0 / P, 'ps%d' % pss)
                    t.op('dve', lambda e: e.scalar_tensor_tensor(out=qn[i][:, :n], in0=qf[i][:, :n], scalar=qkw_t[:, nwcol:nwcol + 1],
                                                                 in1=rs, op0=ALU.mult, op1=ALU.mult),
                         r=('qf%d' % i, rsb, 'qkw_t'), w=('qn%d' % i,))
                    src, srcb = qn[i], 'qn%d' % i
                if cs_ap is None:
                    t.op('pool', lambda e: e.tensor_copy(out=out_ap, in_=src[:, :n]), r=(srcb,), w=(outbuf,))
                    return
                t.op('pool', lambda e: e.tensor_copy(out=qb[i][:, :n], in_=src[:, :n]), r=(srcb,), w=('qb%d' % i,))
                t.op('pe', lambda e: e.matmul(ps[pss][:, :n], lhsT=rot_t[:], rhs=qb[i][:, :n], start=True, stop=True),
                     r=('qb%d' % i, 'rot_t'), w=('ps%d' % pss,))
                a = rr(3, 'tmpf')
                t.op('dve', lambda e: e.tensor_tensor(out=tmpf[a][:, :n], in0=ps[pss][:, :n], in1=sn_ap, op=ALU.mult),
                     r=('ps%d' % pss, csbuf), w=('tmpf%d' % a,))
                t.op('pool', lambda e: e.tensor_tensor(out=src[:, :n], in0=src[:, :n], in1=cs_ap, op=ALU.mult),
                     r=(srcb, csbuf), w=(srcb,))
                t.op('pool', lambda e: e.tensor_tensor(out=out_ap, in0=src[:, :n], in1=tmpf[a][:, :n], op=ALU.add),
                     r=(srcb, 'tmpf%d' % a), w=(outbuf,))

            with Scope() as es:
                NB = 256
                xb = [sb(es, "xb%d" % i, [P, KC, NB], F32) for i in range(2)]
                hb = [sb(es, "hb%d" % i, [P, KC, NB], BF16) for i in range(2)]
                csb = [sb(es, "csb%d" % i, [P, 2, NB], F32) for i in range(2)]
                wkv = sb(es, "wkv", [P, KC, 1024], BF16)
                for i, c0 in enumerate((1024, 1280, 2560, 2816)):
                    t.dma('pool', wkv[:, :, i * 256:(i + 1) * 256], w_in[:, :, c0:c0 + 256], 'wkv')
                blocks0a = [0, 1, 2, 3, 4, 16] if fused else list(range(17))
                slot0a = {b_: i_ % 2 for i_, b_ in enumerate(blocks0a)}

                def stage_norm(bi):
                    isc = (bi == 16)
                    j = 1 if isc else 0
                    s = slot0a[bi]
                    if isc:
                        t.dma('sp', xb[s][:], cT[:, :, 0:NB], 'xb%d' % s)
                    else:
                        t.dma('sp', xb[s][:], xT[:, :, bi * NB:(bi + 1) * NB], 'xb%d' % s)
                        t.dma('sp', csb[s][:, 0, :], cosT[:, bi * NB:(bi + 1) * NB], 'csb%d' % s)
                        t.dma('sp', csb[s][:, 1, :], sinT[:, bi * NB:(bi + 1) * NB], 'csb%d' % s)
                    norm_blk(xb[s], 'xb%d' % s, NB, j, se1, SH1, hb[s], 'hb%d' % s, psi=4)

                def run_interleaved(gens):
                    gens = list(gens)
                    while gens:
                        nxt = []
                        for g_ in gens:
                            try:
                                next(g_)
                                nxt.append(g_)
                            except StopIteration:
                                pass
                        gens = nxt

                def k_chain(ci, s, cb, kv, norm, rope, out_ap, outbuf):
                    kb, kh = ci // 2, ci % 2
                    psK = ps[kb][:, kh * NB:(kh + 1) * NB]
                    psKn = 'ps%d' % kb
                    psP = ps[5 + kb][:, kh * NB:(kh + 1) * NB]
                    psPn = 'ps%d' % (5 + kb)
                    qf_ = qf[kb][:, kh * NB:(kh + 1) * NB]
                    qn_ = qn[kb][:, kh * NB:(kh + 1) * NB]
                    qb_ = qb[kb][:, kh * NB:(kh + 1) * NB]
                    qfn, qnn, qbn = 'qf_%d' % ci, 'qn_%d' % ci, 'qb_%d' % ci
                    for kc in range(KC):
                        t.op('pe', lambda e: e.matmul(psK, lhsT=wkv[:, kc, cb + kv * P:cb + (kv + 1) * P], rhs=hb[s][:, kc, :],
                                                      start=(kc == 0), stop=(kc == KC - 1)), r=('wkv', 'hb%d' % s), w=(psKn,))
                    yield
                    t.op('act', lambda e: e.activation(out=qf_, in_=psK, func=AF.Copy), r=(psKn,), w=(qfn,))
                    yield
                    src, srcn, oth, othn = qf_, qfn, qn_, qnn
                    if norm:
                        t.op('dve', lambda e: e.tensor_tensor(out=qb_, in0=qf_, in1=qf_, op=ALU.mult), r=(qfn,), w=(qbn,))
                        yield
                        t.op('pe', lambda e: e.matmul(psP, lhsT=ones[:], rhs=qb_, start=True, stop=True), r=(qbn, 'ones'), w=(psPn,))
                        yield
                        t.op('act', lambda e: e.activation(out=qn_, in_=psP, func=AF.Ln, bias=epst[:, 0:1], scale=1.0 / P),
                             r=(psPn, 'epst'), w=(qnn,))
                        yield
                        t.op('act', lambda e: e.activation(out=qn_, in_=qn_, func=AF.Exp, scale=-0.5), r=(qnn,), w=(qnn,))
                        yield
                        t.op('dve', lambda e: e.scalar_tensor_tensor(out=qn_, in0=qf_, scalar=qkw_t[:, 1:2], in1=qn_,
                                                                     op0=ALU.mult, op1=ALU.mult), r=(qfn, qnn, 'qkw_t'), w=(qnn,))
                        yield
                        src, srcn, oth, othn = qn_, qnn, qf_, qfn
                    if not rope:
                        t.op('pool', lambda e: e.tensor_copy(out=out_ap, in_=src), r=(srcn,), w=(outbuf,))
                        return
                    t.op('act', lambda e: e.activation(out=qb_, in_=src, func=AF.Copy), r=(srcn,), w=(qbn,))
                    yield
                    t.op('pe', lambda e: e.matmul(psP, lhsT=rot_t[:], rhs=qb_, start=True, stop=True), r=(qbn, 'rot_t'), w=(psPn,))
                    yield
                    t.op('dve', lambda e: e.tensor_tensor(out=oth, in0=psP, in1=csb[s][:, 1, :], op=ALU.mult),
                         r=(psPn, 'csb%d' % s), w=(othn,))
                    yield
                    t.op('pool', lambda e: e.tensor_tensor(out=src, in0=src, in1=csb[s][:, 0, :], op=ALU.mult),
                         r=(srcn, 'csb%d' % s), w=(srcn,))
                    yield
                    t.op('pool', lambda e: e.tensor_tensor(out=out_ap, in0=src, in1=oth, op=ALU.add), r=(srcn, othn), w=(outbuf,))

                def v_chain(vi, s, cb, tt, out_ap, outbuf):
                    vb_, vh = 2 + vi // 2, vi % 2
                    psV = ps[vb_][:, vh * 256:(vh + 1) * 256]
                    psVn = 'ps%d' % vb_
                    for kc in range(KC):
                        t.op('pe', lambda e: e.matmul(psV, lhsT=hb[s][:, kc, tt * P:(tt + 1) * P], rhs=wkv[:, kc, cb + 256:cb + 512],
                                                      start=(kc == 0), stop=(kc == KC - 1)), r=('wkv', 'hb%d' % s), w=(psVn,))
                    yield
                    t.op('act', lambda e: e.activation(out=out_ap, in_=psV, func=AF.Copy), r=(psVn,), w=(outbuf,))

                def stage_kv(bi):
                    isc = (bi == 16)
                    s = slot0a[bi]
                    pos = 0 if isc else CTXL + bi * NB
                    doB = isc or bi < 5
                    gens = []
                    ci = 0
                    vi = 0
                    for kind in range(2 if doB else 1):
                        kT_t, kname, V_t, vname = ((kTA, 'kTA', VA, 'VA'), (kTB, 'kTB', VB, 'VB'))[kind]
                        cb = kind * 512
                        for kv in range(2):
                            gens.append(k_chain(ci, s, cb, kv, kind == 0, not isc, kT_t[:, kv, pos:pos + NB], kname))
                            ci += 1
                        for tt in range(NB // P):
                            gens.append(v_chain(vi, s, cb, tt, V_t[:, pos // P + tt, :], vname))
                            vi += 1
                    run_interleaved(gens)
                    if isc:
                        t.op('pool', lambda e: e.tensor_copy(out=hkeep[:, :, NOWN:NOWN + 64], in_=hb[s][:, :, 0:64]),
                             r=('hb%d' % s,), w=('hkeep',))
                    else:
                        lo = max(bi * NB, OWN0)
                        hi = min((bi + 1) * NB, OWN0 + NOWN)
                        if hi > lo:
                            t.op('pool', lambda e: e.tensor_copy(out=hkeep[:, :, lo - OWN0:hi - OWN0],
                                                                 in_=hb[s][:, :, lo - bi * NB:hi - bi * NB]),
                                 r=('hb%d' % s,), w=('hkeep',))

                stage_norm(blocks0a[0])
                for i_, bi in enumerate(blocks0a):
                    if i_ + 1 < len(blocks0a):
                        stage_norm(blocks0a[i_ + 1])
                    stage_kv(bi)
                t.barrier()

            oT = sb(ost, "oT", [P, KC, NTOK], BF16)
            with Scope() as es:
                wq = [sb(es, "wq%d" % i, [P, KC, 512], BF16) for i in range(2)]
                qg0 = sb(es, "qg0", [P, 4, NTOK], BF16)
                qg = [qg0, qg0]
                cso = sb(es, "cso", [P, 2, NOWN], F32)
                t.dma('sp', cso[:, 0, :], cosT[:, OWN0:OWN0 + NOWN], 'cso')
                t.dma('sp', cso[:, 1, :], sinT[:, OWN0:OWN0 + NOWN], 'cso')

                seq = [2, 3, 0, 1] if fused else [0, 1, 2, 3]
                gslot = {g_: i_ % 2 for i_, g_ in enumerate(seq)}

                def load_wq(g):
                    c0w = (g * 512) if g < 2 else (1536 + (g - 2) * 512)
                    for hlf in range(2):
                        t.dma('pool', wq[gslot[g]][:, hlf * 8:(hlf + 1) * 8, :], w_in[:, hlf * 8:(hlf + 1) * 8, c0w:c0w + 512], 'wq%d' % gslot[g])

                def exchange_kv():
                    ks_d = nc.dram_tensor("ksend", [256, NOWN], BF16)
                    kg_d = nc.dram_tensor("kgath", [4 * 256, NOWN], BF16)
                    vs_d = nc.dram_tensor("vsend", [NOWN, 256], BF16)
                    vg_d = nc.dram_tensor("vgath", [4 * NOWN, 256], BF16)
                    for kv_ in range(2):
                        t.dma('sp', ks_d.ap()[kv_ * P:(kv_ + 1) * P, :], kTA[:, kv_, CTXL + OWN0:CTXL + OWN0 + NOWN], 'dram:ksend', src=('kTA',))
                    t.dma('sp', vs_d.ap().rearrange("(t p) c -> p t c", p=P), VA[:, 3:11, :], 'dram:vsend', src=('VA',))
                    for nm, s_d, g_d in (('k', ks_d, kg_d), ('v', vs_d, vg_d)):
                        t.collective(lambda e, s_d=s_d, g_d=g_d: e.collective_compute(
                            "AllGather", ALU.bypass, replica_groups=[[0, 1, 2, 3], [4, 5, 6, 7]],
                            ins=[s_d.ap().opt()], outs=[g_d.ap().opt()]), 'dram:%sgath' % nm, src=('dram:%ssend' % nm,))
                    for r_ in range(4):
                        for kv_ in range(2):
                            t.dma('sp', kTA[:, kv_, CTXL + r_ * NOWN:CTXL + (r_ + 1) * NOWN],
                                  kg_d.ap()[r_ * 256 + kv_ * P:r_ * 256 + (kv_ + 1) * P, :], 'kTA', src=('dram:kgath',))
                    for r_ in range(4):
                        t.dma('sp', VA[:, 2 + r_ * 8:2 + (r_ + 1) * 8, :],
                              vg_d.ap()[r_ * NOWN:(r_ + 1) * NOWN, :].rearrange("(t p) c -> p t c", p=P), 'VA', src=('dram:vgath',))
                def qproj(g, hh):
                    isA = g < 2
                    s = gslot[g]
                    for (tb0, n) in TBS:
                        pk = rr(3, 'pkA')
                        for kc in range(KC):
                            t.op('pe', lambda e: e.matmul(
                                ps[pk][:, :n], lhsT=wq[s][:, kc, hh * P:(hh + 1) * P], rhs=hkeep[:, kc, tb0:tb0 + n],
                                start=(kc == 0), stop=(kc == KC - 1)), r=('wq%d' % s, 'hkeep'), w=('ps%d' % pk,))
                        lat = tb0 < NOWN
                        qk_post(ps[pk][:, :n], 'ps%d' % pk, n, (0 if isA else None),
                                cso[:, 0, tb0:tb0 + n] if lat else None, cso[:, 1, tb0:tb0 + n] if lat else None, 'cso',
                                qg0[:, hh, tb0:tb0 + n], 'qg_h%d' % hh)

                def jobs(g, hh):
                    isA = g < 2
                    kv = g % 2
                    hidx = g * 4 + hh
                    for bi_, (tb0, n) in enumerate(TBS):
                        lat = tb0 < NOWN
                        items = []
                        kT_t, kname, V_t, vname = (kTA, 'kTA', VA, 'VA') if isA else (kTB, 'kTB', VB, 'VB')
                        for ct in range(2):
                            items.append(dict(kT=kT_t[:, kv, ct * P:(ct + 1) * P], kbuf=kname,
                                              v=V_t[:, ct, kv * P:(kv + 1) * P], vbuf=vname, c0=0, n=n))
                        if lat and isA:
                            for kt in range(2, NKA // P):
                                items.append(dict(kT=kT_t[:, kv, kt * P:(kt + 1) * P], kbuf=kname,
                                                  v=V_t[:, kt, kv * P:(kv + 1) * P], vbuf=vname, c0=0, n=n))
                        elif lat:
                            m = bi_
                            for jj in range(4 * m, 4 * m + 6):
                                lo = max(jj - 2, 4 * m)
                                hi = min(jj, 4 * m + 3)
                                sel = 1 if jj == 0 else (2 if jj == 9 else 0)
                                bc0 = (lo - (jj - 2)) * P
                                nn = (hi - lo + 1) * P
                                items.append(dict(kT=kT_t[:, kv, CTXL + jj * P:CTXL + (jj + 1) * P], kbuf=kname,
                                                  v=V_t[:, 2 + jj, kv * P:(kv + 1) * P], vbuf=vname,
                                                  c0=(lo - 4 * m) * P, n=nn, bias=band_t[:, sel, bc0:bc0 + nn], bbuf='band_t'))
                        extra = None if isA else sinke[:, (g - 2) * 4 + hh:(g - 2) * 4 + hh + 1]
                        attn_job(qg0[:, hh, tb0:tb0 + n], 'qg_h%d' % hh, n, items, oT[:, hidx, tb0:tb0 + n], 'oT', extra=extra)

                load_wq(seq[0])
                load_wq(seq[1])
                if fused:
                    exchange_kv()
                for hh in range(4):
                    qproj(seq[0], hh)
                for i_, g in enumerate(seq):
                    if i_ >= 1 and i_ + 1 < 4:
                        load_wq(seq[i_ + 1])
                    for hh in range(4):
                        jobs(g, hh)
                        if i_ + 1 < 4:
                            qproj(seq[i_ + 1], hh)
                t.barrier()
        else:
            NEXT = TEXT + CTXL
            hext = sb(att, "hext", [P, KC, NEXT], BF16)
            if fused:
                with Scope() as es:
                    NB = 256
                    xo_prev = carry['xo']
                    hs_d = [nc.dram_tensor("hsend%d" % i, [D, 256 if i < 2 else 64], BF16) for i in range(3)]
                    hg_d = [nc.dram_tensor("hgath%d" % i, [4 * D, 256 if i < 2 else 64], BF16) for i in range(3)]
                    hc = sb(es, "hc", [P, KC, 64], BF16)
                    gb = [sb(es, "gb%d" % i, [P, KC, NB], BF16) for i in range(2)]
                    sel_t = sb(es, "sel_t", [P, 8], F32)
                    t.dma('sp', sel_t[:], selv, 'sel_t')

                    def own_blk(bi):
                        norm_blk(xo_prev[:, :, (bi - 1) * NB:bi * NB], 'xo', NB, 0, se1, SH1,
                                 hext[:, :, bi * NB:(bi + 1) * NB], 'hext', psi=6)
                    own_blk(1)
                    own_blk(4)
                    norm_blk(xo_prev[:, :, NOWN:NOWN + 64], 'xo', 64, 1, se1, SH1, hc, 'hc', psi=6)
                    srcs = (hext[:, :, 256:512], hext[:, :, 1024:1280], hc[:])
                    for i_ in range(3):
                        t.dma('sp', hs_d[i_].ap().rearrange("(c p) t -> p c t", p=P), srcs[i_], 'dram:hsend%d' % i_,
                              src=('hext' if i_ < 2 else 'hc',))
                    for i_ in range(3):
                        t.collective(lambda e, i_=i_: e.collective_compute(
                            "AllGather", ALU.bypass, replica_groups=[[0, 1, 2, 3], [4, 5, 6, 7]],
                            ins=[hs_d[i_].ap().opt()], outs=[hg_d[i_].ap().opt()]),
                            'dram:hgath%d' % i_, src=('dram:hsend%d' % i_,))
                    own_blk(2)
                    own_blk(3)
                    for (gi_, so, c0_) in ((1, 0, 0), (0, 4, 1280)):
                        dst = hext[:, :, c0_:c0_ + 256]
                        for r_ in range(4):
                            g_ = r_ % 2
                            t.dma('sp', gb[g_][:], hg_d[gi_].ap()[r_ * D:(r_ + 1) * D, :].rearrange("(c p) t -> p c t", p=P),
                                  'gb%d' % g_, src=('dram:hgath%d' % gi_,))
                            if r_ == 0:
                                t.op('dve', lambda e: e.tensor_scalar(out=dst, in0=gb[g_][:], scalar1=sel_t[:, so + r_:so + r_ + 1],
                                                                      scalar2=None, op0=ALU.mult),
                                     r=('gb%d' % g_, 'sel_t'), w=('hext',))
                            else:
                                t.op('dve', lambda e: e.scalar_tensor_tensor(out=dst, in0=gb[g_][:], scalar=sel_t[:, so + r_:so + r_ + 1],
                                                                             in1=dst, op0=ALU.mult, op1=ALU.add),
                                     r=('gb%d' % g_, 'sel_t', 'hext'), w=('hext',))
                    for r_ in range(4):
                        t.dma('sp', hext[:, :, TEXT + r_ * 64:TEXT + (r_ + 1) * 64],
                              hg_d[2].ap()[r_ * D:(r_ + 1) * D, :].rearrange("(c p) t -> p c t", p=P), 'hext', src=('dram:hgath2',))
                    t.barrier()
                carry['post'].close()
            else:
                with Scope() as es:
                    NB = 256
                    xb = [sb(es, "xb%d" % i, [P, KC, NB], F32) for i in range(2)]
                    for bi in range(7):
                        isc = (bi == 6)
                        s = bi % 2
                        if isc:
                            t.dma('sp', xb[s][:], cT[:, :, 0:NB], 'xb%d' % s)
                        else:
                            t.dma('sp', xb[s][:], xT[:, :, bi * NB:(bi + 1) * NB], 'xb%d' % s)
                        norm_blk(xb[s], 'xb%d' % s, NB, 1 if isc else 0, se1, SH1, hext[:, :, bi * NB:(bi + 1) * NB], 'hext', psi=6)
                    t.barrier()
            oT = sb(ost, "oT", [P, KC, NTOK], BF16)
            with Scope() as es:
                wg = [sb(es, "wg%d" % i, [P, KC, 768], BF16) for i in range(2)]
                qg0 = sb(es, "qg0", [P, 2, NOWN], BF16)
                kg0 = sb(es, "kg0", [P, 2, NEXT], BF16)
                vg0 = sb(es, "vg0", [P, NEXT // P, 256], BF16)
                qg, kg, vg = [qg0, qg0], [kg0, kg0], [vg0, vg0]
                bt = [sb(es, "bt%d" % i, [P, 17, P], F32) for i in range(2)]
                bjobs = [(h_, m_) for h_ in range(16) for m_ in range(2)]

                def load_bt(ji):
                    h_, m_ = bjobs[ji]
                    t.dma('sp', bt[ji % 2][:], biasT[h_, :, 12 * m_:12 * m_ + 17, :], 'bt%d' % (ji % 2))

                def load_wg(g):
                    for i3 in range(3):
                        t.dma('pool', wg[g % 2][:, :, i3 * 256:(i3 + 1) * 256],
                              w_in[:, :, i3 * 2048 + g * 256:i3 * 2048 + (g + 1) * 256], 'wg%d' % (g % 2))
                load_wg(0)
                load_bt(0)

                def slot_of(p, tt):
                    if p == 0:
                        return tt
                    if p == 1:
                        return 6 + tt
                    if p == 6:
                        return 17 + (tt + 1)
                    if p == 7:
                        return 23 + (tt + 1)
                    return 12 + tt

                def tset(p):
                    if p in (0, 1):
                        return range(0, 6)
                    if p in (6, 7):
                        return range(-1, 5)
                    return range(0, 5)

                for g in range(8):
                    s = g % 2
                    if g + 1 < 8:
                        load_wg(g + 1)
                    for hh in range(2):
                        for (tb0, n) in TBS:
                            pk = rr(3, 'pkA')
                            for kc in range(KC):
                                t.op('pe', lambda e, kc=kc, hh=hh, pk=pk, tb0=tb0, n=n: e.matmul(
                                    ps[pk][:, :n], lhsT=wg[s][:, kc, hh * P:(hh + 1) * P], rhs=hext[:, kc, OWN0 + tb0:OWN0 + tb0 + n],
                                    start=(kc == 0), stop=(kc == KC - 1)), r=('wg%d' % s, 'hext'), w=('ps%d' % pk,))
                            t.op('act', lambda e, hh=hh, pk=pk, tb0=tb0, n=n: e.activation(out=qg[s][:, hh, tb0:tb0 + n], in_=ps[pk][:, :n], func=AF.Copy),
                                 r=('ps%d' % pk,), w=('qg',))
                        for (k0, n) in ((0, 512), (512, 512), (1024, 512), (1536, 256)):
                            pk = rr(3, 'pkA')
                            for kc in range(KC):
                                t.op('pe', lambda e, kc=kc, hh=hh, pk=pk, k0=k0, n=n: e.matmul(
                                    ps[pk][:, :n], lhsT=wg[s][:, kc, 256 + hh * P:256 + (hh + 1) * P], rhs=hext[:, kc, k0:k0 + n],
                                    start=(kc == 0), stop=(kc == KC - 1)), r=('wg%d' % s, 'hext'), w=('ps%d' % pk,))
                            t.op('dve', lambda e, hh=hh, pk=pk, k0=k0, n=n: e.tensor_copy(out=kg[s][:, hh, k0:k0 + n], in_=ps[pk][:, :n]),
                                 r=('ps%d' % pk,), w=('kg',))
                    for tt in range(NEXT // P):
                        pk = 3 + rr(3, 'pkB')
                        for kc in range(KC):
                            t.op('pe', lambda e, kc=kc, tt=tt, pk=pk: e.matmul(
                                ps[pk][:, :256], lhsT=hext[:, kc, tt * P:(tt + 1) * P], rhs=wg[s][:, kc, 512:768],
                                start=(kc == 0), stop=(kc == KC - 1)), r=('wg%d' % s, 'hext'), w=('ps%d' % pk,))
                        t.op('act', lambda e, tt=tt, pk=pk: e.activation(out=vg[s][:, tt, :], in_=ps[pk][:, :256], func=AF.Copy),
                             r=('ps%d' % pk,), w=('vg',))
                    for hh in range(2):
                        hidx = g * 2 + hh
                        for m, (tb0, n) in enumerate(TBS):
                            ji = hidx * 2 + m
                            b_ = ji % 2
                            if ji + 1 < len(bjobs):
                                load_bt(ji + 1)
                            items = []
                            for ct in range(2):
                                kt = TEXT // P + ct
                                items.append(dict(kT=kg[s][:, hh, kt * P:(kt + 1) * P], kbuf='kg',
                                                  v=vg[s][:, kt, hh * P:(hh + 1) * P], vbuf='vg', c0=0, n=n))
                            for p_ in range(4 * m, 4 * m + 4):
                                tl = list(tset(p_))
                                for grp in (tl[0:3], tl[3:]):
                                    subs = []
                                    for tt in grp:
                                        jj = p_ + tt
                                        subs.append((kg[s][:, hh, jj * P:(jj + 1) * P], 'kg', vg[s][:, jj, hh * P:(hh + 1) * P], 'vg'))
                                    sl0 = slot_of(p_, grp[0]) - 12 * m
                                    items.append(dict(subs=subs, c0=(p_ - 4 * m) * P, n=P,
                                                      bias=bt[b_][:, sl0:sl0 + len(grp), :].rearrange("p a b -> p (a b)"), bbuf='bt%d' % b_))
                            attn_job(qg[s][:, hh, tb0:tb0 + n], 'qg', n, items, oT[:, hidx, tb0:tb0 + n], 'oT')
                t.barrier()

        att.close()
        post = Scope()
        xo = sb(post, "xo", [P, KC, NTOK], F32)
        xoff = 0 if (fused and not L0) else OWN0
        xsrc = ('dram:x1own',) if (fused and not L0) else ()
        t.dma('sp', xo[:, 0:8, 0:NOWN], xT[:, 0:8, xoff:xoff + NOWN], 'xo', src=xsrc)
        t.dma('sp', xo[:, 8:16, 0:NOWN], xT[:, 8:16, xoff:xoff + NOWN], 'xo', src=xsrc)
        if L0:
            t.dma('sp', xo[:, :, NOWN:NTOK], cT[:, :, 0:64], 'xo')
        with Scope() as es:
            wo = [sb(es, "wo%d" % i, [P, KC, 512], BF16) for i in range(2)]

            def load_wo(dg):
                for hlf in range(2):
                    t.dma('pool', wo[dg % 2][:, hlf * 8:(hlf + 1) * 8, :], w_out[:, hlf * 8:(hlf + 1) * 8, dg * 512:(dg + 1) * 512], 'wo%d' % (dg % 2))
            load_wo(0)
            for dg in range(4):
                s = dg % 2
                if dg + 1 < 4:
                    load_wo(dg + 1)
                for dl in range(4):
                    d = dg * 4 + dl
                    for (tb0, n) in TBS:
                        j = 0 if tb0 < NOWN else 1
                        pk = rr(6, 'pk6')
                        for kc in range(KC):
                            t.op('pe', lambda e, kc=kc, dl=dl, pk=pk, tb0=tb0, n=n: e.matmul(
                                ps[pk][:, :n], lhsT=wo[s][:, kc, dl * P:(dl + 1) * P], rhs=oT[:, kc, tb0:tb0 + n],
                                start=(kc == 0), stop=(kc == KC - 1)), r=('wo%d' % s, 'oT'), w=('ps%d' % pk,))
                        t.op('dve', lambda e, d=d, pk=pk, tb0=tb0, n=n, j=j: e.scalar_tensor_tensor(
                            out=xo[:, d, tb0:tb0 + n], in0=ps[pk][:, :n], scalar=mod[:, G1 + d, j:j + 1], in1=xo[:, d, tb0:tb0 + n],
                            op0=ALU.mult, op1=ALU.add), r=('ps%d' % pk, 'xo', 'mod'), w=('xo',))
            t.barrier()
        ost.close()

        with Scope() as es:
            h2 = sb(es, "h2", [P, KC, NTOK], BF16)
            wA = [sb(es, "wA%d" % i, [P, KC, 512], BF16) for i in range(2)]
            wB = [sb(es, "wB%d" % i, [P, 4, D], BF16) for i in range(2)]
            uT0 = sb(es, "uT0", [P, 4, NTOK], BF16)
            if L0:
                uT, uTn = [uT0, uT0], ['uT0', 'uT0']
            else:
                uT, uTn = [uT0, sb(es, "uT1", [P, 4, NTOK], BF16)], ['uT0', 'uT1']
            rl = [sb(es, "rl%d" % i, [P, 512], BF16) for i in range(2)]
            NBLK = DFF // 512

            def load_mlp(blk):
                s_ = blk % 2
                for hlf in range(2):
                    t.dma('pool', wA[s_][:, hlf * 8:(hlf + 1) * 8, :], w1[:, hlf * 8:(hlf + 1) * 8, blk * 512:(blk + 1) * 512], 'wA%d' % s_)
                for hlf in range(2):
                    t.dma('pool', wB[s_][:, hlf * 2:(hlf + 1) * 2, :], w2[:, blk * 4 + hlf * 2:blk * 4 + (hlf + 1) * 2, :], 'wB%d' % s_)

            def phase1(s, m, ti, tb0, n):
                pk = rr(4, 'pk4a')
                for kc in range(KC):
                    t.op('pe', lambda e: e.matmul(
                        ps[pk][:, :n], lhsT=wA[s][:, kc, m * P:(m + 1) * P], rhs=h2[:, kc, tb0:tb0 + n],
                        start=(kc == 0), stop=(kc == KC - 1)), r=('wA%d' % s, 'h2_%d' % ti), w=('ps%d' % pk,))
                ri = rr(2, 'rl')
                t.op('act', lambda e: e.activation(out=rl[ri][:, :n], in_=ps[pk][:, :n], func=AF.Relu),
                     r=('ps%d' % pk,), w=('rl%d' % ri,))
                t.op('pool', lambda e: e.tensor_tensor(out=uT[s][:, m, tb0:tb0 + n], in0=rl[ri][:, :n], in1=rl[ri][:, :n], op=ALU.mult),
                     r=('rl%d' % ri,), w=(uTn[s],))

            load_mlp(0)
            for blk in range(NBLK):
                s = blk % 2
                if blk + 1 < NBLK:
                    load_mlp(blk + 1)
                if blk == 0:
                    for ti, (tb0, n) in enumerate(TBS):
                        j = 0 if tb0 < NOWN else 1
                        norm_blk(xo[:, :, tb0:tb0 + n], 'xo', n, j, se2, SH2, h2[:, :, tb0:tb0 + n], 'h2_%d' % ti, psi=6)
                        for m in range(4):
                            phase1(s, m, ti, tb0, n)
                else:
                    for m in range(4):
                        for ti, (tb0, n) in enumerate(TBS):
                            phase1(s, m, ti, tb0, n)
                for d in range(KC):
                    for (tb0, n) in TBS:
                        j = 0 if tb0 < NOWN else 1
                        pk = 4 + rr(4, 'pk4b')
                        for m in range(4):
                            t.op('pe', lambda e, m=m, d=d, pk=pk, tb0=tb0, n=n: e.matmul(
                                ps[pk][:, :n], lhsT=wB[s][:, m, d * P:(d + 1) * P], rhs=uT[s][:, m, tb0:tb0 + n],
                                start=(m == 0), stop=(m == 3)), r=('wB%d' % s, uTn[s]), w=('ps%d' % pk,))
                        t.op('dve', lambda e, d=d, pk=pk, tb0=tb0, n=n, j=j: e.scalar_tensor_tensor(
                            out=xo[:, d, tb0:tb0 + n], in0=ps[pk][:, :n], scalar=mod[:, G2 + d, j:j + 1], in1=xo[:, d, tb0:tb0 + n],
                            op0=ALU.mult, op1=ALU.add), r=('ps%d' % pk, 'xo', 'mod'), w=('xo',))
            t.barrier()

        if L0 and fused:
            x1o = carry['x1own'].ap().rearrange("(c p) t -> p c t", p=P)
            for hlf in range(2):
                t.dma('sp', x1o[:, hlf * 8:(hlf + 1) * 8, :], xo[:, hlf * 8:(hlf + 1) * 8, 0:NOWN], 'dram:x1own', src=('xo',))
            carry['xo'] = xo
            carry['post'] = post
        elif L0:
            for hlf in range(2):
                t.dma('sp', x1T[:, hlf * 8:(hlf + 1) * 8, :], xo[:, hlf * 8:(hlf + 1) * 8, 0:NOWN], 'out:x1', src=('xo',))
            t.dma('sp', c1T, xo[:, :, NOWN:NTOK], 'out:c1', src=('xo',))
        else:
            with Scope() as es:
                fn_t = sb(es, "fn_t", [P, KC], F32)
                t.dma('sp', fn_t[:], fnw, 'fn_t')
                for (tb0, n) in TBS:
                    for c in range(KC):
                        i = c % 4
                        t.op('act', lambda e, c=c, i=i, tb0=tb0, n=n: e.activation(out=sq[i][:, :n], in_=xo[:, c, tb0:tb0 + n], func=AF.Square),
                             r=('xo',), w=('sq%d' % i,))
                        t.op('pe', lambda e, c=c, i=i, n=n: e.matmul(ps[6][:, :n], lhsT=ones[:], rhs=sq[i][:, :n],
                                                                    start=(c == 0), stop=(c == KC - 1)),
                             r=('sq%d' % i, 'ones'), w=('ps6',))
                    rs, rsb = rstd_from(ps[6][:, :n], n, 1.0 / D, 'ps6')
                    for c in range(KC):
                        t.op('dve', lambda e, c=c, tb0=tb0, n=n: e.scalar_tensor_tensor(
                            out=xo[:, c, tb0:tb0 + n], in0=xo[:, c, tb0:tb0 + n], scalar=fn_t[:, c:c + 1], in1=rs,
                            op0=ALU.mult, op1=ALU.mult), r=('xo', rsb, 'fn_t'), w=('xo',))
                for hlf in range(2):
                    t.dma('sp', outT[:, hlf * 8:(hlf + 1) * 8, :], xo[:, hlf * 8:(hlf + 1) * 8, :], 'out:o', src=('xo',))
                t.barrier()
        if not (L0 and fused):
            t.barrier()
            post.close()

    if fused:
        ada_wq = [din0("ada_wq_%d" % L_, [D, 3072]).rearrange("(c p) n -> p c n", p=P) for L_ in range(2)]
        ada_bq = din0("ada_bq", [P, 48])
        normw2 = din0("normw2", [P, 2, 2, KC])
        msend = nc.dram_tensor("msend", [P, 96], F32)
        mgath = nc.dram_tensor("mgath", [4 * P, 96], F32)
        with Scope() as es:
            cv = sb(es, "cv", [P, KC, 2], F32)
            scv = sb(es, "scv", [P, KC, 2], BF16)
            adab = sb(es, "adab", [P, 48], F32)
            nw = sb(es, "nw", [P, 4, KC], F32)
            part = sb(es, "part", [P, 96], F32)
            wa = [sb(es, "wa%d" % i, [P, KC, 1024], BF16) for i in range(2)]
            t.dma('sp', cv[:], cvec, 'cv')
            t.dma('sp', adab[:], ada_bq, 'adab')
            t.dma('sp', nw[:], normw2.rearrange("p l s c -> p (l s) c"), 'nw')
            t.op('act', lambda e: e.activation(out=scv[:], in_=cv[:], func=AF.Silu), r=('cv',), w=('scv',))
            psm = ps[7]
            gi = 0
            for L_ in range(2):
                for g in range(3):
                    s_ = gi % 2
                    gi += 1
                    for hlf in range(2):
                        t.dma('pool', wa[s_][:, hlf * 8:(hlf + 1) * 8, :],
                              ada_wq[L_][:, hlf * 8:(hlf + 1) * 8, g * 1024:(g + 1) * 1024], 'wa%d' % s_)
                    for cl in range(8):
                        cc = L_ * 24 + g * 8 + cl
                        for kc in range(KC):
                            t.op('pe', lambda e: e.matmul(
                                psm[:, 2 * cc:2 * cc + 2], lhsT=wa[s_][:, kc, cl * P:(cl + 1) * P], rhs=scv[:, kc, :],
                                start=(kc == 0), stop=(kc == KC - 1)), r=('wa%d' % s_, 'scv'), w=('ps7',))
            psm3 = psm[:, 0:96].rearrange("p (c j) -> p c j", j=2)
            part3 = part[:].rearrange("p (c j) -> p c j", j=2)
            for j in range(2):
                t.op('dve', lambda e: e.tensor_tensor(out=part3[:, :, j], in0=psm3[:, :, j], in1=adab[:, :], op=ALU.add),
                     r=('ps7', 'adab'), w=('part',))
            t.dma('sp', msend.ap(), part[:], 'dram:msend', src=('part',))
            t.collective(lambda e: e.collective_compute("AllGather", ALU.bypass, replica_groups=[[0, 1, 2, 3], [4, 5, 6, 7]],
                                                        ins=[msend.ap().opt()], outs=[mgath.ap().opt()]),
                         'dram:mgath', src=('dram:msend',))
            mg = mgath.ap().rearrange("(r p) f -> p r f", p=P)
            for L_ in range(2):
                md = modsL[L_]['mod']
                t.dma('sp', md[:].rearrange("p (r c) j -> p r (c j)", r=4), mg[:, :, L_ * 48:(L_ + 1) * 48], 'mod', src=('dram:mgath',))
            for L_ in range(2):
                md = modsL[L_]['mod']
                for j in range(2):
                    t.op('dve', lambda e: e.scalar_tensor_tensor(out=modsL[L_]['se1'][:, :, j], in0=md[:, 16:32, j], scalar=1.0,
                                                                 in1=nw[:, L_ * 2, :], op0=ALU.add, op1=ALU.mult),
                         r=('mod', 'nw'), w=('se1',))
                    t.op('dve', lambda e: e.scalar_tensor_tensor(out=modsL[L_]['se2'][:, :, j], in0=md[:, 64:80, j], scalar=1.0,
                                                                 in1=nw[:, L_ * 2 + 1, :], op0=ALU.add, op1=ALU.mult),
                         r=('mod', 'nw'), w=('se2',))
            t.barrier()

    for layer_ in layers:
        emit_layer(layer_)
    t.barrier()
    top.close()
    t.es.close()
    return nc


def _rope_tables():
    tt = np.arange(SEQ)
    row = (tt // 64).astype(np.float32)
    col = (tt % 64).astype(np.float32)
    inv = (np.float32(10000.0) ** (-np.arange(32, dtype=np.float32) / np.float32(32))).astype(np.float32)
    ang_r = row[:, None] * inv
    ang_c = col[:, None] * inv
    ang = np.concatenate([ang_r, ang_r, ang_c, ang_c], axis=-1).astype(np.float32)
    cos = np.cos(ang).astype(np.float32)
    sin = np.sin(ang).astype(np.float32)
    sgn = np.ones(128, np.float32)
    sgn[0:32] = -1.0
    sgn[64:96] = -1.0
    return cos.T.copy(), (sin * sgn[None, :]).T.copy()


def _rot_matrix():
    R = np.zeros((128, 128), np.float32)
    for base in (0, 64):
        for d in range(32):
            R[base + d + 32, base + d] = 1.0
            R[base + d, base + d + 32] = 1.0
    return R


def _band_tables(qt):
    p = np.arange(128)[:, None]
    f = np.arange(128)[None, :]
    m = np.zeros((128, 384), np.float32)
    m[:, 0:128] = np.where(p <= f, 0.0, NEG)
    m[:, 256:384] = np.where(p >= f, 0.0, NEG)
    full = np.full((128, 384), NEG, np.float32)
    out = np.stack([m, m if qt > 0 else full, m if qt < 3 else full], axis=1)
    return np.ascontiguousarray(out.astype(np.float32))


def _natten_bias(rpb, qt):
    R0 = 16 * qt
    H = rpb.shape[0]
    out = np.full((H, 128, 29, 128), NEG, np.float32)

    def tset(p):
        if p in (0, 1):
            return list(range(0, 6)), (0 if p == 0 else 6), 0
        if p in (6, 7):
            return list(range(-1, 5)), (17 if p == 6 else 23), -1
        return list(range(0, 5)), 12, 0
    w = np.arange(64)
    cq = np.arange(64)
    cs = np.clip(cq - 8, 0, 48)
    col_valid = (w[None, :] >= cs[:, None]) & (w[None, :] < cs[:, None] + 16)
    ci = np.clip(w[None, :] - cq[:, None] + 15, 0, 30)
    for p in (0, 1, 2, 6, 7):
        ts, base, t0 = tset(p)
        r0 = R0 + 2 * p
        for tt in ts:
            slot = base + (tt - t0)
            for a in range(2):
                rq = r0 + a
                rs = min(max(rq - 4, 0), 56)
                for b in range(2):
                    rk = r0 - 4 + 2 * tt + b
                    if rk < 0 or rk >= 64 or rk < rs or rk >= rs + 8:
                        continue
                    ri = rk - rq + 7
                    vals = rpb[:, ri, :][:, ci]
                    vals = np.where(col_valid[None], vals, NEG)
                    out[:, b * 64:(b + 1) * 64, slot, a * 64:(a + 1) * 64] = np.transpose(vals, (0, 2, 1))
    return out


_NC_CACHE = {}


def _get_nc(layer):
    if layer not in _NC_CACHE:
        _NC_CACHE[layer] = build(layer)
    return _NC_CACHE[layer]


def _fm(a):
    return np.ascontiguousarray(a.T)


def layer_inputs(layer, core, x, ctx, c, c_ctx, ada_w, ada_b, norm_w, mlp_w1, mlp_w2, w_in, w_out, extra):
    b, qt = core // 4, core % 4
    m = {}
    m["cvec"] = np.ascontiguousarray(np.stack([c[b], c_ctx], axis=1).astype(np.float32))
    m["ada_w"] = ada_w[layer]
    m["ada_b"] = np.ascontiguousarray(ada_b[layer].reshape(96, 128).T)
    m["normw"] = np.ascontiguousarray(norm_w[layer].reshape(2, KC, 128).transpose(2, 0, 1))
    m["w_in"] = w_in
    m["w_out"] = w_out
    m["w1"] = mlp_w1[layer]
    m["w2"] = mlp_w2[layer]
    if layer == 0:
        shift = 1024 * qt - 128
        m["xT"] = _fm(np.roll(x[b], -shift, axis=0))
        m["cT"] = _fm(np.roll(ctx[b], -64 * qt, axis=0))
        cosT, sinT = extra["rope"]
        m["cosT"] = np.ascontiguousarray(np.roll(cosT, -shift, axis=1))
        m["sinT"] = np.ascontiguousarray(np.roll(sinT, -shift, axis=1))
        m["rotm"] = extra["rotm"]
        m["bandb"] = _band_tables(qt)
        m["qkw"] = np.ascontiguousarray(np.stack([extra["qn"], extra["kn"]], axis=1).astype(np.float32))
        m["sinkb"] = np.ascontiguousarray(np.broadcast_to(extra["sink"][None, :], (128, 8)).astype(np.float32))
    else:
        xe = np.zeros((1536, D), np.float32)
        lo = 1024 * qt - 256
        hi = lo + 1536
        slo, shi = max(lo, 0), min(hi, SEQ)
        xe[slo - lo:shi - lo] = x[b][slo:shi]
        m["xT"] = _fm(xe)
        m["cT"] = _fm(ctx[b])
        m["biasT"] = _natten_bias(extra["rpb"], qt)
        m["fnw"] = np.ascontiguousarray(extra["fnw"].reshape(KC, 128).T)
    return m


FUSED = True


def kernel(x, c, ctx, c_ctx, ada_w, ada_b, norm_w, mlp_w1, mlp_w2, ev_w_in, ev_w_out,
           ev_q_norm, ev_k_norm, ev_sink, od_w_in, od_w_out, od_rpb, final_norm_w):
    f = lambda a: np.asarray(a, dtype=np.float32)
    x, c, ctx, c_ctx = f(x), f(c), f(ctx), f(c_ctx)
    ada_w, ada_b, norm_w, mlp_w1, mlp_w2 = f(ada_w), f(ada_b), f(norm_w), f(mlp_w1), f(mlp_w2)
    cores = list(range(8))
    cosT, sinT = _rope_tables()
    ex0 = dict(rope=(cosT, sinT), rotm=_rot_matrix(), qn=f(ev_q_norm)[0], kn=f(ev_k_norm)[0], sink=f(ev_sink)[0])
    ex1 = dict(rpb=f(od_rpb)[0], fnw=f(final_norm_w))
    out = np.empty_like(x)
    if FUSED:
        in_maps = []
        for k in cores:
            b, qt = k // 4, k % 4
            m0 = layer_inputs(0, k, x, ctx, c, c_ctx, ada_w, ada_b, norm_w, mlp_w1, mlp_w2, f(ev_w_in)[0], f(ev_w_out)[0], ex0)
            m = {"cvec": m0.pop("cvec")}
            for kk in ("ada_w", "ada_b", "normw"):
                m0.pop(kk)
            for kk, v in m0.items():
                m[kk + "_0"] = v
            for L_ in range(2):
                m["ada_wq_%d" % L_] = np.ascontiguousarray(ada_w[L_][:, 3072 * qt:3072 * (qt + 1)])
            m["ada_bq"] = np.ascontiguousarray(
                ada_b.reshape(2, 96, 128)[:, 24 * qt:24 * (qt + 1), :].transpose(2, 0, 1).reshape(128, 48))
            m["normw2"] = np.ascontiguousarray(norm_w.reshape(2, 2, KC, 128).transpose(3, 0, 1, 2))
            m["w_in_1"] = f(od_w_in)[0]
            m["w_out_1"] = f(od_w_out)[0]
            m["w1_1"] = mlp_w1[1]
            m["w2_1"] = mlp_w2[1]
            m["biasT_1"] = _natten_bias(ex1["rpb"], qt)
            m["fnw_1"] = np.ascontiguousarray(ex1["fnw"].reshape(KC, 128).T)
            sel = np.zeros((128, 8), np.float32)
            if qt > 0:
                sel[:, qt - 1] = 1.0
            if qt < 3:
                sel[:, 4 + qt + 1] = 1.0
            m["selv_1"] = sel
            in_maps.append(m)
        r = run_bass_kernel_spmd(_get_nc('fused'), in_maps, core_ids=cores).results
        for k in cores:
            b, qt = k // 4, k % 4
            out[b, 1024 * qt:1024 * (qt + 1)] = r[k]["outT"].T
        return out
    in0 = [layer_inputs(0, k, x, ctx, c, c_ctx, ada_w, ada_b, norm_w, mlp_w1, mlp_w2, f(ev_w_in)[0], f(ev_w_out)[0], ex0)
           for k in cores]
    r0 = run_bass_kernel_spmd(_get_nc(0), in0, core_ids=cores).results
    x1 = np.empty_like(x)
    ctx1 = np.empty_like(ctx)
    for k in cores:
        b, qt = k // 4, k % 4
        x1[b, 1024 * qt:1024 * (qt + 1)] = r0[k]["x1T"].T
        ctx1[b, 64 * qt:64 * (qt + 1)] = r0[k]["c1T"].T
    in1 = [layer_inputs(1, k, x1, ctx1, c, c_ctx, ada_w, ada_b, norm_w, mlp_w1, mlp_w2, f(od_w_in)[0], f(od_w_out)[0], ex1)
           for k in cores]
    r1 = run_bass_kernel_spmd(_get_nc(1), in1, core_ids=cores).results
    for k in cores:
        b, qt = k // 4, k % 4
        out[b, 1024 * qt:1024 * (qt + 1)] = r1[k]["outT"].T
    return out
```

```python
import numpy as np
from contextlib import ExitStack
import concourse.bass as bass
import concourse.mybir as mybir
from concourse.bass_utils import run_bass_kernel_spmd

F32 = mybir.dt.float32
BF16 = mybir.dt.bfloat16
AF = mybir.ActivationFunctionType
ALU = mybir.AluOpType

P = 128
D = 2048
KC = 16
DFF = 8192
SEQ = 4096
CTXL = 256
NOWN = 1024
NEG = -1.0e30
EPS = 1e-6
SCALE = 128 ** -0.5


class Trk:
    def __init__(self, nc):
        self.nc = nc
        self.es = ExitStack()
        self.E = {'pe': nc.tensor, 'act': nc.scalar, 'dve': nc.vector, 'pool': nc.gpsimd, 'sp': nc.sync}
        self.sem = {}
        self.cnt = {}
        for k in self.E:
            self.sem[k] = self.es.enter_context(nc.semaphore('e_' + k))
            self.cnt[k] = 0
        self.seen = {k: {} for k in self.E}
        self.bw = {}
        self.br = {}
        self.dcount = {}

    def _need(self, eng, r, w):
        need = {}

        def add(ev):
            if ev is None:
                return
            s, v = ev
            if need.get(s, 0) < v:
                need[s] = v
        for b in r:
            add(self.bw.get(b))
        for b in w:
            add(self.bw.get(b))
            for s, v in self.br.get(b, {}).items():
                add((s, v))
        e = self.E[eng]
        for s, v in need.items():
            if eng == 'pe' and s == 'pe':
                continue
            if self.seen[eng].get(s, 0) >= v:
                continue
            e.wait_ge(self.sem[s], v)
            self.seen[eng][s] = v

    def _mark(self, ev, r, w):
        s, v = ev
        for b in r:
            d = self.br.setdefault(b, {})
            if d.get(s, 0) < v:
                d[s] = v
        for b in w:
            self.bw[b] = ev
            self.br[b] = {}

    def op(self, eng, fn, r=(), w=()):
        self._need(eng, r, w)
        inst = fn(self.E[eng])
        self.cnt[eng] += 1
        inst.then_inc(self.sem[eng], 1)
        self._mark((eng, self.cnt[eng]), r, w)

    def dma(self, eng, out, in_, dst, src=()):
        self._need(eng, src, (dst,))
        key = 'd_' + dst
        if key not in self.sem:
            self.sem[key] = self.es.enter_context(self.nc.semaphore(key.replace(':', '_')))
            self.dcount[key] = 0
        inst = self.E[eng].dma_start(out=out, in_=in_)
        self.dcount[key] += 16
        inst.then_inc(self.sem[key], 16)
        self._mark((key, self.dcount[key]), src, (dst,))

    def collective(self, fn, dst, src=()):
        self._need('pool', src, (dst,))
        key = 'c_' + dst
        if key not in self.sem:
            self.sem[key] = self.es.enter_context(self.nc.semaphore(key.replace(':', '_')))
            self.dcount[key] = 0
        inst = fn(self.E['pool'])
        self.dcount[key] += 1
        inst.then_inc(self.sem[key])
        self._mark((key, self.dcount[key]), src, (dst,))

    def barrier(self):
        evs = {}
        for k in self.E:
            if self.cnt[k] > 0:
                evs[k] = self.cnt[k]
        for k, v in self.dcount.items():
            if v > 0:
                evs[k] = v
        for eng in self.E:
            for s, v in evs.items():
                if s == eng and eng == 'pe':
                    continue
                if self.seen[eng].get(s, 0) >= v:
                    continue
                self.E[eng].wait_ge(self.sem[s], v)
                self.seen[eng][s] = v


def build(mode):
    nc = bass.Bass("TRN2", target_bir_lowering=False)
    t = Trk(nc)
    fused = (mode == 'fused')
    layers = [0, 1] if fused else [mode]

    keep = []

    def din0(name, shape):
        h = nc.dram_tensor(name, shape, F32, kind="ExternalInput")
        keep.append(h)
        return h.ap()

    cvec = din0("cvec", [D, 2]).rearrange("(c p) j -> p c j", p=P)
    carry = {}
    top = ExitStack()
    ARENA_W = 53000
    arena = top.enter_context(nc.sbuf_tensor("arena", [P, ARENA_W], F32))
    free_list = [(0, ARENA_W)]

    class Scope:
        def __init__(self):
            self.items = []

        def __enter__(self):
            return self

        def __exit__(self, *a):
            self.close()
            return False

        def close(self):
            for (o, n) in self.items:
                free_list.append((o, n))
            self.items = []
            free_list.sort()
            merged = []
            for (o, n) in free_list:
                if merged and merged[-1][0] + merged[-1][1] == o:
                    merged[-1] = (merged[-1][0], merged[-1][1] + n)
                else:
                    merged.append((o, n))
            free_list[:] = merged

    topS = Scope()

    def sb(es, name, shape, dt):
        nel = 1
        for d_ in shape[1:]:
            nel *= d_
        nw = (nel * (2 if dt == BF16 else 4) + 3) // 4
        nw = (nw + 15) // 16 * 16
        for i, (o, n) in enumerate(free_list):
            if n >= nw:
                free_list[i] = (o + nw, n - nw)
                es.items.append((o, nw))
                v = arena[:, o:o + nw]
                if dt == BF16:
                    v = v.bitcast(BF16)
                v = v[:, 0:nel]
                if len(shape) == 3:
                    v = v.rearrange("p (a b) -> p a b", a=shape[1])
                return v
        raise RuntimeError("arena full allocating %s %s; free=%s" % (name, shape, free_list))

    ps = [top.enter_context(nc.psum_tensor("ps%d" % i, [P, 512], F32)) for i in range(8)]
    ones = sb(topS, "ones", [P, P], BF16)
    epst = sb(topS, "epst", [P, 1], F32)
    modsL = {}
    for L_ in layers:
        modsL[L_] = dict(mod=sb(topS, "mod%d" % L_, [P, 96, 2], F32), se1=sb(topS, "se1_%d" % L_, [P, KC, 2], F32),
                         se2=sb(topS, "se2_%d" % L_, [P, KC, 2], F32))
    cur = {}
    t.op('dve', lambda e: e.memset(ones[:], 1.0), w=('ones',))
    t.op('dve', lambda e: e.memset(epst[:], EPS), w=('epst',))

    uid = {}

    def rr(n, key=None):
        key = key or ('k%d' % n)
        uid[key] = uid.get(key, -1) + 1
        return uid[key] % n

    sq = [sb(topS, "sq%d" % i, [P, 512], BF16) for i in range(4)]
    nr = [sb(topS, "nr%d" % i, [P, 512], F32) for i in range(2)]
    tmpf = [sb(topS, "tmpf%d" % i, [P, 512], F32) for i in range(3)]

    def rstd_from(ps_ap, n, inv_n, psbuf):
        i = rr(2, 'nr')
        t.op('act', lambda e: e.activation(out=nr[i][:, :n], in_=ps_ap, func=AF.Ln, bias=epst[:, 0:1], scale=inv_n),
             r=(psbuf, 'epst'), w=('nr%d' % i,))
        t.op('act', lambda e: e.activation(out=nr[i][:, :n], in_=nr[i][:, :n], func=AF.Exp, scale=-0.5),
             r=('nr%d' % i,), w=('nr%d' % i,))
        return nr[i][:, :n], 'nr%d' % i

    def norm_blk(x3, xbuf, n, j, se, shoff, h3, hbuf, psi=6):
        pb = 'ps%d' % psi
        for c in range(KC):
            i = c % 4
            if c % 3 != 2:
                t.op('act', lambda e, c=c, i=i: e.activation(out=sq[i][:, :n], in_=x3[:, c, :], func=AF.Square),
                     r=(xbuf,), w=('sq%d' % i,))
            else:
                t.op('dve', lambda e, c=c, i=i: e.tensor_tensor(out=sq[i][:, :n], in0=x3[:, c, :], in1=x3[:, c, :], op=ALU.mult),
                     r=(xbuf,), w=('sq%d' % i,))
            t.op('pe', lambda e, c=c, i=i: e.matmul(ps[psi][:, :n], lhsT=ones[:], rhs=sq[i][:, :n],
                                                     start=(c == 0), stop=(c == KC - 1)),
                 r=('sq%d' % i, 'ones'), w=(pb,))
        rs, rsb = rstd_from(ps[psi][:, :n], n, 1.0 / D, pb)
        for c in range(KC):
            i = rr(3, 'tmpf')
            t.op('dve', lambda e, c=c, i=i: e.scalar_tensor_tensor(out=tmpf[i][:, :n], in0=x3[:, c, :], scalar=se[:, c, j:j + 1],
                                                                   in1=rs, op0=ALU.mult, op1=ALU.mult),
                 r=(xbuf, rsb, 'se'), w=('tmpf%d' % i,))
            t.op('act', lambda e, c=c, i=i: e.activation(out=h3[:, c, :], in_=tmpf[i][:, :n], func=AF.Identity,
                                                         bias=cur['mod'][:, shoff + c, j:j + 1], scale=1.0),
                 r=('tmpf%d' % i, 'mod'), w=(hbuf,))

    pT = [sb(topS, "pT%d" % i, [P, 512], BF16) for i in range(5)]
    rdn = [sb(topS, "rdn%d" % i, [P, 512], F32) for i in range(2)]
    SB_ = [0, 1, 2, 7]
    ACC = [(3, 4), (5, 6)]
    jobn = [0]

    def attn_job(q_ap, qbuf, N, items, out_ap, outbuf, extra=None):
        assert items[0]['c0'] == 0 and items[0]['n'] == N
        for it in items:
            if 'subs' not in it:
                it['subs'] = [(it['kT'], it['kbuf'], it['v'], it['vbuf'])]
            assert len(it['subs']) * it['n'] <= 512
        ob, db = ACC[jobn[0] % 2]
        jobn[0] += 1
        nit = len(items)
        st = {}

        def emit_s(i):
            it = items[i]
            sbk = SB_[rr(4, 'sbk')]
            pi = rr(5, 'pT')
            n = it['n']
            c0 = it['c0']
            ns = len(it['subs'])
            w_ = ns * n
            for k_, (kT_, kb_, _, _) in enumerate(it['subs']):
                t.op('pe', lambda e: e.matmul(ps[sbk][:, k_ * n:(k_ + 1) * n], lhsT=kT_, rhs=q_ap[:, c0:c0 + n], start=True, stop=True),
                     r=(kb_, qbuf), w=('ps%d' % sbk,))
            if it.get('bias') is not None:
                k = rr(3, 'tmpf')
                t.op('dve', lambda e: e.scalar_tensor_tensor(out=tmpf[k][:, :w_], in0=ps[sbk][:, :w_], scalar=SCALE,
                                                             in1=it['bias'], op0=ALU.mult, op1=ALU.add),
                     r=('ps%d' % sbk, it['bbuf']), w=('tmpf%d' % k,))
                t.op('act', lambda e: e.activation(out=pT[pi][:, :w_], in_=tmpf[k][:, :w_], func=AF.Exp),
                     r=('tmpf%d' % k,), w=('pT%d' % pi,))
            else:
                t.op('act', lambda e: e.activation(out=pT[pi][:, :w_], in_=ps[sbk][:, :w_], func=AF.Exp, scale=SCALE),
                     r=('ps%d' % sbk,), w=('pT%d' % pi,))
            st[i] = pi

        def emit_pv(i):
            it = items[i]
            pi = st[i]
            n = it['n']
            c0 = it['c0']
            ns = len(it['subs'])
            for k_, (_, _, v_, vb_) in enumerate(it['subs']):
                first = (i == 0 and k_ == 0)
                last = (i == nit - 1 and k_ == ns - 1)
                t.op('pe', lambda e: e.matmul(ps[ob][:, c0:c0 + n], lhsT=v_, rhs=pT[pi][:, k_ * n:(k_ + 1) * n],
                                              start=first, stop=last, skip_group_check=True),
                     r=(vb_, 'pT%d' % pi), w=('ps%d' % ob,))
                t.op('pe', lambda e: e.matmul(ps[db][:, c0:c0 + n], lhsT=ones[:], rhs=pT[pi][:, k_ * n:(k_ + 1) * n],
                                              start=first, stop=last, skip_group_check=True),
                     r=('ones', 'pT%d' % pi), w=('ps%d' % db,))

        DEPTH = 3
        for i in range(min(DEPTH, nit)):
            emit_s(i)
        for i in range(nit):
            if i + DEPTH < nit:
                emit_s(i + DEPTH)
            emit_pv(i)
        k = rr(2, 'rdn')
        if extra is not None:
            t.op('act', lambda e: e.activation(out=rdn[k][:, :N], in_=ps[db][:, :N], func=AF.Ln, bias=extra, scale=1.0),
                 r=('ps%d' % db, 'sinke'), w=('rdn%d' % k,))
        else:
            t.op('act', lambda e: e.activation(out=rdn[k][:, :N], in_=ps[db][:, :N], func=AF.Ln),
                 r=('ps%d' % db,), w=('rdn%d' % k,))
        t.op('act', lambda e: e.activation(out=rdn[k][:, :N], in_=rdn[k][:, :N], func=AF.Exp, scale=-1.0),
             r=('rdn%d' % k,), w=('rdn%d' % k,))
        t.op('dve', lambda e: e.tensor_tensor(out=out_ap, in0=ps[ob][:, :N], in1=rdn[k][:, :N], op=ALU.mult),
             r=('ps%d' % ob, 'rdn%d' % k), w=(outbuf,))


    def emit_layer(layer):
        def din(name, shape):
            return din0(name + ("_%d" % layer if fused else ""), shape)
        L0 = (layer == 0)
        TEXT = SEQ if L0 else 1536
        OWN0 = 128 if L0 else 256
        NCQ = 64 if L0 else 0
        NTOK = NOWN + NCQ
        TBS = [(0, 512), (512, 512)] + ([(1024, 64)] if L0 else [])
        CIN = 3072 if L0 else 6144

        if fused and not L0:
            xT = carry['x1own'].ap().rearrange("(c p) t -> p c t", p=P)
            cT = None
            selv = din("selv", [P, 8])
        else:
            xT = din("xT", [D, TEXT]).rearrange("(c p) t -> p c t", p=P)
            cT = din("cT", [D, CTXL]).rearrange("(c p) t -> p c t", p=P)
        mod, se1, se2 = modsL[layer]['mod'], modsL[layer]['se1'], modsL[layer]['se2']
        cur['mod'] = mod
        if not fused:
            ada_w = din("ada_w", [D, 6 * D]).rearrange("(c p) n -> p c n", p=P)
            ada_b = din("ada_b", [P, 96])
            normw = din("normw", [P, 2, KC])
        w_in = din("w_in", [D, CIN]).rearrange("(c p) n -> p c n", p=P)
        w_out = din("w_out", [D, D]).rearrange("(c p) n -> p c n", p=P)
        w1 = din("w1", [D, DFF]).rearrange("(c p) n -> p c n", p=P)
        w2 = din("w2", [DFF, D]).rearrange("(m p) n -> p m n", p=P)
        if L0:
            qkw = din("qkw", [P, 2])
            sinkb = din("sinkb", [P, 8])
            cosT = din("cosT", [P, SEQ])
            sinT = din("sinT", [P, SEQ])
            rotm = din("rotm", [P, P])
            bandb = din("bandb", [P, 3, 384])
            if not fused:
                x1T = nc.dram_tensor("x1T", [D, NOWN], F32, kind="ExternalOutput").ap().rearrange("(c p) t -> p c t", p=P)
                c1T = nc.dram_tensor("c1T", [D, 64], F32, kind="ExternalOutput").ap().rearrange("(c p) t -> p c t", p=P)
            else:
                carry['x1own'] = nc.dram_tensor("x1own", [D, NOWN], F32)
        else:
            biasT = din("biasT", [16, P, 29, P])
            fnw = din("fnw", [P, KC])
            outT = nc.dram_tensor("outT", [D, NOWN], F32, kind="ExternalOutput").ap().rearrange("(c p) t -> p c t", p=P)

        if not fused:
            with Scope() as es:
                cv = sb(es, "cv", [P, KC, 2], F32)
                scv = sb(es, "scv", [P, KC, 2], BF16)
                adab = sb(es, "adab", [P, 96], F32)
                nw = sb(es, "nw", [P, 2, KC], F32)
                wa = [sb(es, "wa%d" % i, [P, KC, 1024], BF16) for i in range(2)]
                t.dma('sp', cv[:], cvec, 'cv')
                t.dma('sp', adab[:], ada_b, 'adab')
                t.dma('sp', nw[:], normw, 'nw')
                t.op('act', lambda e: e.activation(out=scv[:], in_=cv[:], func=AF.Silu), r=('cv',), w=('scv',))
                psm = ps[7]
                for g in range(12):
                    s = g % 2
                    for hlf in range(2):
                        t.dma('pool', wa[s][:, hlf * 8:(hlf + 1) * 8, :],
                              ada_w[:, hlf * 8:(hlf + 1) * 8, g * 1024:(g + 1) * 1024], 'wa%d' % s)
                    for cl in range(8):
                        cc = g * 8 + cl
                        for kc in range(KC):
                            t.op('pe', lambda e, s=s, cl=cl, kc=kc, cc=cc: e.matmul(
                                psm[:, 2 * cc:2 * cc + 2], lhsT=wa[s][:, kc, cl * P:(cl + 1) * P], rhs=scv[:, kc, :],
                                start=(kc == 0), stop=(kc == KC - 1)), r=('wa%d' % s, 'scv'), w=('ps7',))
                psm3 = psm[:, 0:192].rearrange("p (c j) -> p c j", j=2)
                for j in range(2):
                    t.op('dve', lambda e, j=j: e.tensor_tensor(out=mod[:, :, j], in0=psm3[:, :, j], in1=adab[:, :], op=ALU.add),
                         r=('ps7', 'adab'), w=('mod',))
                for j in range(2):
                    t.op('dve', lambda e, j=j: e.scalar_tensor_tensor(out=se1[:, :, j], in0=mod[:, 16:32, j], scalar=1.0,
                                                                       in1=nw[:, 0, :], op0=ALU.add, op1=ALU.mult),
                         r=('mod', 'nw'), w=('se1',))
                    t.op('dve', lambda e, j=j: e.scalar_tensor_tensor(out=se2[:, :, j], in0=mod[:, 64:80, j], scalar=1.0,
                                                                       in1=nw[:, 1, :], op0=ALU.add, op1=ALU.mult),
                         r=('mod', 'nw'), w=('se2',))
                t.barrier()
        SH1, G1, SH2, G2 = 0, 32, 48, 80

        att = Scope()
        ost = Scope()
        oT = None
        if L0:
            NKA = CTXL + SEQ
            NKB = CTXL + 1280
            kTA = sb(att, "kTA", [P, 2, NKA], BF16)
            VA = sb(att, "VA", [P, NKA // P, 256], BF16)
            kTB = sb(att, "kTB", [P, 2, NKB], BF16)
            VB = sb(att, "VB", [P, NKB // P, 256], BF16)
            hkeep = sb(att, "hkeep", [P, KC, NTOK], BF16)
            qkw_t = sb(att, "qkw_t", [P, 2], F32)
            sinke = sb(att, "sinke", [P, 8], F32)
            rot_t = sb(att, "rot_t", [P, P], BF16)
            band_t = sb(att, "band_t", [P, 3, 384], F32)
            qf = [sb(att, "qf%d" % i, [P, 512], F32) for i in range(2)]
            qn = [sb(att, "qn%d" % i, [P, 512], F32) for i in range(2)]
            qb = [sb(att, "qb%d" % i, [P, 512], BF16) for i in range(2)]
            t.dma('sp', qkw_t[:], qkw, 'qkw_t')
            t.dma('sp', sinke[:], sinkb, 'sinke')
            t.dma('pool', rot_t[:], rotm, 'rot_t')
            t.dma('sp', band_t[:], bandb, 'band_t')
            t.op('act', lambda e: e.activation(out=sinke[:], in_=sinke[:], func=AF.Exp), r=('sinke',), w=('sinke',))

            def qk_post(ps_ap, psbuf, n, nwcol, cs_ap, sn_ap, csbuf, out_ap, outbuf, pss=7):
                i = rr(2, 'qf')
                t.op('act', lambda e: e.activation(out=qf[i][:, :n], in_=ps_ap, func=AF.Copy), r=(psbuf,), w=('qf%d' % i,))
                src, srcb = qf[i], 'qf%d' % i
                if nwcol is not None:
                    s2 = rr(2, 'sq')
                    t.op('pool', lambda e: e.tensor_tensor(out=sq[s2][:, :n], in0=qf[i][:, :n], in1=qf[i][:, :n], op=ALU.mult),
                         r=('qf%d' % i,), w=('sq%d' % s2,))
                    t.op('pe', lambda e: e.matmul(ps[pss][:, :n], lhsT=ones[:], rhs=sq[s2][:, :n], start=True, stop=True),
                         r=('sq%d' % s2, 'ones'), w=('ps%d' % pss,))
                    rs, rsb = rstd_from(ps[pss][:, :n], n, 1.0 / P, 'ps%d' % pss)
                    t.op('dve', lambda e: e.scalar_tensor_tensor(out=qn[i][:, :n], in0=qf[i][:, :n], scalar=qkw_t[:, nwcol:nwcol + 1],
                                                                 in1=rs, op0=ALU.mult, op1=ALU.mult),
                         r=('qf%d' % i, rsb, 'qkw_t'), w=('qn%d' % i,))
                    src, srcb = qn[i], 'qn%d' % i
                if cs_ap is None:
                    t.op('pool', lambda e: e.tensor_copy(out=out_ap, in_=src[:, :n]), r=(srcb,), w=(outbuf,))
                    return
                t.op('pool', lambda e: e.tensor_copy(out=qb[i][:, :n], in_=src[:, :n]), r=(srcb,), w=('qb%d' % i,))
                t.op('pe', lambda e: e.matmul(ps[pss][:, :n], lhsT=rot_t[:], rhs=qb[i][:, :n], start=True, stop=True),
                     r=('qb%d' % i, 'rot_t'), w=('ps%d' % pss,))
                a = rr(3, 'tmpf')
                t.op('dve', lambda e: e.tensor_tensor(out=tmpf[a][:, :n], in0=ps[pss][:, :n], in1=sn_ap, op=ALU.mult),
                     r=('ps%d' % pss, csbuf), w=('tmpf%d' % a,))
                t.op('pool', lambda e: e.tensor_tensor(out=src[:, :n], in0=src[:, :n], in1=cs_ap, op=ALU.mult),
                     r=(srcb, csbuf), w=(srcb,))
                t.op('pool', lambda e: e.tensor_tensor(out=out_ap, in0=src[:, :n], in1=tmpf[a][:, :n], op=ALU.add),
                     r=(srcb, 'tmpf%d' % a), w=(outbuf,))

            with Scope() as es:
                NB = 256
                xb = [sb(es, "xb%d" % i, [P, KC, NB], F32) for i in range(2)]
                hb = [sb(es, "hb%d" % i, [P, KC, NB], BF16) for i in range(2)]
                csb = [sb(es, "csb%d" % i, [P, 2, NB], F32) for i in range(2)]
                wkv = sb(es, "wkv", [P, KC, 1024], BF16)
                for i, c0 in enumerate((1024, 1280, 2560, 2816)):
                    t.dma('pool', wkv[:, :, i * 256:(i + 1) * 256], w_in[:, :, c0:c0 + 256], 'wkv')
                blocks0a = [0, 1, 2, 3, 4, 16] if fused else list(range(17))
                slot0a = {b_: i_ % 2 for i_, b_ in enumerate(blocks0a)}

                def stage_norm(bi):
                    isc = (bi == 16)
                    j = 1 if isc else 0
                    s = slot0a[bi]
                    if isc:
                        t.dma('sp', xb[s][:], cT[:, :, 0:NB], 'xb%d' % s)
                    else:
                        t.dma('sp', xb[s][:], xT[:, :, bi * NB:(bi + 1) * NB], 'xb%d' % s)
                        t.dma('sp', csb[s][:, 0, :], cosT[:, bi * NB:(bi + 1) * NB], 'csb%d' % s)
                        t.dma('sp', csb[s][:, 1, :], sinT[:, bi * NB:(bi + 1) * NB], 'csb%d' % s)
                    norm_blk(xb[s], 'xb%d' % s, NB, j, se1, SH1, hb[s], 'hb%d' % s, psi=4)

                def run_interleaved(gens):
                    gens = list(gens)
                    while gens:
                        nxt = []
                        for g_ in gens:
                            try:
                                next(g_)
                                nxt.append(g_)
                            except StopIteration:
                                pass
                        gens = nxt

                def k_chain(ci, s, cb, kv, norm, rope, out_ap, outbuf):
                    kb, kh = ci // 2, ci % 2
                    psK = ps[kb][:, kh * NB:(kh + 1) * NB]
                    psKn = 'ps%d' % kb
                    psP = ps[5 + kb][:, kh * NB:(kh + 1) * NB]
                    psPn = 'ps%d' % (5 + kb)
                    qf_ = qf[kb][:, kh * NB:(kh + 1) * NB]
                    qn_ = qn[kb][:, kh * NB:(kh + 1) * NB]
                    qb_ = qb[kb][:, kh * NB:(kh + 1) * NB]
                    qfn, qnn, qbn = 'qf_%d' % ci, 'qn_%d' % ci, 'qb_%d' % ci
                    for kc in range(KC):
                        t.op('pe', lambda e: e.matmul(psK, lhsT=wkv[:, kc, cb + kv * P:cb + (kv + 1) * P], rhs=hb[s][:, kc, :],
                                                      start=(kc == 0), stop=(kc == KC - 1)), r=('wkv', 'hb%d' % s), w=(psKn,))
                    yield
                    t.op('act', lambda e: e.activation(out=qf_, in_=psK, func=AF.Copy), r=(psKn,), w=(qfn,))
                    yield
                    src, srcn, oth, othn = qf_, qfn, qn_, qnn
                    if norm:
                        t.op('dve', lambda e: e.tensor_tensor(out=qb_, in0=qf_, in1=qf_, op=ALU.mult), r=(qfn,), w=(qbn,))
                        yield
                        t.op('pe', lambda e: e.matmul(psP, lhsT=ones[:], rhs=qb_, start=True, stop=True), r=(qbn, 'ones'), w=(psPn,))
                        yield
                        t.op('act', lambda e: e.activation(out=qn_, in_=psP, func=AF.Ln, bias=epst[:, 0:1], scale=1.0 / P),
                             r=(psPn, 'epst'), w=(qnn,))
                        yield
                        t.op('act', lambda e: e.activation(out=qn_, in_=qn_, func=AF.Exp, scale=-0.5), r=(qnn,), w=(qnn,))
                        yield
                        t.op('dve', lambda e: e.scalar_tensor_tensor(out=qn_, in0=qf_, scalar=qkw_t[:, 1:2], in1=qn_,
                                                                     op0=ALU.mult, op1=ALU.mult), r=(qfn, qnn, 'qkw_t'), w=(qnn,))
                        yield
                        src, srcn, oth, othn = qn_, qnn, qf_, qfn
                    if not rope:
                        t.op('pool', lambda e: e.tensor_copy(out=out_ap, in_=src), r=(srcn,), w=(outbuf,))
                        return
                    t.op('act', lambda e: e.activation(out=qb_, in_=src, func=AF.Copy), r=(srcn,), w=(qbn,))
                    yield
                    t.op('pe', lambda e: e.matmul(psP, lhsT=rot_t[:], rhs=qb_, start=True, stop=True), r=(qbn, 'rot_t'), w=(psPn,))
                    yield
                    t.op('dve', lambda e: e.tensor_tensor(out=oth, in0=psP, in1=csb[s][:, 1, :], op=ALU.mult),
                         r=(psPn, 'csb%d' % s), w=(othn,))
                    yield
                    t.op('pool', lambda e: e.tensor_tensor(out=src, in0=src, in1=csb[s][:, 0, :], op=ALU.mult),
                         r=(srcn, 'csb%d' % s), w=(srcn,))
                    yield
                    t.op('pool', lambda e: e.tensor_tensor(out=out_ap, in0=src, in1=oth, op=ALU.add), r=(srcn, othn), w=(outbuf,))

                def v_chain(vi, s, cb, tt, out_ap, outbuf):
                    vb_, vh = 2 + vi // 2, vi % 2
                    psV = ps[vb_][:, vh * 256:(vh + 1) * 256]
                    psVn = 'ps%d' % vb_
                    for kc in range(KC):
                        t.op('pe', lambda e: e.matmul(psV, lhsT=hb[s][:, kc, tt * P:(tt + 1) * P], rhs=wkv[:, kc, cb + 256:cb + 512],
                                                      start=(kc == 0), stop=(kc == KC - 1)), r=('wkv', 'hb%d' % s), w=(psVn,))
                    yield
                    t.op('act', lambda e: e.activation(out=out_ap, in_=psV, func=AF.Copy), r=(psVn,), w=(outbuf,))

                def stage_kv(bi):
                    isc = (bi == 16)
                    s = slot0a[bi]
                    pos = 0 if isc else CTXL + bi * NB
                    doB = isc or bi < 5
                    gens = []
                    ci = 0
                    vi = 0
                    for kind in range(2 if doB else 1):
                        kT_t, kname, V_t, vname = ((kTA, 'kTA', VA, 'VA'), (kTB, 'kTB', VB, 'VB'))[kind]
                        cb = kind * 512
                        for kv in range(2):
                            gens.append(k_chain(ci, s, cb, kv, kind == 0, not isc, kT_t[:, kv, pos:pos + NB], kname))
                            ci += 1
                        for tt in range(NB // P):
                            gens.append(v_chain(vi, s, cb, tt, V_t[:, pos // P + tt, :], vname))
                            vi += 1
                    run_interleaved(gens)
                    if isc:
                        t.op('pool', lambda e: e.tensor_copy(out=hkeep[:, :, NOWN:NOWN + 64], in_=hb[s][:, :, 0:64]),
                             r=('hb%d' % s,), w=('hkeep',))
                    else:
                        lo = max(bi * NB, OWN0)
                        hi = min((bi + 1) * NB, OWN0 + NOWN)
                        if hi > lo:
                            t.op('pool', lambda e: e.tensor_copy(out=hkeep[:, :, lo - OWN0:hi - OWN0],
                                                                 in_=hb[s][:, :, lo - bi * NB:hi - bi * NB]),
                                 r=('hb%d' % s,), w=('hkeep',))

                stage_norm(blocks0a[0])
                for i_, bi in enumerate(blocks0a):
                    if i_ + 1 < len(blocks0a):
                        stage_norm(blocks0a[i_ + 1])
                    stage_kv(bi)
                t.barrier()

            oT = sb(ost, "oT", [P, KC, NTOK], BF16)
            with Scope() as es:
                wq = [sb(es, "wq%d" % i, [P, KC, 512], BF16) for i in range(2)]
                qg0 = sb(es, "qg0", [P, 4, NTOK], BF16)
                qg = [qg0, qg0]
                cso = sb(es, "cso", [P, 2, NOWN], F32)
                t.dma('sp', cso[:, 0, :], cosT[:, OWN0:OWN0 + NOWN], 'cso')
                t.dma('sp', cso[:, 1, :], sinT[:, OWN0:OWN0 + NOWN], 'cso')

                seq = [2, 3, 0, 1] if fused else [0, 1, 2, 3]
                gslot = {g_: i_ % 2 for i_, g_ in enumerate(seq)}

                def load_wq(g):
                    c0w = (g * 512) if g < 2 else (1536 + (g - 2) * 512)
                    for hlf in range(2):
                        t.dma('pool', wq[gslot[g]][:, hlf * 8:(hlf + 1) * 8, :], w_in[:, hlf * 8:(hlf + 1) * 8, c0w:c0w + 512], 'wq%d' % gslot[g])

                def exchange_kv():
                    ks_d = nc.dram_tensor("ksend", [256, NOWN], BF16)
                    kg_d = nc.dram_tensor("kgath", [4 * 256, NOWN], BF16)
                    vs_d = nc.dram_tensor("vsend", [NOWN, 256], BF16)
                    vg_d = nc.dram_tensor("vgath", [4 * NOWN, 256], BF16)
                    for kv_ in range(2):
                        t.dma('sp', ks_d.ap()[kv_ * P:(kv_ + 1) * P, :], kTA[:, kv_, CTXL + OWN0:CTXL + OWN0 + NOWN], 'dram:ksend', src=('kTA',))
                    t.dma('sp', vs_d.ap().rearrange("(t p) c -> p t c", p=P), VA[:, 3:11, :], 'dram:vsend', src=('VA',))
                    for nm, s_d, g_d in (('k', ks_d, kg_d), ('v', vs_d, vg_d)):
                        t.collective(lambda e, s_d=s_d, g_d=g_d: e.collective_compute(
                            "AllGather", ALU.bypass, replica_groups=[[0, 1, 2, 3], [4, 5, 6, 7]],
                            ins=[s_d.ap().opt()], outs=[g_d.ap().opt()]), 'dram:%sgath' % nm, src=('dram:%ssend' % nm,))
                    for r_ in range(4):
                        for kv_ in range(2):
                            t.dma('sp', kTA[:, kv_, CTXL + r_ * NOWN:CTXL + (r_ + 1) * NOWN],
                                  kg_d.ap()[r_ * 256 + kv_ * P:r_ * 256 + (kv_ + 1) * P, :], 'kTA', src=('dram:kgath',))
                    for r_ in range(4):
                        t.dma('sp', VA[:, 2 + r_ * 8:2 + (r_ + 1) * 8, :],
                              vg_d.ap()[r_ * NOWN:(r_ + 1) * NOWN, :].rearrange("(t p) c -> p t c", p=P), 'VA', src=('dram:vgath',))
                qf2 = sb(es, "qf2", [P, 64], F32)
                qn2 = sb(es, "qn2", [P, 64], F32)
                qb2 = sb(es, "qb2", [P, 64], BF16)

                def q_chain(ci, g, hh, tb0, n):
                    isA = g < 2
                    s = gslot[g]
                    lat = tb0 < NOWN
                    pp = (7, 3, 5)[ci]
                    psQ, psQn = ps[ci][:, :n], 'ps%d' % ci
                    psP, psPn = ps[pp][:, :n], 'ps%d' % pp
                    qf_, qn_, qb_ = [(qf[0], qn[0], qb[0]), (qf[1], qn[1], qb[1]), (qf2, qn2, qb2)][ci]
                    qf_, qn_, qb_ = qf_[:, :n], qn_[:, :n], qb_[:, :n]
                    qfn, qnn, qbn = 'qcf%d' % ci, 'qcn%d' % ci, 'qcb%d' % ci
                    out_ap, outbuf = qg0[:, hh, tb0:tb0 + n], 'qg_h%d' % hh
                    for kc in range(KC):
                        t.op('pe', lambda e: e.matmul(psQ, lhsT=wq[s][:, kc, hh * P:(hh + 1) * P], rhs=hkeep[:, kc, tb0:tb0 + n],
                                                      start=(kc == 0), stop=(kc == KC - 1)), r=('wq%d' % s, 'hkeep'), w=(psQn,))
                    yield
                    t.op('act', lambda e: e.activation(out=qf_, in_=psQ, func=AF.Copy), r=(psQn,), w=(qfn,))
                    yield
                    src, srcn, oth, othn = qf_, qfn, qn_, qnn
                    if isA:
                        t.op('dve', lambda e: e.tensor_tensor(out=qb_, in0=qf_, in1=qf_, op=ALU.mult), r=(qfn,), w=(qbn,))
                        yield
                        t.op('pe', lambda e: e.matmul(psP, lhsT=ones[:], rhs=qb_, start=True, stop=True), r=(qbn, 'ones'), w=(psPn,))
                        yield
                        t.op('act', lambda e: e.activation(out=qn_, in_=psP, func=AF.Ln, bias=epst[:, 0:1], scale=1.0 / P),
                             r=(psPn, 'epst'), w=(qnn,))
                        yield
                        t.op('act', lambda e: e.activation(out=qn_, in_=qn_, func=AF.Exp, scale=-0.5), r=(qnn,), w=(qnn,))
                        yield
                        t.op('dve', lambda e: e.scalar_tensor_tensor(out=qn_, in0=qf_, scalar=qkw_t[:, 0:1], in1=qn_,
                                                                     op0=ALU.mult, op1=ALU.mult), r=(qfn, qnn, 'qkw_t'), w=(qnn,))
                        yield
                        src, srcn, oth, othn = qn_, qnn, qf_, qfn
                    if not lat:
                        t.op('pool', lambda e: e.tensor_copy(out=out_ap, in_=src), r=(srcn,), w=(outbuf,))
                        return
                    t.op('act', lambda e: e.activation(out=qb_, in_=src, func=AF.Copy), r=(srcn,), w=(qbn,))
                    yield
                    t.op('pe', lambda e: e.matmul(psP, lhsT=rot_t[:], rhs=qb_, start=True, stop=True), r=(qbn, 'rot_t'), w=(psPn,))
                    yield
                    t.op('dve', lambda e: e.tensor_tensor(out=oth, in0=psP, in1=cso[:, 1, tb0:tb0 + n], op=ALU.mult),
                         r=(psPn, 'cso'), w=(othn,))
                    yield
                    t.op('pool', lambda e: e.tensor_tensor(out=src, in0=src, in1=cso[:, 0, tb0:tb0 + n], op=ALU.mult),
                         r=(srcn, 'cso'), w=(srcn,))
                    yield
                    t.op('pool', lambda e: e.tensor_tensor(out=out_ap, in0=src, in1=oth, op=ALU.add), r=(srcn, othn), w=(outbuf,))

                def qproj(g, hh):
                    run_interleaved([q_chain(ci, g, hh, tb0, n) for ci, (tb0, n) in enumerate(TBS)])

                def jobs(g, hh):
                    isA = g < 2
                    kv = g % 2
                    hidx = g * 4 + hh
                    for bi_, (tb0, n) in enumerate(TBS):
                        lat = tb0 < NOWN
                        items = []
                        kT_t, kname, V_t, vname = (kTA, 'kTA', VA, 'VA') if isA else (kTB, 'kTB', VB, 'VB')
                        for ct in range(2):
                            items.append(dict(kT=kT_t[:, kv, ct * P:(ct + 1) * P], kbuf=kname,
                                              v=V_t[:, ct, kv * P:(kv + 1) * P], vbuf=vname, c0=0, n=n))
                        if lat and isA:
                            for kt in range(2, NKA // P):
                                items.append(dict(kT=kT_t[:, kv, kt * P:(kt + 1) * P], kbuf=kname,
                                                  v=V_t[:, kt, kv * P:(kv + 1) * P], vbuf=vname, c0=0, n=n))
                        elif lat:
                            m = bi_
                            for jj in range(4 * m, 4 * m + 6):
                                lo = max(jj - 2, 4 * m)
                                hi = min(jj, 4 * m + 3)
                                sel = 1 if jj == 0 else (2 if jj == 9 else 0)
                                bc0 = (lo - (jj - 2)) * P
                                nn = (hi - lo + 1) * P
                                items.append(dict(kT=kT_t[:, kv, CTXL + jj * P:CTXL + (jj + 1) * P], kbuf=kname,
                                                  v=V_t[:, 2 + jj, kv * P:(kv + 1) * P], vbuf=vname,
                                                  c0=(lo - 4 * m) * P, n=nn, bias=band_t[:, sel, bc0:bc0 + nn], bbuf='band_t'))
                        extra = None if isA else sinke[:, (g - 2) * 4 + hh:(g - 2) * 4 + hh + 1]
                        attn_job(qg0[:, hh, tb0:tb0 + n], 'qg_h%d' % hh, n, items, oT[:, hidx, tb0:tb0 + n], 'oT', extra=extra)

                load_wq(seq[0])
                load_wq(seq[1])
                if fused:
                    exchange_kv()
                for hh in range(4):
                    qproj(seq[0], hh)
                for i_, g in enumerate(seq):
                    if i_ >= 1 and i_ + 1 < 4:
                        load_wq(seq[i_ + 1])
                    for hh in range(4):
                        jobs(g, hh)
                        if i_ + 1 < 4:
                            qproj(seq[i_ + 1], hh)
                t.barrier()
        else:
            NEXT = TEXT + CTXL
            hext = sb(att, "hext", [P, KC, NEXT], BF16)
            if fused:
                with Scope() as es:
                    NB = 256
                    xo_prev = carry['xo']
                    hs_d = [nc.dram_tensor("hsend%d" % i, [D, 256 if i < 2 else 64], BF16) for i in range(3)]
                    hg_d = [nc.dram_tensor("hgath%d" % i, [4 * D, 256 if i < 2 else 64], BF16) for i in range(3)]
                    hc = sb(es, "hc", [P, KC, 64], BF16)
                    gb = [sb(es, "gb%d" % i, [P, KC, NB], BF16) for i in range(2)]
                    sel_t = sb(es, "sel_t", [P, 8], F32)
                    t.dma('sp', sel_t[:], selv, 'sel_t')

                    def own_blk(bi):
                        norm_blk(xo_prev[:, :, (bi - 1) * NB:bi * NB], 'xo', NB, 0, se1, SH1,
                                 hext[:, :, bi * NB:(bi + 1) * NB], 'hext', psi=6)
                    own_blk(1)
                    own_blk(4)
                    norm_blk(xo_prev[:, :, NOWN:NOWN + 64], 'xo', 64, 1, se1, SH1, hc, 'hc', psi=6)
                    srcs = (hext[:, :, 256:512], hext[:, :, 1024:1280], hc[:])
                    for i_ in range(3):
                        t.dma('sp', hs_d[i_].ap().rearrange("(c p) t -> p c t", p=P), srcs[i_], 'dram:hsend%d' % i_,
                              src=('hext' if i_ < 2 else 'hc',))
                    for i_ in range(3):
                        t.collective(lambda e, i_=i_: e.collective_compute(
                            "AllGather", ALU.bypass, replica_groups=[[0, 1, 2, 3], [4, 5, 6, 7]],
                            ins=[hs_d[i_].ap().opt()], outs=[hg_d[i_].ap().opt()]),
                            'dram:hgath%d' % i_, src=('dram:hsend%d' % i_,))
                    own_blk(2)
                    own_blk(3)
                    for (gi_, so, c0_) in ((1, 0, 0), (0, 4, 1280)):
                        dst = hext[:, :, c0_:c0_ + 256]
                        for r_ in range(4):
                            g_ = r_ % 2
                            t.dma('sp', gb[g_][:], hg_d[gi_].ap()[r_ * D:(r_ + 1) * D, :].rearrange("(c p) t -> p c t", p=P),
                                  'gb%d' % g_, src=('dram:hgath%d' % gi_,))
                            if r_ == 0:
                                t.op('dve', lambda e: e.tensor_scalar(out=dst, in0=gb[g_][:], scalar1=sel_t[:, so + r_:so + r_ + 1],
                                                                      scalar2=None, op0=ALU.mult),
                                     r=('gb%d' % g_, 'sel_t'), w=('hext',))
                            else:
                                t.op('dve', lambda e: e.scalar_tensor_tensor(out=dst, in0=gb[g_][:], scalar=sel_t[:, so + r_:so + r_ + 1],
                                                                             in1=dst, op0=ALU.mult, op1=ALU.add),
                                     r=('gb%d' % g_, 'sel_t', 'hext'), w=('hext',))
                    for r_ in range(4):
                        t.dma('sp', hext[:, :, TEXT + r_ * 64:TEXT + (r_ + 1) * 64],
                              hg_d[2].ap()[r_ * D:(r_ + 1) * D, :].rearrange("(c p) t -> p c t", p=P), 'hext', src=('dram:hgath2',))
                    t.barrier()
                carry['post'].close()
            else:
                with Scope() as es:
                    NB = 256
                    xb = [sb(es, "xb%d" % i, [P, KC, NB], F32) for i in range(2)]
                    for bi in range(7):
                        isc = (bi == 6)
                        s = bi % 2
                        if isc:
                            t.dma('sp', xb[s][:], cT[:, :, 0:NB], 'xb%d' % s)
                        else:
                            t.dma('sp', xb[s][:], xT[:, :, bi * NB:(bi + 1) * NB], 'xb%d' % s)
                        norm_blk(xb[s], 'xb%d' % s, NB, 1 if isc else 0, se1, SH1, hext[:, :, bi * NB:(bi + 1) * NB], 'hext', psi=6)
                    t.barrier()
            oT = sb(ost, "oT", [P, KC, NTOK], BF16)
            with Scope() as es:
                wg = [sb(es, "wg%d" % i, [P, KC, 768], BF16) for i in range(2)]
                qg0 = sb(es, "qg0", [P, 2, NOWN], BF16)
                kg0 = sb(es, "kg0", [P, 2, NEXT], BF16)
                vg0 = sb(es, "vg0", [P, NEXT // P, 256], BF16)
                qg, kg, vg = [qg0, qg0], [kg0, kg0], [vg0, vg0]
                bt = [sb(es, "bt%d" % i, [P, 17, P], F32) for i in range(2)]
                bjobs = [(h_, m_) for h_ in range(16) for m_ in range(2)]

                def load_bt(ji):
                    h_, m_ = bjobs[ji]
                    t.dma('sp', bt[ji % 2][:], biasT[h_, :, 12 * m_:12 * m_ + 17, :], 'bt%d' % (ji % 2))

                def load_wg(g):
                    for i3 in range(3):
                        t.dma('pool', wg[g % 2][:, :, i3 * 256:(i3 + 1) * 256],
                              w_in[:, :, i3 * 2048 + g * 256:i3 * 2048 + (g + 1) * 256], 'wg%d' % (g % 2))
                load_wg(0)
                load_bt(0)

                def slot_of(p, tt):
                    if p == 0:
                        return tt
                    if p == 1:
                        return 6 + tt
                    if p == 6:
                        return 17 + (tt + 1)
                    if p == 7:
                        return 23 + (tt + 1)
                    return 12 + tt

                def tset(p):
                    if p in (0, 1):
                        return range(0, 6)
                    if p in (6, 7):
                        return range(-1, 5)
                    return range(0, 5)

                for g in range(8):
                    s = g % 2
                    if g + 1 < 8:
                        load_wg(g + 1)
                    for hh in range(2):
                        for (tb0, n) in TBS:
                            pk = rr(3, 'pkA')
                            for kc in range(KC):
                                t.op('pe', lambda e, kc=kc, hh=hh, pk=pk, tb0=tb0, n=n: e.matmul(
                                    ps[pk][:, :n], lhsT=wg[s][:, kc, hh * P:(hh + 1) * P], rhs=hext[:, kc, OWN0 + tb0:OWN0 + tb0 + n],
                                    start=(kc == 0), stop=(kc == KC - 1)), r=('wg%d' % s, 'hext'), w=('ps%d' % pk,))
                            t.op('act', lambda e, hh=hh, pk=pk, tb0=tb0, n=n: e.activation(out=qg[s][:, hh, tb0:tb0 + n], in_=ps[pk][:, :n], func=AF.Copy),
                                 r=('ps%d' % pk,), w=('qg',))
                        for (k0, n) in ((0, 512), (512, 512), (1024, 512), (1536, 256)):
                            pk = rr(3, 'pkA')
                            for kc in range(KC):
                                t.op('pe', lambda e, kc=kc, hh=hh, pk=pk, k0=k0, n=n: e.matmul(
                                    ps[pk][:, :n], lhsT=wg[s][:, kc, 256 + hh * P:256 + (hh + 1) * P], rhs=hext[:, kc, k0:k0 + n],
                                    start=(kc == 0), stop=(kc == KC - 1)), r=('wg%d' % s, 'hext'), w=('ps%d' % pk,))
                            t.op('dve', lambda e, hh=hh, pk=pk, k0=k0, n=n: e.tensor_copy(out=kg[s][:, hh, k0:k0 + n], in_=ps[pk][:, :n]),
                                 r=('ps%d' % pk,), w=('kg',))
                    for tt in range(NEXT // P):
                        pk = 3 + rr(3, 'pkB')
                        for kc in range(KC):
                            t.op('pe', lambda e, kc=kc, tt=tt, pk=pk: e.matmul(
                                ps[pk][:, :256], lhsT=hext[:, kc, tt * P:(tt + 1) * P], rhs=wg[s][:, kc, 512:768],
                                start=(kc == 0), stop=(kc == KC - 1)), r=('wg%d' % s, 'hext'), w=('ps%d' % pk,))
                        t.op('act', lambda e, tt=tt, pk=pk: e.activation(out=vg[s][:, tt, :], in_=ps[pk][:, :256], func=AF.Copy),
                             r=('ps%d' % pk,), w=('vg',))
                    for hh in range(2):
                        hidx = g * 2 + hh
                        for m, (tb0, n) in enumerate(TBS):
                            ji = hidx * 2 + m
                            b_ = ji % 2
                            if ji + 1 < len(bjobs):
                                load_bt(ji + 1)
                            items = []
                            for ct in range(2):
                                kt = TEXT // P + ct
                                items.append(dict(kT=kg[s][:, hh, kt * P:(kt + 1) * P], kbuf='kg',
                                                  v=vg[s][:, kt, hh * P:(hh + 1) * P], vbuf='vg', c0=0, n=n))
                            for p_ in range(4 * m, 4 * m + 4):
                                tl = list(tset(p_))
                                for grp in (tl[0:3], tl[3:]):
                                    subs = []
                                    for tt in grp:
                                        jj = p_ + tt
                                        subs.append((kg[s][:, hh, jj * P:(jj + 1) * P], 'kg', vg[s][:, jj, hh * P:(hh + 1) * P], 'vg'))
                                    sl0 = slot_of(p_, grp[0]) - 12 * m
                                    items.append(dict(subs=subs, c0=(p_ - 4 * m) * P, n=P,
                                                      bias=bt[b_][:, sl0:sl0 + len(grp), :].rearrange("p a b -> p (a b)"), bbuf='bt%d' % b_))
                            attn_job(qg[s][:, hh, tb0:tb0 + n], 'qg', n, items, oT[:, hidx, tb0:tb0 + n], 'oT')
                t.barrier()

        att.close()
        post = Scope()
        xo = sb(post, "xo", [P, KC, NTOK], F32)
        xoff = 0 if (fused and not L0) else OWN0
        xsrc = ('dram:x1own',) if (fused and not L0) else ()
        t.dma('sp', xo[:, 0:8, 0:NOWN], xT[:, 0:8, xoff:xoff + NOWN], 'xo', src=xsrc)
        t.dma('sp', xo[:, 8:16, 0:NOWN], xT[:, 8:16, xoff:xoff + NOWN], 'xo', src=xsrc)
        if L0:
            t.dma('sp', xo[:, :, NOWN:NTOK], cT[:, :, 0:64], 'xo')
        with Scope() as es:
            wo = [sb(es, "wo%d" % i, [P, KC, 512], BF16) for i in range(2)]

            def load_wo(dg):
                for hlf in range(2):
                    t.dma('pool', wo[dg % 2][:, hlf * 8:(hlf + 1) * 8, :], w_out[:, hlf * 8:(hlf + 1) * 8, dg * 512:(dg + 1) * 512], 'wo%d' % (dg % 2))
            load_wo(0)
            for dg in range(4):
                s = dg % 2
                if dg + 1 < 4:
                    load_wo(dg + 1)
                for dl in range(4):
                    d = dg * 4 + dl
                    for (tb0, n) in TBS:
                        j = 0 if tb0 < NOWN else 1
                        pk = rr(6, 'pk6')
                        for kc in range(KC):
                            t.op('pe', lambda e, kc=kc, dl=dl, pk=pk, tb0=tb0, n=n: e.matmul(
                                ps[pk][:, :n], lhsT=wo[s][:, kc, dl * P:(dl + 1) * P], rhs=oT[:, kc, tb0:tb0 + n],
                                start=(kc == 0), stop=(kc == KC - 1)), r=('wo%d' % s, 'oT'), w=('ps%d' % pk,))
                        t.op('dve', lambda e, d=d, pk=pk, tb0=tb0, n=n, j=j: e.scalar_tensor_tensor(
                            out=xo[:, d, tb0:tb0 + n], in0=ps[pk][:, :n], scalar=mod[:, G1 + d, j:j + 1], in1=xo[:, d, tb0:tb0 + n],
                            op0=ALU.mult, op1=ALU.add), r=('ps%d' % pk, 'xo', 'mod'), w=('xo',))
            t.barrier()
        ost.close()

        with Scope() as es:
            h2 = sb(es, "h2", [P, KC, NTOK], BF16)
            for (tb0, n) in TBS:
                j = 0 if tb0 < NOWN else 1
                norm_blk(xo[:, :, tb0:tb0 + n], 'xo', n, j, se2, SH2, h2[:, :, tb0:tb0 + n], 'h2', psi=6)
            wA = [sb(es, "wA%d" % i, [P, KC, 512], BF16) for i in range(2)]
            wB = [sb(es, "wB%d" % i, [P, 4, D], BF16) for i in range(2)]
            uT0 = sb(es, "uT0", [P, 4, NTOK], BF16)
            if L0:
                uT, uTn = [uT0, uT0], ['uT0', 'uT0']
            else:
                uT, uTn = [uT0, sb(es, "uT1", [P, 4, NTOK], BF16)], ['uT0', 'uT1']
            rl = [sb(es, "rl%d" % i, [P, 512], BF16) for i in range(2)]
            NBLK = DFF // 512

            def load_mlp(blk):
                s_ = blk % 2
                for hlf in range(2):
                    t.dma('pool', wA[s_][:, hlf * 8:(hlf + 1) * 8, :], w1[:, hlf * 8:(hlf + 1) * 8, blk * 512:(blk + 1) * 512], 'wA%d' % s_)
                for hlf in range(2):
                    t.dma('pool', wB[s_][:, hlf * 2:(hlf + 1) * 2, :], w2[:, blk * 4 + hlf * 2:blk * 4 + (hlf + 1) * 2, :], 'wB%d' % s_)
            load_mlp(0)
            for blk in range(NBLK):
                s = blk % 2
                if blk + 1 < NBLK:
                    load_mlp(blk + 1)
                for m in range(4):
                    for (tb0, n) in TBS:
                        pk = rr(4, 'pk4a')
                        for kc in range(KC):
                            t.op('pe', lambda e, kc=kc, m=m, pk=pk, tb0=tb0, n=n: e.matmul(
                                ps[pk][:, :n], lhsT=wA[s][:, kc, m * P:(m + 1) * P], rhs=h2[:, kc, tb0:tb0 + n],
                                start=(kc == 0), stop=(kc == KC - 1)), r=('wA%d' % s, 'h2'), w=('ps%d' % pk,))
                        ri = rr(2, 'rl')
                        t.op('act', lambda e, pk=pk, n=n, ri=ri: e.activation(out=rl[ri][:, :n], in_=ps[pk][:, :n], func=AF.Relu),
                             r=('ps%d' % pk,), w=('rl%d' % ri,))
                        t.op('pool', lambda e, m=m, tb0=tb0, n=n, ri=ri: e.tensor_tensor(out=uT[s][:, m, tb0:tb0 + n], in0=rl[ri][:, :n],
                                                                                         in1=rl[ri][:, :n], op=ALU.mult),
                             r=('rl%d' % ri,), w=(uTn[s],))
                for d in range(KC):
                    for (tb0, n) in TBS:
                        j = 0 if tb0 < NOWN else 1
                        pk = 4 + rr(4, 'pk4b')
                        for m in range(4):
                            t.op('pe', lambda e, m=m, d=d, pk=pk, tb0=tb0, n=n: e.matmul(
                                ps[pk][:, :n], lhsT=wB[s][:, m, d * P:(d + 1) * P], rhs=uT[s][:, m, tb0:tb0 + n],
                                start=(m == 0), stop=(m == 3)), r=('wB%d' % s, uTn[s]), w=('ps%d' % pk,))
                        t.op('dve', lambda e, d=d, pk=pk, tb0=tb0, n=n, j=j: e.scalar_tensor_tensor(
                            out=xo[:, d, tb0:tb0 + n], in0=ps[pk][:, :n], scalar=mod[:, G2 + d, j:j + 1], in1=xo[:, d, tb0:tb0 + n],
                            op0=ALU.mult, op1=ALU.add), r=('ps%d' % pk, 'xo', 'mod'), w=('xo',))
            t.barrier()

        if L0 and fused:
            x1o = carry['x1own'].ap().rearrange("(c p) t -> p c t", p=P)
            for hlf in range(2):
                t.dma('sp', x1o[:, hlf * 8:(hlf + 1) * 8, :], xo[:, hlf * 8:(hlf + 1) * 8, 0:NOWN], 'dram:x1own', src=('xo',))
            carry['xo'] = xo
            carry['post'] = post
        elif L0:
            for hlf in range(2):
                t.dma('sp', x1T[:, hlf * 8:(hlf + 1) * 8, :], xo[:, hlf * 8:(hlf + 1) * 8, 0:NOWN], 'out:x1', src=('xo',))
            t.dma('sp', c1T, xo[:, :, NOWN:NTOK], 'out:c1', src=('xo',))
        else:
            with Scope() as es:
                fn_t = sb(es, "fn_t", [P, KC], F32)
                t.dma('sp', fn_t[:], fnw, 'fn_t')
                for (tb0, n) in TBS:
                    for c in range(KC):
                        i = c % 4
                        t.op('act', lambda e, c=c, i=i, tb0=tb0, n=n: e.activation(out=sq[i][:, :n], in_=xo[:, c, tb0:tb0 + n], func=AF.Square),
                             r=('xo',), w=('sq%d' % i,))
                        t.op('pe', lambda e, c=c, i=i, n=n: e.matmul(ps[6][:, :n], lhsT=ones[:], rhs=sq[i][:, :n],
                                                                    start=(c == 0), stop=(c == KC - 1)),
                             r=('sq%d' % i, 'ones'), w=('ps6',))
                    rs, rsb = rstd_from(ps[6][:, :n], n, 1.0 / D, 'ps6')
                    for c in range(KC):
                        t.op('dve', lambda e, c=c, tb0=tb0, n=n: e.scalar_tensor_tensor(
                            out=xo[:, c, tb0:tb0 + n], in0=xo[:, c, tb0:tb0 + n], scalar=fn_t[:, c:c + 1], in1=rs,
                            op0=ALU.mult, op1=ALU.mult), r=('xo', rsb, 'fn_t'), w=('xo',))
                for hlf in range(2):
                    t.dma('sp', outT[:, hlf * 8:(hlf + 1) * 8, :], xo[:, hlf * 8:(hlf + 1) * 8, :], 'out:o', src=('xo',))
                t.barrier()
        if not (L0 and fused):
            t.barrier()
            post.close()

    if fused:
        ada_wq = [din0("ada_wq_%d" % L_, [D, 3072]).rearrange("(c p) n -> p c n", p=P) for L_ in range(2)]
        ada_bq = din0("ada_bq", [P, 48])
        normw2 = din0("normw2", [P, 2, 2, KC])
        msend = nc.dram_tensor("msend", [P, 96], F32)
        mgath = nc.dram_tensor("mgath", [4 * P, 96], F32)
        with Scope() as es:
            cv = sb(es, "cv", [P, KC, 2], F32)
            scv = sb(es, "scv", [P, KC, 2], BF16)
            adab = sb(es, "adab", [P, 48], F32)
            nw = sb(es, "nw", [P, 4, KC], F32)
            part = sb(es, "part", [P, 96], F32)
            wa = [sb(es, "wa%d" % i, [P, KC, 1024], BF16) for i in range(2)]
            t.dma('sp', cv[:], cvec, 'cv')
            t.dma('sp', adab[:], ada_bq, 'adab')
            t.dma('sp', nw[:], normw2.rearrange("p l s c -> p (l s) c"), 'nw')
            t.op('act', lambda e: e.activation(out=scv[:], in_=cv[:], func=AF.Silu), r=('cv',), w=('scv',))
            psm = ps[7]
            gi = 0
            for L_ in range(2):
                for g in range(3):
                    s_ = gi % 2
                    gi += 1
                    for hlf in range(2):
                        t.dma('pool', wa[s_][:, hlf * 8:(hlf + 1) * 8, :],
                              ada_wq[L_][:, hlf * 8:(hlf + 1) * 8, g * 1024:(g + 1) * 1024], 'wa%d' % s_)
                    for cl in range(8):
                        cc = L_ * 24 + g * 8 + cl
                        for kc in range(KC):
                            t.op('pe', lambda e: e.matmul(
                                psm[:, 2 * cc:2 * cc + 2], lhsT=wa[s_][:, kc, cl * P:(cl + 1) * P], rhs=scv[:, kc, :],
                                start=(kc == 0), stop=(kc == KC - 1)), r=('wa%d' % s_, 'scv'), w=('ps7',))
            psm3 = psm[:, 0:96].rearrange("p (c j) -> p c j", j=2)
            part3 = part[:].rearrange("p (c j) -> p c j", j=2)
            for j in range(2):
                t.op('dve', lambda e: e.tensor_tensor(out=part3[:, :, j], in0=psm3[:, :, j], in1=adab[:, :], op=ALU.add),
                     r=('ps7', 'adab'), w=('part',))
            t.dma('sp', msend.ap(), part[:], 'dram:msend', src=('part',))
            t.collective(lambda e: e.collective_compute("AllGather", ALU.bypass, replica_groups=[[0, 1, 2, 3], [4, 5, 6, 7]],
                                                        ins=[msend.ap().opt()], outs=[mgath.ap().opt()]),
                         'dram:mgath', src=('dram:msend',))
            mg = mgath.ap().rearrange("(r p) f -> p r f", p=P)
            for L_ in range(2):
                md = modsL[L_]['mod']
                t.dma('sp', md[:].rearrange("p (r c) j -> p r (c j)", r=4), mg[:, :, L_ * 48:(L_ + 1) * 48], 'mod', src=('dram:mgath',))
            for L_ in range(2):
                md = modsL[L_]['mod']
                for j in range(2):
                    t.op('dve', lambda e: e.scalar_tensor_tensor(out=modsL[L_]['se1'][:, :, j], in0=md[:, 16:32, j], scalar=1.0,
                                                                 in1=nw[:, L_ * 2, :], op0=ALU.add, op1=ALU.mult),
                         r=('mod', 'nw'), w=('se1',))
                    t.op('dve', lambda e: e.scalar_tensor_tensor(out=modsL[L_]['se2'][:, :, j], in0=md[:, 64:80, j], scalar=1.0,
                                                                 in1=nw[:, L_ * 2 + 1, :], op0=ALU.add, op1=ALU.mult),
                         r=('mod', 'nw'), w=('se2',))
            t.barrier()

    for layer_ in layers:
        emit_layer(layer_)
    t.barrier()
    top.close()
    t.es.close()
    return nc


def _rope_tables():
    tt = np.arange(SEQ)
    row = (tt // 64).astype(np.float32)
    col = (tt % 64).astype(np.float32)
    inv = (np.float32(10000.0) ** (-np.arange(32, dtype=np.float32) / np.float32(32))).astype(np.float32)
    ang_r = row[:, None] * inv
    ang_c = col[:, None] * inv
    ang = np.concatenate([ang_r, ang_r, ang_c, ang_c], axis=-1).astype(np.float32)
    cos = np.cos(ang).astype(np.float32)
    sin = np.sin(ang).astype(np.float32)
    sgn = np.ones(128, np.float32)
    sgn[0:32] = -1.0
    sgn[64:96] = -1.0
    return cos.T.copy(), (sin * sgn[None, :]).T.copy()


def _rot_matrix():
    R = np.zeros((128, 128), np.float32)
    for base in (0, 64):
        for d in range(32):
            R[base + d + 32, base + d] = 1.0
            R[base + d, base + d + 32] = 1.0
    return R


def _band_tables(qt):
    p = np.arange(128)[:, None]
    f = np.arange(128)[None, :]
    m = np.zeros((128, 384), np.float32)
    m[:, 0:128] = np.where(p <= f, 0.0, NEG)
    m[:, 256:384] = np.where(p >= f, 0.0, NEG)
    full = np.full((128, 384), NEG, np.float32)
    out = np.stack([m, m if qt > 0 else full, m if qt < 3 else full], axis=1)
    return np.ascontiguousarray(out.astype(np.float32))


def _natten_bias(rpb, qt):
    R0 = 16 * qt
    H = rpb.shape[0]
    out = np.full((H, 128, 29, 128), NEG, np.float32)

    def tset(p):
        if p in (0, 1):
            return list(range(0, 6)), (0 if p == 0 else 6), 0
        if p in (6, 7):
            return list(range(-1, 5)), (17 if p == 6 else 23), -1
        return list(range(0, 5)), 12, 0
    w = np.arange(64)
    cq = np.arange(64)
    cs = np.clip(cq - 8, 0, 48)
    col_valid = (w[None, :] >= cs[:, None]) & (w[None, :] < cs[:, None] + 16)
    ci = np.clip(w[None, :] - cq[:, None] + 15, 0, 30)
    for p in (0, 1, 2, 6, 7):
        ts, base, t0 = tset(p)
        r0 = R0 + 2 * p
        for tt in ts:
            slot = base + (tt - t0)
            for a in range(2):
                rq = r0 + a
                rs = min(max(rq - 4, 0), 56)
                for b in range(2):
                    rk = r0 - 4 + 2 * tt + b
                    if rk < 0 or rk >= 64 or rk < rs or rk >= rs + 8:
                        continue
                    ri = rk - rq + 7
                    vals = rpb[:, ri, :][:, ci]
                    vals = np.where(col_valid[None], vals, NEG)
                    out[:, b * 64:(b + 1) * 64, slot, a * 64:(a + 1) * 64] = np.transpose(vals, (0, 2, 1))
    return out


_NC_CACHE = {}


def _get_nc(layer):
    if layer not in _NC_CACHE:
        _NC_CACHE[layer] = build(layer)
    return _NC_CACHE[layer]


def _fm(a):
    return np.ascontiguousarray(a.T)


def layer_inputs(layer, core, x, ctx, c, c_ctx, ada_w, ada_b, norm_w, mlp_w1, mlp_w2, w_in, w_out, extra):
    b, qt = core // 4, core % 4
    m = {}
    m["cvec"] = np.ascontiguousarray(np.stack([c[b], c_ctx], axis=1).astype(np.float32))
    m["ada_w"] = ada_w[layer]
    m["ada_b"] = np.ascontiguousarray(ada_b[layer].reshape(96, 128).T)
    m["normw"] = np.ascontiguousarray(norm_w[layer].reshape(2, KC, 128).transpose(2, 0, 1))
    m["w_in"] = w_in
    m["w_out"] = w_out
    m["w1"] = mlp_w1[layer]
    m["w2"] = mlp_w2[layer]
    if layer == 0:
        shift = 1024 * qt - 128
        m["xT"] = _fm(np.roll(x[b], -shift, axis=0))
        m["cT"] = _fm(np.roll(ctx[b], -64 * qt, axis=0))
        cosT, sinT = extra["rope"]
        m["cosT"] = np.ascontiguousarray(np.roll(cosT, -shift, axis=1))
        m["sinT"] = np.ascontiguousarray(np.roll(sinT, -shift, axis=1))
        m["rotm"] = extra["rotm"]
        m["bandb"] = _band_tables(qt)
        m["qkw"] = np.ascontiguousarray(np.stack([extra["qn"], extra["kn"]], axis=1).astype(np.float32))
        m["sinkb"] = np.ascontiguousarray(np.broadcast_to(extra["sink"][None, :], (128, 8)).astype(np.float32))
    else:
        xe = np.zeros((1536, D), np.float32)
        lo = 1024 * qt - 256
        hi = lo + 1536
        slo, shi = max(lo, 0), min(hi, SEQ)
        xe[slo - lo:shi - lo] = x[b][slo:shi]
        m["xT"] = _fm(xe)
        m["cT"] = _fm(ctx[b])
        m["biasT"] = _natten_bias(extra["rpb"], qt)
        m["fnw"] = np.ascontiguousarray(extra["fnw"].reshape(KC, 128).T)
    return m


FUSED = True


def kernel(x, c, ctx, c_ctx, ada_w, ada_b, norm_w, mlp_w1, mlp_w2, ev_w_in, ev_w_out,
           ev_q_norm, ev_k_norm, ev_sink, od_w_in, od_w_out, od_rpb, final_norm_w):
    f = lambda a: np.asarray(a, dtype=np.float32)
    x, c, ctx, c_ctx = f(x), f(c), f(ctx), f(c_ctx)
    ada_w, ada_b, norm_w, mlp_w1, mlp_w2 = f(ada_w), f(ada_b), f(norm_w), f(mlp_w1), f(mlp_w2)
    cores = list(range(8))
    cosT, sinT = _rope_tables()
    ex0 = dict(rope=(cosT, sinT), rotm=_rot_matrix(), qn=f(ev_q_norm)[0], kn=f(ev_k_norm)[0], sink=f(ev_sink)[0])
    ex1 = dict(rpb=f(od_rpb)[0], fnw=f(final_norm_w))
    out = np.empty_like(x)
    if FUSED:
        in_maps = []
        for k in cores:
            b, qt = k // 4, k % 4
            m0 = layer_inputs(0, k, x, ctx, c, c_ctx, ada_w, ada_b, norm_w, mlp_w1, mlp_w2, f(ev_w_in)[0], f(ev_w_out)[0], ex0)
            m = {"cvec": m0.pop("cvec")}
            for kk in ("ada_w", "ada_b", "normw"):
                m0.pop(kk)
            for kk, v in m0.items():
                m[kk + "_0"] = v
            for L_ in range(2):
                m["ada_wq_%d" % L_] = np.ascontiguousarray(ada_w[L_][:, 3072 * qt:3072 * (qt + 1)])
            m["ada_bq"] = np.ascontiguousarray(
                ada_b.reshape(2, 96, 128)[:, 24 * qt:24 * (qt + 1), :].transpose(2, 0, 1).reshape(128, 48))
            m["normw2"] = np.ascontiguousarray(norm_w.reshape(2, 2, KC, 128).transpose(3, 0, 1, 2))
            m["w_in_1"] = f(od_w_in)[0]
            m["w_out_1"] = f(od_w_out)[0]
            m["w1_1"] = mlp_w1[1]
            m["w2_1"] = mlp_w2[1]
            m["biasT_1"] = _natten_bias(ex1["rpb"], qt)
            m["fnw_1"] = np.ascontiguousarray(ex1["fnw"].reshape(KC, 128).T)
            sel = np.zeros((128, 8), np.float32)
            if qt > 0:
                sel[:, qt - 1] = 1.0
            if qt < 3:
                sel[:, 4 + qt + 1] = 1.0
            m["selv_1"] = sel
            in_maps.append(m)
        r = run_bass_kernel_spmd(_get_nc('fused'), in_maps, core_ids=cores).results
        for k in cores:
            b, qt = k // 4, k % 4
            out[b, 1024 * qt:1024 * (qt + 1)] = r[k]["outT"].T
        return out
    in0 = [layer_inputs(0, k, x, ctx, c, c_ctx, ada_w, ada_b, norm_w, mlp_w1, mlp_w2, f(ev_w_in)[0], f(ev_w_out)[0], ex0)
           for k in cores]
    r0 = run_bass_kernel_spmd(_get_nc(0), in0, core_ids=cores).results
    x1 = np.empty_like(x)
    ctx1 = np.empty_like(ctx)
    for k in cores:
        b, qt = k // 4, k % 4
        x1[b, 1024 * qt:1024 * (qt + 1)] = r0[k]["x1T"].T
        ctx1[b, 64 * qt:64 * (qt + 1)] = r0[k]["c1T"].T
    in1 = [layer_inputs(1, k, x1, ctx1, c, c_ctx, ada_w, ada_b, norm_w, mlp_w1, mlp_w2, f(od_w_in)[0], f(od_w_out)[0], ex1)
           for k in cores]
    r1 = run_bass_kernel_spmd(_get_nc(1), in1, core_ids=cores).results
    for k in cores:
        b, qt = k // 4, k % 4
        out[b, 1024 * qt:1024 * (qt + 1)] = r1[k]["outT"].T
    return out
```

```python
import numpy as np
from contextlib import ExitStack
import concourse.bass as bass
import concourse.mybir as mybir
from concourse.bass_utils import run_bass_kernel_spmd

F32 = mybir.dt.float32
BF16 = mybir.dt.bfloat16
AF = mybir.ActivationFunctionType
ALU = mybir.AluOpType

P = 128
D = 2048
KC = 16
DFF = 8192
SEQ = 4096
CTXL = 256
NOWN = 1024
NEG = -1.0e30
EPS = 1e-6
SCALE = 128 ** -0.5


class Trk:
    def __init__(self, nc):
        self.nc = nc
        self.es = ExitStack()
        self.E = {'pe': nc.tensor, 'act': nc.scalar, 'dve': nc.vector, 'pool': nc.gpsimd, 'sp': nc.sync}
        self.sem = {}
        self.cnt = {}
        for k in self.E:
            self.sem[k] = self.es.enter_context(nc.semaphore('e_' + k))
            self.cnt[k] = 0
        self.seen = {k: {} for k in self.E}
        self.bw = {}
        self.br = {}
        self.dcount = {}

    def _need(self, eng, r, w):
        need = {}

        def add(ev):
            if ev is None:
                return
            s, v = ev
            if need.get(s, 0) < v:
                need[s] = v
        for b in r:
            add(self.bw.get(b))
        for b in w:
            add(self.bw.get(b))
            for s, v in self.br.get(b, {}).items():
                add((s, v))
        e = self.E[eng]
        for s, v in need.items():
            if eng == 'pe' and s == 'pe':
                continue
            if self.seen[eng].get(s, 0) >= v:
                continue
            e.wait_ge(self.sem[s], v)
            self.seen[eng][s] = v

    def _mark(self, ev, r, w):
        s, v = ev
        for b in r:
            d = self.br.setdefault(b, {})
            if d.get(s, 0) < v:
                d[s] = v
        for b in w:
            self.bw[b] = ev
            self.br[b] = {}

    def op(self, eng, fn, r=(), w=()):
        self._need(eng, r, w)
        inst = fn(self.E[eng])
        self.cnt[eng] += 1
        inst.then_inc(self.sem[eng], 1)
        self._mark((eng, self.cnt[eng]), r, w)

    def dma(self, eng, out, in_, dst, src=()):
        self._need(eng, src, (dst,))
        key = 'd_' + dst
        if key not in self.sem:
            self.sem[key] = self.es.enter_context(self.nc.semaphore(key.replace(':', '_')))
            self.dcount[key] = 0
        inst = self.E[eng].dma_start(out=out, in_=in_)
        self.dcount[key] += 16
        inst.then_inc(self.sem[key], 16)
        self._mark((key, self.dcount[key]), src, (dst,))

    def collective(self, fn, dst, src=()):
        self._need('pool', src, (dst,))
        key = 'c_' + dst
        if key not in self.sem:
            self.sem[key] = self.es.enter_context(self.nc.semaphore(key.replace(':', '_')))
            self.dcount[key] = 0
        inst = fn(self.E['pool'])
        self.dcount[key] += 1
        inst.then_inc(self.sem[key])
        self._mark((key, self.dcount[key]), src, (dst,))

    def barrier(self):
        evs = {}
        for k in self.E:
            if self.cnt[k] > 0:
                evs[k] = self.cnt[k]
        for k, v in self.dcount.items():
            if v > 0:
                evs[k] = v
        for eng in self.E:
            for s, v in evs.items():
                if s == eng and eng == 'pe':
                    continue
                if self.seen[eng].get(s, 0) >= v:
                    continue
                self.E[eng].wait_ge(self.sem[s], v)
                self.seen[eng][s] = v


def build(mode):
    nc = bass.Bass("TRN2", target_bir_lowering=False)
    t = Trk(nc)
    fused = (mode == 'fused')
    layers = [0, 1] if fused else [mode]

    keep = []

    def din0(name, shape):
        h = nc.dram_tensor(name, shape, F32, kind="ExternalInput")
        keep.append(h)
        return h.ap()

    cvec = din0("cvec", [D, 2]).rearrange("(c p) j -> p c j", p=P)
    carry = {}
    top = ExitStack()
    ARENA_W = 53000
    arena = top.enter_context(nc.sbuf_tensor("arena", [P, ARENA_W], F32))
    free_list = [(0, ARENA_W)]

    class Scope:
        def __init__(self):
            self.items = []

        def __enter__(self):
            return self

        def __exit__(self, *a):
            self.close()
            return False

        def close(self):
            for (o, n) in self.items:
                free_list.append((o, n))
            self.items = []
            free_list.sort()
            merged = []
            for (o, n) in free_list:
                if merged and merged[-1][0] + merged[-1][1] == o:
                    merged[-1] = (merged[-1][0], merged[-1][1] + n)
                else:
                    merged.append((o, n))
            free_list[:] = merged

    topS = Scope()

    def sb(es, name, shape, dt):
        nel = 1
        for d_ in shape[1:]:
            nel *= d_
        nw = (nel * (2 if dt == BF16 else 4) + 3) // 4
        nw = (nw + 15) // 16 * 16
        for i, (o, n) in enumerate(free_list):
            if n >= nw:
                free_list[i] = (o + nw, n - nw)
                es.items.append((o, nw))
                v = arena[:, o:o + nw]
                if dt == BF16:
                    v = v.bitcast(BF16)
                v = v[:, 0:nel]
                if len(shape) == 3:
                    v = v.rearrange("p (a b) -> p a b", a=shape[1])
                return v
        raise RuntimeError("arena full allocating %s %s; free=%s" % (name, shape, free_list))

    ps = [top.enter_context(nc.psum_tensor("ps%d" % i, [P, 512], F32)) for i in range(8)]
    ones = sb(topS, "ones", [P, P], BF16)
    epst = sb(topS, "epst", [P, 1], F32)
    modsL = {}
    for L_ in layers:
        modsL[L_] = dict(mod=sb(topS, "mod%d" % L_, [P, 96, 2], F32), se1=sb(topS, "se1_%d" % L_, [P, KC, 2], F32),
                         se2=sb(topS, "se2_%d" % L_, [P, KC, 2], F32))
    cur = {}
    t.op('dve', lambda e: e.memset(ones[:], 1.0), w=('ones',))
    t.op('dve', lambda e: e.memset(epst[:], EPS), w=('epst',))

    uid = {}

    def rr(n, key=None):
        key = key or ('k%d' % n)
        uid[key] = uid.get(key, -1) + 1
        return uid[key] % n

    sq = [sb(topS, "sq%d" % i, [P, 512], BF16) for i in range(4)]
    nr = [sb(topS, "nr%d" % i, [P, 512], F32) for i in range(2)]
    tmpf = [sb(topS, "tmpf%d" % i, [P, 512], F32) for i in range(3)]

    def rstd_from(ps_ap, n, inv_n, psbuf):
        i = rr(2, 'nr')
        t.op('act', lambda e: e.activation(out=nr[i][:, :n], in_=ps_ap, func=AF.Ln, bias=epst[:, 0:1], scale=inv_n),
             r=(psbuf, 'epst'), w=('nr%d' % i,))
        t.op('act', lambda e: e.activation(out=nr[i][:, :n], in_=nr[i][:, :n], func=AF.Exp, scale=-0.5),
             r=('nr%d' % i,), w=('nr%d' % i,))
        return nr[i][:, :n], 'nr%d' % i

    def norm_blk(x3, xbuf, n, j, se, shoff, h3, hbuf, psi=6):
        pb = 'ps%d' % psi
        for c in range(KC):
            i = c % 4
            if c % 3 != 2:
                t.op('act', lambda e, c=c, i=i: e.activation(out=sq[i][:, :n], in_=x3[:, c, :], func=AF.Square),
                     r=(xbuf,), w=('sq%d' % i,))
            else:
                t.op('dve', lambda e, c=c, i=i: e.tensor_tensor(out=sq[i][:, :n], in0=x3[:, c, :], in1=x3[:, c, :], op=ALU.mult),
                     r=(xbuf,), w=('sq%d' % i,))
            t.op('pe', lambda e, c=c, i=i: e.matmul(ps[psi][:, :n], lhsT=ones[:], rhs=sq[i][:, :n],
                                                     start=(c == 0), stop=(c == KC - 1)),
                 r=('sq%d' % i, 'ones'), w=(pb,))
        rs, rsb = rstd_from(ps[psi][:, :n], n, 1.0 / D, pb)
        for c in range(KC):
            i = rr(3, 'tmpf')
            t.op('dve', lambda e, c=c, i=i: e.scalar_tensor_tensor(out=tmpf[i][:, :n], in0=x3[:, c, :], scalar=se[:, c, j:j + 1],
                                                                   in1=rs, op0=ALU.mult, op1=ALU.mult),
                 r=(xbuf, rsb, 'se'), w=('tmpf%d' % i,))
            t.op('act', lambda e, c=c, i=i: e.activation(out=h3[:, c, :], in_=tmpf[i][:, :n], func=AF.Identity,
                                                         bias=cur['mod'][:, shoff + c, j:j + 1], scale=1.0),
                 r=('tmpf%d' % i, 'mod'), w=(hbuf,))

    pT = [sb(topS, "pT%d" % i, [P, 512], BF16) for i in range(5)]
    rdn = [sb(topS, "rdn%d" % i, [P, 512], F32) for i in range(2)]
    SB_ = [0, 1, 2, 7]
    ACC = [(3, 4), (5, 6)]
    jobn = [0]

    def attn_job(q_ap, qbuf, N, items, out_ap, outbuf, extra=None):
        assert items[0]['c0'] == 0 and items[0]['n'] == N
        for it in items:
            if 'subs' not in it:
                it['subs'] = [(it['kT'], it['kbuf'], it['v'], it['vbuf'])]
            assert len(it['subs']) * it['n'] <= 512
        ob, db = ACC[jobn[0] % 2]
        jobn[0] += 1
        nit = len(items)
        st = {}

        def emit_s(i):
            it = items[i]
            sbk = SB_[rr(4, 'sbk')]
            pi = rr(5, 'pT')
            n = it['n']
            c0 = it['c0']
            ns = len(it['subs'])
            w_ = ns * n
            for k_, (kT_, kb_, _, _) in enumerate(it['subs']):
                t.op('pe', lambda e: e.matmul(ps[sbk][:, k_ * n:(k_ + 1) * n], lhsT=kT_, rhs=q_ap[:, c0:c0 + n], start=True, stop=True),
                     r=(kb_, qbuf), w=('ps%d' % sbk,))
            if it.get('bias') is not None:
                k = rr(3, 'tmpf')
                t.op('dve', lambda e: e.scalar_tensor_tensor(out=tmpf[k][:, :w_], in0=ps[sbk][:, :w_], scalar=SCALE,
                                                             in1=it['bias'], op0=ALU.mult, op1=ALU.add),
                     r=('ps%d' % sbk, it['bbuf']), w=('tmpf%d' % k,))
                t.op('act', lambda e: e.activation(out=pT[pi][:, :w_], in_=tmpf[k][:, :w_], func=AF.Exp),
                     r=('tmpf%d' % k,), w=('pT%d' % pi,))
            else:
                t.op('act', lambda e: e.activation(out=pT[pi][:, :w_], in_=ps[sbk][:, :w_], func=AF.Exp, scale=SCALE),
                     r=('ps%d' % sbk,), w=('pT%d' % pi,))
            st[i] = pi

        def emit_pv(i):
            it = items[i]
            pi = st[i]
            n = it['n']
            c0 = it['c0']
            ns = len(it['subs'])
            for k_, (_, _, v_, vb_) in enumerate(it['subs']):
                first = (i == 0 and k_ == 0)
                last = (i == nit - 1 and k_ == ns - 1)
                t.op('pe', lambda e: e.matmul(ps[ob][:, c0:c0 + n], lhsT=v_, rhs=pT[pi][:, k_ * n:(k_ + 1) * n],
                                              start=first, stop=last, skip_group_check=True),
                     r=(vb_, 'pT%d' % pi), w=('ps%d' % ob,))
                t.op('pe', lambda e: e.matmul(ps[db][:, c0:c0 + n], lhsT=ones[:], rhs=pT[pi][:, k_ * n:(k_ + 1) * n],
                                              start=first, stop=last, skip_group_check=True),
                     r=('ones', 'pT%d' % pi), w=('ps%d' % db,))

        DEPTH = 3
        for i in range(min(DEPTH, nit)):
            emit_s(i)
        for i in range(nit):
            if i + DEPTH < nit:
                emit_s(i + DEPTH)
            emit_pv(i)
        k = rr(2, 'rdn')
        if extra is not None:
            t.op('act', lambda e: e.activation(out=rdn[k][:, :N], in_=ps[db][:, :N], func=AF.Ln, bias=extra, scale=1.0),
                 r=('ps%d' % db, 'sinke'), w=('rdn%d' % k,))
        else:
            t.op('act', lambda e: e.activation(out=rdn[k][:, :N], in_=ps[db][:, :N], func=AF.Ln),
                 r=('ps%d' % db,), w=('rdn%d' % k,))
        t.op('act', lambda e: e.activation(out=rdn[k][:, :N], in_=rdn[k][:, :N], func=AF.Exp, scale=-1.0),
             r=('rdn%d' % k,), w=('rdn%d' % k,))
        t.op('dve', lambda e: e.tensor_tensor(out=out_ap, in0=ps[ob][:, :N], in1=rdn[k][:, :N], op=ALU.mult),
             r=('ps%d' % ob, 'rdn%d' % k), w=(outbuf,))


    def emit_layer(layer):
        def din(name, shape):
            return din0(name + ("_%d" % layer if fused else ""), shape)
        L0 = (layer == 0)
        TEXT = SEQ if L0 else 1536
        OWN0 = 128 if L0 else 256
        NCQ = 64 if L0 else 0
        NTOK = NOWN + NCQ
        TBS = [(0, 512), (512, 512)] + ([(1024, 64)] if L0 else [])
        CIN = 3072 if L0 else 6144

        if fused and not L0:
            xT = carry['x1own'].ap().rearrange("(c p) t -> p c t", p=P)
            cT = None
            selv = din("selv", [P, 8])
        else:
            xT = din("xT", [D, TEXT]).rearrange("(c p) t -> p c t", p=P)
            cT = din("cT", [D, CTXL]).rearrange("(c p) t -> p c t", p=P)
        mod, se1, se2 = modsL[layer]['mod'], modsL[layer]['se1'], modsL[layer]['se2']
        cur['mod'] = mod
        if not fused:
            ada_w = din("ada_w", [D, 6 * D]).rearrange("(c p) n -> p c n", p=P)
            ada_b = din("ada_b", [P, 96])
            normw = din("normw", [P, 2, KC])
        w_in = din("w_in", [D, CIN]).rearrange("(c p) n -> p c n", p=P)
        w_out = din("w_out", [D, D]).rearrange("(c p) n -> p c n", p=P)
        w1 = din("w1", [D, DFF]).rearrange("(c p) n -> p c n", p=P)
        w2 = din("w2", [DFF, D]).rearrange("(m p) n -> p m n", p=P)
        if L0:
            qkw = din("qkw", [P, 2])
            sinkb = din("sinkb", [P, 8])
            cosT = din("cosT", [P, SEQ])
            sinT = din("sinT", [P, SEQ])
            rotm = din("rotm", [P, P])
            bandb = din("bandb", [P, 3, 384])
            if not fused:
                x1T = nc.dram_tensor("x1T", [D, NOWN], F32, kind="ExternalOutput").ap().rearrange("(c p) t -> p c t", p=P)
                c1T = nc.dram_tensor("c1T", [D, 64], F32, kind="ExternalOutput").ap().rearrange("(c p) t -> p c t", p=P)
            else:
                carry['x1own'] = nc.dram_tensor("x1own", [D, NOWN], F32)
        else:
            biasT = din("biasT", [16, P, 29, P])
            fnw = din("fnw", [P, KC])
            outT = nc.dram_tensor("outT", [D, NOWN], F32, kind="ExternalOutput").ap().rearrange("(c p) t -> p c t", p=P)

        if not fused:
            with Scope() as es:
                cv = sb(es, "cv", [P, KC, 2], F32)
                scv = sb(es, "scv", [P, KC, 2], BF16)
                adab = sb(es, "adab", [P, 96], F32)
                nw = sb(es, "nw", [P, 2, KC], F32)
                wa = [sb(es, "wa%d" % i, [P, KC, 1024], BF16) for i in range(2)]
                t.dma('sp', cv[:], cvec, 'cv')
                t.dma('sp', adab[:], ada_b, 'adab')
                t.dma('sp', nw[:], normw, 'nw')
                t.op('act', lambda e: e.activation(out=scv[:], in_=cv[:], func=AF.Silu), r=('cv',), w=('scv',))
                psm = ps[7]
                for g in range(12):
                    s = g % 2
                    for hlf in range(2):
                        t.dma('pool', wa[s][:, hlf * 8:(hlf + 1) * 8, :],
                              ada_w[:, hlf * 8:(hlf + 1) * 8, g * 1024:(g + 1) * 1024], 'wa%d' % s)
                    for cl in range(8):
                        cc = g * 8 + cl
                        for kc in range(KC):
                            t.op('pe', lambda e, s=s, cl=cl, kc=kc, cc=cc: e.matmul(
                                psm[:, 2 * cc:2 * cc + 2], lhsT=wa[s][:, kc, cl * P:(cl + 1) * P], rhs=scv[:, kc, :],
                                start=(kc == 0), stop=(kc == KC - 1)), r=('wa%d' % s, 'scv'), w=('ps7',))
                psm3 = psm[:, 0:192].rearrange("p (c j) -> p c j", j=2)
                for j in range(2):
                    t.op('dve', lambda e, j=j: e.tensor_tensor(out=mod[:, :, j], in0=psm3[:, :, j], in1=adab[:, :], op=ALU.add),
                         r=('ps7', 'adab'), w=('mod',))
                for j in range(2):
                    t.op('dve', lambda e, j=j: e.scalar_tensor_tensor(out=se1[:, :, j], in0=mod[:, 16:32, j], scalar=1.0,
                                                                       in1=nw[:, 0, :], op0=ALU.add, op1=ALU.mult),
                         r=('mod', 'nw'), w=('se1',))
                    t.op('dve', lambda e, j=j: e.scalar_tensor_tensor(out=se2[:, :, j], in0=mod[:, 64:80, j], scalar=1.0,
                                                                       in1=nw[:, 1, :], op0=ALU.add, op1=ALU.mult),
                         r=('mod', 'nw'), w=('se2',))
                t.barrier()
        SH1, G1, SH2, G2 = 0, 32, 48, 80

        att = Scope()
        ost = Scope()
        oT = None
        if L0:
            NKA = CTXL + SEQ
            NKB = CTXL + 1280
            kTA = sb(att, "kTA", [P, 2, NKA], BF16)
            VA = sb(att, "VA", [P, NKA // P, 256], BF16)
            kTB = sb(att, "kTB", [P, 2, NKB], BF16)
            VB = sb(att, "VB", [P, NKB // P, 256], BF16)
            hkeep = sb(att, "hkeep", [P, KC, NTOK], BF16)
            qkw_t = sb(att, "qkw_t", [P, 2], F32)
            sinke = sb(att, "sinke", [P, 8], F32)
            rot_t = sb(att, "rot_t", [P, P], BF16)
            band_t = sb(att, "band_t", [P, 3, 384], F32)
            qf = [sb(att, "qf%d" % i, [P, 512], F32) for i in range(2)]
            qn = [sb(att, "qn%d" % i, [P, 512], F32) for i in range(2)]
            qb = [sb(att, "qb%d" % i, [P, 512], BF16) for i in range(2)]
            t.dma('sp', qkw_t[:], qkw, 'qkw_t')
            t.dma('sp', sinke[:], sinkb, 'sinke')
            t.dma('pool', rot_t[:], rotm, 'rot_t')
            t.dma('sp', band_t[:], bandb, 'band_t')
            t.op('act', lambda e: e.activation(out=sinke[:], in_=sinke[:], func=AF.Exp), r=('sinke',), w=('sinke',))

            def qk_post(ps_ap, psbuf, n, nwcol, cs_ap, sn_ap, csbuf, out_ap, outbuf, pss=7):
                i = rr(2, 'qf')
                t.op('act', lambda e: e.activation(out=qf[i][:, :n], in_=ps_ap, func=AF.Copy), r=(psbuf,), w=('qf%d' % i,))
                src, srcb = qf[i], 'qf%d' % i
                if nwcol is not None:
                    s2 = rr(2, 'sq')
                    t.op('pool', lambda e: e.tensor_tensor(out=sq[s2][:, :n], in0=qf[i][:, :n], in1=qf[i][:, :n], op=ALU.mult),
                         r=('qf%d' % i,), w=('sq%d' % s2,))
                    t.op('pe', lambda e: e.matmul(ps[pss][:, :n], lhsT=ones[:], rhs=sq[s2][:, :n], start=True, stop=True),
                         r=('sq%d' % s2, 'ones'), w=('ps%d' % pss,))
                    rs, rsb = rstd_from(ps[pss][:, :n], n, 1.0 / P, 'ps%d' % pss)
                    t.op('dve', lambda e: e.scalar_tensor_tensor(out=qn[i][:, :n], in0=qf[i][:, :n], scalar=qkw_t[:, nwcol:nwcol + 1],
                                                                 in1=rs, op0=ALU.mult, op1=ALU.mult),
                         r=('qf%d' % i, rsb, 'qkw_t'), w=('qn%d' % i,))
                    src, srcb = qn[i], 'qn%d' % i
                if cs_ap is None:
                    t.op('pool', lambda e: e.tensor_copy(out=out_ap, in_=src[:, :n]), r=(srcb,), w=(outbuf,))
                    return
                t.op('pool', lambda e: e.tensor_copy(out=qb[i][:, :n], in_=src[:, :n]), r=(srcb,), w=('qb%d' % i,))
                t.op('pe', lambda e: e.matmul(ps[pss][:, :n], lhsT=rot_t[:], rhs=qb[i][:, :n], start=True, stop=True),
                     r=('qb%d' % i, 'rot_t'), w=('ps%d' % pss,))
                a = rr(3, 'tmpf')
                t.op('dve', lambda e: e.tensor_tensor(out=tmpf[a][:, :n], in0=ps[pss][:, :n], in1=sn_ap, op=ALU.mult),
                     r=('ps%d' % pss, csbuf), w=('tmpf%d' % a,))
                t.op('pool', lambda e: e.tensor_tensor(out=src[:, :n], in0=src[:, :n], in1=cs_ap, op=ALU.mult),
                     r=(srcb, csbuf), w=(srcb,))
                t.op('pool', lambda e: e.tensor_tensor(out=out_ap, in0=src[:, :n], in1=tmpf[a][:, :n], op=ALU.add),
                     r=(srcb, 'tmpf%d' % a), w=(outbuf,))

            with Scope() as es:
                NB = 256
                xb = [sb(es, "xb%d" % i, [P, KC, NB], F32) for i in range(2)]
                hb = [sb(es, "hb%d" % i, [P, KC, NB], BF16) for i in range(2)]
                csb = [sb(es, "csb%d" % i, [P, 2, NB], F32) for i in range(2)]
                wkv = sb(es, "wkv", [P, KC, 1024], BF16)
                for i, c0 in enumerate((1024, 1280, 2560, 2816)):
                    t.dma('pool', wkv[:, :, i * 256:(i + 1) * 256], w_in[:, :, c0:c0 + 256], 'wkv')
                blocks0a = [0, 1, 2, 3, 4, 16] if fused else list(range(17))
                slot0a = {b_: i_ % 2 for i_, b_ in enumerate(blocks0a)}

                def stage_norm(bi):
                    isc = (bi == 16)
                    j = 1 if isc else 0
                    s = slot0a[bi]
                    if isc:
                        t.dma('sp', xb[s][:], cT[:, :, 0:NB], 'xb%d' % s)
                    else:
                        t.dma('sp', xb[s][:], xT[:, :, bi * NB:(bi + 1) * NB], 'xb%d' % s)
                        t.dma('sp', csb[s][:, 0, :], cosT[:, bi * NB:(bi + 1) * NB], 'csb%d' % s)
                        t.dma('sp', csb[s][:, 1, :], sinT[:, bi * NB:(bi + 1) * NB], 'csb%d' % s)
                    norm_blk(xb[s], 'xb%d' % s, NB, j, se1, SH1, hb[s], 'hb%d' % s, psi=4)

                def run_interleaved(gens):
                    gens = list(gens)
                    while gens:
                        nxt = []
                        for g_ in gens:
                            try:
                                next(g_)
                                nxt.append(g_)
                            except StopIteration:
                                pass
                        gens = nxt

                def k_chain(ci, s, cb, kv, norm, rope, out_ap, outbuf):
                    kb, kh = ci // 2, ci % 2
                    psK = ps[kb][:, kh * NB:(kh + 1) * NB]
                    psKn = 'ps%d' % kb
                    psP = ps[5 + kb][:, kh * NB:(kh + 1) * NB]
                    psPn = 'ps%d' % (5 + kb)
                    qf_ = qf[kb][:, kh * NB:(kh + 1) * NB]
                    qn_ = qn[kb][:, kh * NB:(kh + 1) * NB]
                    qb_ = qb[kb][:, kh * NB:(kh + 1) * NB]
                    qfn, qnn, qbn = 'qf_%d' % ci, 'qn_%d' % ci, 'qb_%d' % ci
                    for kc in range(KC):
                        t.op('pe', lambda e: e.matmul(psK, lhsT=wkv[:, kc, cb + kv * P:cb + (kv + 1) * P], rhs=hb[s][:, kc, :],
                                                      start=(kc == 0), stop=(kc == KC - 1)), r=('wkv', 'hb%d' % s), w=(psKn,))
                    yield
                    t.op('act', lambda e: e.activation(out=qf_, in_=psK, func=AF.Copy), r=(psKn,), w=(qfn,))
                    yield
                    src, srcn, oth, othn = qf_, qfn, qn_, qnn
                    if norm:
                        t.op('dve', lambda e: e.tensor_tensor(out=qb_, in0=qf_, in1=qf_, op=ALU.mult), r=(qfn,), w=(qbn,))
                        yield
                        t.op('pe', lambda e: e.matmul(psP, lhsT=ones[:], rhs=qb_, start=True, stop=True), r=(qbn, 'ones'), w=(psPn,))
                        yield
                        t.op('act', lambda e: e.activation(out=qn_, in_=psP, func=AF.Ln, bias=epst[:, 0:1], scale=1.0 / P),
                             r=(psPn, 'epst'), w=(qnn,))
                        yield
                        t.op('act', lambda e: e.activation(out=qn_, in_=qn_, func=AF.Exp, scale=-0.5), r=(qnn,), w=(qnn,))
                        yield
                        t.op('dve', lambda e: e.scalar_tensor_tensor(out=qn_, in0=qf_, scalar=qkw_t[:, 1:2], in1=qn_,
                                                                     op0=ALU.mult, op1=ALU.mult), r=(qfn, qnn, 'qkw_t'), w=(qnn,))
                        yield
                        src, srcn, oth, othn = qn_, qnn, qf_, qfn
                    if not rope:
                        t.op('pool', lambda e: e.tensor_copy(out=out_ap, in_=src), r=(srcn,), w=(outbuf,))
                        return
                    t.op('act', lambda e: e.activation(out=qb_, in_=src, func=AF.Copy), r=(srcn,), w=(qbn,))
                    yield
                    t.op('pe', lambda e: e.matmul(psP, lhsT=rot_t[:], rhs=qb_, start=True, stop=True), r=(qbn, 'rot_t'), w=(psPn,))
                    yield
                    t.op('dve', lambda e: e.tensor_tensor(out=oth, in0=psP, in1=csb[s][:, 1, :], op=ALU.mult),
                         r=(psPn, 'csb%d' % s), w=(othn,))
                    yield
                    t.op('pool', lambda e: e.tensor_tensor(out=src, in0=src, in1=csb[s][:, 0, :], op=ALU.mult),
                         r=(srcn, 'csb%d' % s), w=(srcn,))
                    yield
                    t.op('pool', lambda e: e.tensor_tensor(out=out_ap, in0=src, in1=oth, op=ALU.add), r=(srcn, othn), w=(outbuf,))

                def v_chain(vi, s, cb, tt, out_ap, outbuf):
                    vb_, vh = 2 + vi // 2, vi % 2
                    psV = ps[vb_][:, vh * 256:(vh + 1) * 256]
                    psVn = 'ps%d' % vb_
                    for kc in range(KC):
                        t.op('pe', lambda e: e.matmul(psV, lhsT=hb[s][:, kc, tt * P:(tt + 1) * P], rhs=wkv[:, kc, cb + 256:cb + 512],
                                                      start=(kc == 0), stop=(kc == KC - 1)), r=('wkv', 'hb%d' % s), w=(psVn,))
                    yield
                    t.op('act', lambda e: e.activation(out=out_ap, in_=psV, func=AF.Copy), r=(psVn,), w=(outbuf,))

                def stage_kv(bi):
                    isc = (bi == 16)
                    s = slot0a[bi]
                    pos = 0 if isc else CTXL + bi * NB
                    doB = isc or bi < 5
                    gens = []
                    ci = 0
                    vi = 0
                    for kind in range(2 if doB else 1):
                        kT_t, kname, V_t, vname = ((kTA, 'kTA', VA, 'VA'), (kTB, 'kTB', VB, 'VB'))[kind]
                        cb = kind * 512
                        for kv in range(2):
                            gens.append(k_chain(ci, s, cb, kv, kind == 0, not isc, kT_t[:, kv, pos:pos + NB], kname))
                            ci += 1
                        for tt in range(NB // P):
                            gens.append(v_chain(vi, s, cb, tt, V_t[:, pos // P + tt, :], vname))
                            vi += 1
                    run_interleaved(gens)
                    if isc:
                        t.op('pool', lambda e: e.tensor_copy(out=hkeep[:, :, NOWN:NOWN + 64], in_=hb[s][:, :, 0:64]),
                             r=('hb%d' % s,), w=('hkeep',))
                    else:
                        lo = max(bi * NB, OWN0)
                        hi = min((bi + 1) * NB, OWN0 + NOWN)
                        if hi > lo:
                            t.op('pool', lambda e: e.tensor_copy(out=hkeep[:, :, lo - OWN0:hi - OWN0],
                                                                 in_=hb[s][:, :, lo - bi * NB:hi - bi * NB]),
                                 r=('hb%d' % s,), w=('hkeep',))

                stage_norm(blocks0a[0])
                for i_, bi in enumerate(blocks0a):
                    if i_ + 1 < len(blocks0a):
                        stage_norm(blocks0a[i_ + 1])
                    stage_kv(bi)
                t.barrier()

            oT = sb(ost, "oT", [P, KC, NTOK], BF16)
            with Scope() as es:
                wq = [sb(es, "wq%d" % i, [P, KC, 512], BF16) for i in range(2)]
                qg0 = sb(es, "qg0", [P, 4, NTOK], BF16)
                qg = [qg0, qg0]
                cso = sb(es, "cso", [P, 2, NOWN], F32)
                t.dma('sp', cso[:, 0, :], cosT[:, OWN0:OWN0 + NOWN], 'cso')
                t.dma('sp', cso[:, 1, :], sinT[:, OWN0:OWN0 + NOWN], 'cso')

                seq = [2, 3, 0, 1] if fused else [0, 1, 2, 3]
                gslot = {g_: i_ % 2 for i_, g_ in enumerate(seq)}

                def load_wq(g):
                    c0w = (g * 512) if g < 2 else (1536 + (g - 2) * 512)
                    for hlf in range(2):
                        t.dma('pool', wq[gslot[g]][:, hlf * 8:(hlf + 1) * 8, :], w_in[:, hlf * 8:(hlf + 1) * 8, c0w:c0w + 512], 'wq%d' % gslot[g])

                def exchange_kv():
                    ks_d = nc.dram_tensor("ksend", [256, NOWN], BF16)
                    kg_d = nc.dram_tensor("kgath", [4 * 256, NOWN], BF16)
                    vs_d = nc.dram_tensor("vsend", [NOWN, 256], BF16)
                    vg_d = nc.dram_tensor("vgath", [4 * NOWN, 256], BF16)
                    for kv_ in range(2):
                        t.dma('sp', ks_d.ap()[kv_ * P:(kv_ + 1) * P, :], kTA[:, kv_, CTXL + OWN0:CTXL + OWN0 + NOWN], 'dram:ksend', src=('kTA',))
                    t.dma('sp', vs_d.ap().rearrange("(t p) c -> p t c", p=P), VA[:, 3:11, :], 'dram:vsend', src=('VA',))
                    for nm, s_d, g_d in (('k', ks_d, kg_d), ('v', vs_d, vg_d)):
                        t.collective(lambda e, s_d=s_d, g_d=g_d: e.collective_compute(
                            "AllGather", ALU.bypass, replica_groups=[[0, 1, 2, 3], [4, 5, 6, 7]],
                            ins=[s_d.ap().opt()], outs=[g_d.ap().opt()]), 'dram:%sgath' % nm, src=('dram:%ssend' % nm,))
                    for r_ in range(4):
                        for kv_ in range(2):
                            t.dma('sp', kTA[:, kv_, CTXL + r_ * NOWN:CTXL + (r_ + 1) * NOWN],
                                  kg_d.ap()[r_ * 256 + kv_ * P:r_ * 256 + (kv_ + 1) * P, :], 'kTA', src=('dram:kgath',))
                    for r_ in range(4):
                        t.dma('sp', VA[:, 2 + r_ * 8:2 + (r_ + 1) * 8, :],
                              vg_d.ap()[r_ * NOWN:(r_ + 1) * NOWN, :].rearrange("(t p) c -> p t c", p=P), 'VA', src=('dram:vgath',))
                qf2 = sb(es, "qf2", [P, 64], F32)
                qn2 = sb(es, "qn2", [P, 64], F32)
                qb2 = sb(es, "qb2", [P, 64], BF16)

                def q_chain(ci, g, hh, tb0, n):
                    isA = g < 2
                    s = gslot[g]
                    lat = tb0 < NOWN
                    pp = (7, 3, 5)[ci]
                    psQ, psQn = ps[ci][:, :n], 'ps%d' % ci
                    psP, psPn = ps[pp][:, :n], 'ps%d' % pp
                    qf_, qn_, qb_ = [(qf[0], qn[0], qb[0]), (qf[1], qn[1], qb[1]), (qf2, qn2, qb2)][ci]
                    qf_, qn_, qb_ = qf_[:, :n], qn_[:, :n], qb_[:, :n]
                    qfn, qnn, qbn = 'qcf%d' % ci, 'qcn%d' % ci, 'qcb%d' % ci
                    out_ap, outbuf = qg0[:, hh, tb0:tb0 + n], 'qg_h%d' % hh
                    for kc in range(KC):
                        t.op('pe', lambda e: e.matmul(psQ, lhsT=wq[s][:, kc, hh * P:(hh + 1) * P], rhs=hkeep[:, kc, tb0:tb0 + n],
                                                      start=(kc == 0), stop=(kc == KC - 1)), r=('wq%d' % s, 'hkeep'), w=(psQn,))
                    yield
                    t.op('act', lambda e: e.activation(out=qf_, in_=psQ, func=AF.Copy), r=(psQn,), w=(qfn,))
                    yield
                    src, srcn, oth, othn = qf_, qfn, qn_, qnn
                    if isA:
                        t.op('dve', lambda e: e.tensor_tensor(out=qb_, in0=qf_, in1=qf_, op=ALU.mult), r=(qfn,), w=(qbn,))
                        yield
                        t.op('pe', lambda e: e.matmul(psP, lhsT=ones[:], rhs=qb_, start=True, stop=True), r=(qbn, 'ones'), w=(psPn,))
                        yield
                        t.op('act', lambda e: e.activation(out=qn_, in_=psP, func=AF.Ln, bias=epst[:, 0:1], scale=1.0 / P),
                             r=(psPn, 'epst'), w=(qnn,))
                        yield
                        t.op('act', lambda e: e.activation(out=qn_, in_=qn_, func=AF.Exp, scale=-0.5), r=(qnn,), w=(qnn,))
                        yield
                        t.op('dve', lambda e: e.scalar_tensor_tensor(out=qn_, in0=qf_, scalar=qkw_t[:, 0:1], in1=qn_,
                                                                     op0=ALU.mult, op1=ALU.mult), r=(qfn, qnn, 'qkw_t'), w=(qnn,))
                        yield
                        src, srcn, oth, othn = qn_, qnn, qf_, qfn
                    if not lat:
                        t.op('pool', lambda e: e.tensor_copy(out=out_ap, in_=src), r=(srcn,), w=(outbuf,))
                        return
                    t.op('act', lambda e: e.activation(out=qb_, in_=src, func=AF.Copy), r=(srcn,), w=(qbn,))
                    yield
                    t.op('pe', lambda e: e.matmul(psP, lhsT=rot_t[:], rhs=qb_, start=True, stop=True), r=(qbn, 'rot_t'), w=(psPn,))
                    yield
                    t.op('dve', lambda e: e.tensor_tensor(out=oth, in0=psP, in1=cso[:, 1, tb0:tb0 + n], op=ALU.mult),
                         r=(psPn, 'cso'), w=(othn,))
                    yield
                    t.op('pool', lambda e: e.tensor_tensor(out=src, in0=src, in1=cso[:, 0, tb0:tb0 + n], op=ALU.mult),
                         r=(srcn, 'cso'), w=(srcn,))
                    yield
                    t.op('pool', lambda e: e.tensor_tensor(out=out_ap, in0=src, in1=oth, op=ALU.add), r=(srcn, othn), w=(outbuf,))

                def qproj(g, hh):
                    run_interleaved([q_chain(ci, g, hh, tb0, n) for ci, (tb0, n) in enumerate(TBS)])

                def jobs(g, hh):
                    isA = g < 2
                    kv = g % 2
                    hidx = g * 4 + hh
                    for bi_, (tb0, n) in enumerate(TBS):
                        lat = tb0 < NOWN
                        items = []
                        kT_t, kname, V_t, vname = (kTA, 'kTA', VA, 'VA') if isA else (kTB, 'kTB', VB, 'VB')
                        for ct in range(2):
                            items.append(dict(kT=kT_t[:, kv, ct * P:(ct + 1) * P], kbuf=kname,
                                              v=V_t[:, ct, kv * P:(kv + 1) * P], vbuf=vname, c0=0, n=n))
                        if lat and isA:
                            for kt in range(2, NKA // P):
                                items.append(dict(kT=kT_t[:, kv, kt * P:(kt + 1) * P], kbuf=kname,
                                                  v=V_t[:, kt, kv * P:(kv + 1) * P], vbuf=vname, c0=0, n=n))
                        elif lat:
                            m = bi_
                            for jj in range(4 * m, 4 * m + 6):
                                lo = max(jj - 2, 4 * m)
                                hi = min(jj, 4 * m + 3)
                                sel = 1 if jj == 0 else (2 if jj == 9 else 0)
                                bc0 = (lo - (jj - 2)) * P
                                nn = (hi - lo + 1) * P
                                items.append(dict(kT=kT_t[:, kv, CTXL + jj * P:CTXL + (jj + 1) * P], kbuf=kname,
                                                  v=V_t[:, 2 + jj, kv * P:(kv + 1) * P], vbuf=vname,
                                                  c0=(lo - 4 * m) * P, n=nn, bias=band_t[:, sel, bc0:bc0 + nn], bbuf='band_t'))
                        extra = None if isA else sinke[:, (g - 2) * 4 + hh:(g - 2) * 4 + hh + 1]
                        attn_job(qg0[:, hh, tb0:tb0 + n], 'qg_h%d' % hh, n, items, oT[:, hidx, tb0:tb0 + n], 'oT', extra=extra)

                load_wq(seq[0])
                load_wq(seq[1])
                if fused:
                    exchange_kv()
                for hh in range(2):
                    qproj(seq[0], hh)
                for i_, g in enumerate(seq):
                    if i_ >= 1 and i_ + 1 < 4:
                        load_wq(seq[i_ + 1])
                    for hh in range(4):
                        jobs(g, hh)
                        if i_ == 0 and hh + 2 < 4:
                            qproj(g, hh + 2)
                        if i_ + 1 < 4:
                            qproj(seq[i_ + 1], hh)
                t.barrier()
        else:
            NEXT = TEXT + CTXL
            hext = sb(att, "hext", [P, KC, NEXT], BF16)
            if fused:
                with Scope() as es:
                    NB = 256
                    xo_prev = carry['xo']
                    hs_d = [nc.dram_tensor("hsend%d" % i, [D, 256 if i < 2 else 64], BF16) for i in range(3)]
                    hg_d = [nc.dram_tensor("hgath%d" % i, [4 * D, 256 if i < 2 else 64], BF16) for i in range(3)]
                    hc = sb(es, "hc", [P, KC, 64], BF16)
                    gb = [sb(es, "gb%d" % i, [P, KC, NB], BF16) for i in range(2)]
                    sel_t = sb(es, "sel_t", [P, 8], F32)
                    t.dma('sp', sel_t[:], selv, 'sel_t')

                    def own_blk(bi):
                        norm_blk(xo_prev[:, :, (bi - 1) * NB:bi * NB], 'xo', NB, 0, se1, SH1,
                                 hext[:, :, bi * NB:(bi + 1) * NB], 'hext', psi=6)
                    own_blk(1)
                    own_blk(4)
                    norm_blk(xo_prev[:, :, NOWN:NOWN + 64], 'xo', 64, 1, se1, SH1, hc, 'hc', psi=6)
                    srcs = (hext[:, :, 256:512], hext[:, :, 1024:1280], hc[:])
                    for i_ in range(3):
                        t.dma('sp', hs_d[i_].ap().rearrange("(c p) t -> p c t", p=P), srcs[i_], 'dram:hsend%d' % i_,
                              src=('hext' if i_ < 2 else 'hc',))
                    for i_ in range(3):
                        t.collective(lambda e, i_=i_: e.collective_compute(
                            "AllGather", ALU.bypass, replica_groups=[[0, 1, 2, 3], [4, 5, 6, 7]],
                            ins=[hs_d[i_].ap().opt()], outs=[hg_d[i_].ap().opt()]),
                            'dram:hgath%d' % i_, src=('dram:hsend%d' % i_,))
                    own_blk(2)
                    own_blk(3)
                    for (gi_, so, c0_) in ((1, 0, 0), (0, 4, 1280)):
                        dst = hext[:, :, c0_:c0_ + 256]
                        for r_ in range(4):
                            g_ = r_ % 2
                            t.dma('sp', gb[g_][:], hg_d[gi_].ap()[r_ * D:(r_ + 1) * D, :].rearrange("(c p) t -> p c t", p=P),
                                  'gb%d' % g_, src=('dram:hgath%d' % gi_,))
                            if r_ == 0:
                                t.op('dve', lambda e: e.tensor_scalar(out=dst, in0=gb[g_][:], scalar1=sel_t[:, so + r_:so + r_ + 1],
                                                                      scalar2=None, op0=ALU.mult),
                                     r=('gb%d' % g_, 'sel_t'), w=('hext',))
                            else:
                                t.op('dve', lambda e: e.scalar_tensor_tensor(out=dst, in0=gb[g_][:], scalar=sel_t[:, so + r_:so + r_ + 1],
                                                                             in1=dst, op0=ALU.mult, op1=ALU.add),
                                     r=('gb%d' % g_, 'sel_t', 'hext'), w=('hext',))
                    for r_ in range(4):
                        t.dma('sp', hext[:, :, TEXT + r_ * 64:TEXT + (r_ + 1) * 64],
                              hg_d[2].ap()[r_ * D:(r_ + 1) * D, :].rearrange("(c p) t -> p c t", p=P), 'hext', src=('dram:hgath2',))
                    t.barrier()
                carry['post'].close()
            else:
                with Scope() as es:
                    NB = 256
                    xb = [sb(es, "xb%d" % i, [P, KC, NB], F32) for i in range(2)]
                    for bi in range(7):
                        isc = (bi == 6)
                        s = bi % 2
                        if isc:
                            t.dma('sp', xb[s][:], cT[:, :, 0:NB], 'xb%d' % s)
                        else:
                            t.dma('sp', xb[s][:], xT[:, :, bi * NB:(bi + 1) * NB], 'xb%d' % s)
                        norm_blk(xb[s], 'xb%d' % s, NB, 1 if isc else 0, se1, SH1, hext[:, :, bi * NB:(bi + 1) * NB], 'hext', psi=6)
                    t.barrier()
            oT = sb(ost, "oT", [P, KC, NTOK], BF16)
            with Scope() as es:
                wg = [sb(es, "wg%d" % i, [P, KC, 768], BF16) for i in range(2)]
                qg0 = sb(es, "qg0", [P, 2, NOWN], BF16)
                kg0 = sb(es, "kg0", [P, 2, NEXT], BF16)
                vg0 = sb(es, "vg0", [P, NEXT // P, 256], BF16)
                qg, kg, vg = [qg0, qg0], [kg0, kg0], [vg0, vg0]
                bt = [sb(es, "bt%d" % i, [P, 17, P], F32) for i in range(2)]
                bjobs = [(h_, m_) for h_ in range(16) for m_ in range(2)]

                def load_bt(ji):
                    h_, m_ = bjobs[ji]
                    t.dma('sp', bt[ji % 2][:], biasT[h_, :, 12 * m_:12 * m_ + 17, :], 'bt%d' % (ji % 2))

                def load_wg(g):
                    for i3 in range(3):
                        t.dma('pool', wg[g % 2][:, :, i3 * 256:(i3 + 1) * 256],
                              w_in[:, :, i3 * 2048 + g * 256:i3 * 2048 + (g + 1) * 256], 'wg%d' % (g % 2))
                load_wg(0)
                load_bt(0)

                def slot_of(p, tt):
                    if p == 0:
                        return tt
                    if p == 1:
                        return 6 + tt
                    if p == 6:
                        return 17 + (tt + 1)
                    if p == 7:
                        return 23 + (tt + 1)
                    return 12 + tt

                def tset(p):
                    if p in (0, 1):
                        return range(0, 6)
                    if p in (6, 7):
                        return range(-1, 5)
                    return range(0, 5)

                for g in range(8):
                    s = g % 2
                    if g + 1 < 8:
                        load_wg(g + 1)
                    for hh in range(2):
                        for (tb0, n) in TBS:
                            pk = rr(3, 'pkA')
                            for kc in range(KC):
                                t.op('pe', lambda e, kc=kc, hh=hh, pk=pk, tb0=tb0, n=n: e.matmul(
                                    ps[pk][:, :n], lhsT=wg[s][:, kc, hh * P:(hh + 1) * P], rhs=hext[:, kc, OWN0 + tb0:OWN0 + tb0 + n],
                                    start=(kc == 0), stop=(kc == KC - 1)), r=('wg%d' % s, 'hext'), w=('ps%d' % pk,))
                            t.op('act', lambda e, hh=hh, pk=pk, tb0=tb0, n=n: e.activation(out=qg[s][:, hh, tb0:tb0 + n], in_=ps[pk][:, :n], func=AF.Copy),
                                 r=('ps%d' % pk,), w=('qg',))
                        for (k0, n) in ((0, 512), (512, 512), (1024, 512), (1536, 256)):
                            pk = rr(3, 'pkA')
                            for kc in range(KC):
                                t.op('pe', lambda e, kc=kc, hh=hh, pk=pk, k0=k0, n=n: e.matmul(
                                    ps[pk][:, :n], lhsT=wg[s][:, kc, 256 + hh * P:256 + (hh + 1) * P], rhs=hext[:, kc, k0:k0 + n],
                                    start=(kc == 0), stop=(kc == KC - 1)), r=('wg%d' % s, 'hext'), w=('ps%d' % pk,))
                            t.op('dve', lambda e, hh=hh, pk=pk, k0=k0, n=n: e.tensor_copy(out=kg[s][:, hh, k0:k0 + n], in_=ps[pk][:, :n]),
                                 r=('ps%d' % pk,), w=('kg',))
                    for tt in range(NEXT // P):
                        pk = 3 + rr(3, 'pkB')
                        for kc in range(KC):
                            t.op('pe', lambda e, kc=kc, tt=tt, pk=pk: e.matmul(
                                ps[pk][:, :256], lhsT=hext[:, kc, tt * P:(tt + 1) * P], rhs=wg[s][:, kc, 512:768],
                                start=(kc == 0), stop=(kc == KC - 1)), r=('wg%d' % s, 'hext'), w=('ps%d' % pk,))
                        t.op('act', lambda e, tt=tt, pk=pk: e.activation(out=vg[s][:, tt, :], in_=ps[pk][:, :256], func=AF.Copy),
                             r=('ps%d' % pk,), w=('vg',))
                    for hh in range(2):
                        hidx = g * 2 + hh
                        for m, (tb0, n) in enumerate(TBS):
                            ji = hidx * 2 + m
                            b_ = ji % 2
                            if ji + 1 < len(bjobs):
                                load_bt(ji + 1)
                            items = []
                            for ct in range(2):
                                kt = TEXT // P + ct
                                items.append(dict(kT=kg[s][:, hh, kt * P:(kt + 1) * P], kbuf='kg',
                                                  v=vg[s][:, kt, hh * P:(hh + 1) * P], vbuf='vg', c0=0, n=n))
                            for p_ in range(4 * m, 4 * m + 4):
                                tl = list(tset(p_))
                                for grp in (tl[0:3], tl[3:]):
                                    subs = []
                                    for tt in grp:
                                        jj = p_ + tt
                                        subs.append((kg[s][:, hh, jj * P:(jj + 1) * P], 'kg', vg[s][:, jj, hh * P:(hh + 1) * P], 'vg'))
                                    sl0 = slot_of(p_, grp[0]) - 12 * m
                                    items.append(dict(subs=subs, c0=(p_ - 4 * m) * P, n=P,
                                                      bias=bt[b_][:, sl0:sl0 + len(grp), :].rearrange("p a b -> p (a b)"), bbuf='bt%d' % b_))
                            attn_job(qg[s][:, hh, tb0:tb0 + n], 'qg', n, items, oT[:, hidx, tb0:tb0 + n], 'oT')
                t.barrier()

        att.close()
        post = Scope()
        xo = sb(post, "xo", [P, KC, NTOK], F32)
        xoff = 0 if (fused and not L0) else OWN0
        xsrc = ('dram:x1own',) if (fused and not L0) else ()
        t.dma('sp', xo[:, 0:8, 0:NOWN], xT[:, 0:8, xoff:xoff + NOWN], 'xo', src=xsrc)
        t.dma('sp', xo[:, 8:16, 0:NOWN], xT[:, 8:16, xoff:xoff + NOWN], 'xo', src=xsrc)
        if L0:
            t.dma('sp', xo[:, :, NOWN:NTOK], cT[:, :, 0:64], 'xo')
        with Scope() as es:
            wo = [sb(es, "wo%d" % i, [P, KC, 512], BF16) for i in range(2)]

            def load_wo(dg):
                for hlf in range(2):
                    t.dma('pool', wo[dg % 2][:, hlf * 8:(hlf + 1) * 8, :], w_out[:, hlf * 8:(hlf + 1) * 8, dg * 512:(dg + 1) * 512], 'wo%d' % (dg % 2))
            load_wo(0)
            for dg in range(4):
                s = dg % 2
                if dg + 1 < 4:
                    load_wo(dg + 1)
                for dl in range(4):
                    d = dg * 4 + dl
                    for (tb0, n) in TBS:
                        j = 0 if tb0 < NOWN else 1
                        pk = rr(6, 'pk6')
                        for kc in range(KC):
                            t.op('pe', lambda e, kc=kc, dl=dl, pk=pk, tb0=tb0, n=n: e.matmul(
                                ps[pk][:, :n], lhsT=wo[s][:, kc, dl * P:(dl + 1) * P], rhs=oT[:, kc, tb0:tb0 + n],
                                start=(kc == 0), stop=(kc == KC - 1)), r=('wo%d' % s, 'oT'), w=('ps%d' % pk,))
                        t.op('dve', lambda e, d=d, pk=pk, tb0=tb0, n=n, j=j: e.scalar_tensor_tensor(
                            out=xo[:, d, tb0:tb0 + n], in0=ps[pk][:, :n], scalar=mod[:, G1 + d, j:j + 1], in1=xo[:, d, tb0:tb0 + n],
                            op0=ALU.mult, op1=ALU.add), r=('ps%d' % pk, 'xo', 'mod'), w=('xo',))
            t.barrier()
        ost.close()

        with Scope() as es:
            h2 = sb(es, "h2", [P, KC, NTOK], BF16)
            for (tb0, n) in TBS:
                j = 0 if tb0 < NOWN else 1
                norm_blk(xo[:, :, tb0:tb0 + n], 'xo', n, j, se2, SH2, h2[:, :, tb0:tb0 + n], 'h2', psi=6)
            wA = [sb(es, "wA%d" % i, [P, KC, 512], BF16) for i in range(2)]
            wB = [sb(es, "wB%d" % i, [P, 4, D], BF16) for i in range(2)]
            uT0 = sb(es, "uT0", [P, 4, NTOK], BF16)
            if L0:
                uT, uTn = [uT0, uT0], ['uT0', 'uT0']
            else:
                uT, uTn = [uT0, sb(es, "uT1", [P, 4, NTOK], BF16)], ['uT0', 'uT1']
            rl = [sb(es, "rl%d" % i, [P, 512], BF16) for i in range(2)]
            NBLK = DFF // 512

            def load_mlp(blk):
                s_ = blk % 2
                for hlf in range(2):
                    t.dma('pool', wA[s_][:, hlf * 8:(hlf + 1) * 8, :], w1[:, hlf * 8:(hlf + 1) * 8, blk * 512:(blk + 1) * 512], 'wA%d' % s_)
                for hlf in range(2):
                    t.dma('pool', wB[s_][:, hlf * 2:(hlf + 1) * 2, :], w2[:, blk * 4 + hlf * 2:blk * 4 + (hlf + 1) * 2, :], 'wB%d' % s_)
            load_mlp(0)
            for blk in range(NBLK):
                s = blk % 2
                if blk + 1 < NBLK:
                    load_mlp(blk + 1)
                for m in range(4):
                    for (tb0, n) in TBS:
                        pk = rr(4, 'pk4a')
                        for kc in range(KC):
                            t.op('pe', lambda e, kc=kc, m=m, pk=pk, tb0=tb0, n=n: e.matmul(
                                ps[pk][:, :n], lhsT=wA[s][:, kc, m * P:(m + 1) * P], rhs=h2[:, kc, tb0:tb0 + n],
                                start=(kc == 0), stop=(kc == KC - 1)), r=('wA%d' % s, 'h2'), w=('ps%d' % pk,))
                        ri = rr(2, 'rl')
                        t.op('act', lambda e, pk=pk, n=n, ri=ri: e.activation(out=rl[ri][:, :n], in_=ps[pk][:, :n], func=AF.Relu),
                             r=('ps%d' % pk,), w=('rl%d' % ri,))
                        t.op('pool', lambda e, m=m, tb0=tb0, n=n, ri=ri: e.tensor_tensor(out=uT[s][:, m, tb0:tb0 + n], in0=rl[ri][:, :n],
                                                                                         in1=rl[ri][:, :n], op=ALU.mult),
                             r=('rl%d' % ri,), w=(uTn[s],))
                for d in range(KC):
                    for (tb0, n) in TBS:
                        j = 0 if tb0 < NOWN else 1
                        pk = 4 + rr(4, 'pk4b')
                        for m in range(4):
                            t.op('pe', lambda e, m=m, d=d, pk=pk, tb0=tb0, n=n: e.matmul(
                                ps[pk][:, :n], lhsT=wB[s][:, m, d * P:(d + 1) * P], rhs=uT[s][:, m, tb0:tb0 + n],
                                start=(m == 0), stop=(m == 3)), r=('wB%d' % s, uTn[s]), w=('ps%d' % pk,))
                        t.op('dve', lambda e, d=d, pk=pk, tb0=tb0, n=n, j=j: e.scalar_tensor_tensor(
                            out=xo[:, d, tb0:tb0 + n], in0=ps[pk][:, :n], scalar=mod[:, G2 + d, j:j + 1], in1=xo[:, d, tb0:tb0 + n],
                            op0=ALU.mult, op1=ALU.add), r=('ps%d' % pk, 'xo', 'mod'), w=('xo',))
            t.barrier()

        if L0 and fused:
            x1o = carry['x1own'].ap().rearrange("(c p) t -> p c t", p=P)
            for hlf in range(2):
                t.dma('sp', x1o[:, hlf * 8:(hlf + 1) * 8, :], xo[:, hlf * 8:(hlf + 1) * 8, 0:NOWN], 'dram:x1own', src=('xo',))
            carry['xo'] = xo
            carry['post'] = post
        elif L0:
            for hlf in range(2):
                t.dma('sp', x1T[:, hlf * 8:(hlf + 1) * 8, :], xo[:, hlf * 8:(hlf + 1) * 8, 0:NOWN], 'out:x1', src=('xo',))
            t.dma('sp', c1T, xo[:, :, NOWN:NTOK], 'out:c1', src=('xo',))
        else:
            with Scope() as es:
                fn_t = sb(es, "fn_t", [P, KC], F32)
                t.dma('sp', fn_t[:], fnw, 'fn_t')
                for (tb0, n) in TBS:
                    for c in range(KC):
                        i = c % 4
                        t.op('act', lambda e, c=c, i=i, tb0=tb0, n=n: e.activation(out=sq[i][:, :n], in_=xo[:, c, tb0:tb0 + n], func=AF.Square),
                             r=('xo',), w=('sq%d' % i,))
                        t.op('pe', lambda e, c=c, i=i, n=n: e.matmul(ps[6][:, :n], lhsT=ones[:], rhs=sq[i][:, :n],
                                                                    start=(c == 0), stop=(c == KC - 1)),
                             r=('sq%d' % i, 'ones'), w=('ps6',))
                    rs, rsb = rstd_from(ps[6][:, :n], n, 1.0 / D, 'ps6')
                    for c in range(KC):
                        t.op('dve', lambda e, c=c, tb0=tb0, n=n: e.scalar_tensor_tensor(
                            out=xo[:, c, tb0:tb0 + n], in0=xo[:, c, tb0:tb0 + n], scalar=fn_t[:, c:c + 1], in1=rs,
                            op0=ALU.mult, op1=ALU.mult), r=('xo', rsb, 'fn_t'), w=('xo',))
                for hlf in range(2):
                    t.dma('sp', outT[:, hlf * 8:(hlf + 1) * 8, :], xo[:, hlf * 8:(hlf + 1) * 8, :], 'out:o', src=('xo',))
                t.barrier()
        if not (L0 and fused):
            t.barrier()
            post.close()

    if fused:
        ada_wq = [din0("ada_wq_%d" % L_, [D, 3072]).rearrange("(c p) n -> p c n", p=P) for L_ in range(2)]
        ada_bq = din0("ada_bq", [P, 48])
        normw2 = din0("normw2", [P, 2, 2, KC])
        msend = nc.dram_tensor("msend", [P, 96], F32)
        mgath = nc.dram_tensor("mgath", [4 * P, 96], F32)
        with Scope() as es:
            cv = sb(es, "cv", [P, KC, 2], F32)
            scv = sb(es, "scv", [P, KC, 2], BF16)
            adab = sb(es, "adab", [P, 48], F32)
            nw = sb(es, "nw", [P, 4, KC], F32)
            part = sb(es, "part", [P, 96], F32)
            wa = [sb(es, "wa%d" % i, [P, KC, 1024], BF16) for i in range(2)]
            t.dma('sp', cv[:], cvec, 'cv')
            t.dma('sp', adab[:], ada_bq, 'adab')
            t.dma('sp', nw[:], normw2.rearrange("p l s c -> p (l s) c"), 'nw')
            t.op('act', lambda e: e.activation(out=scv[:], in_=cv[:], func=AF.Silu), r=('cv',), w=('scv',))
            psm = ps[7]
            gi = 0
            for L_ in range(2):
                for g in range(3):
                    s_ = gi % 2
                    gi += 1
                    for hlf in range(2):
                        t.dma('pool', wa[s_][:, hlf * 8:(hlf + 1) * 8, :],
                              ada_wq[L_][:, hlf * 8:(hlf + 1) * 8, g * 1024:(g + 1) * 1024], 'wa%d' % s_)
                    for cl in range(8):
                        cc = L_ * 24 + g * 8 + cl
                        for kc in range(KC):
                            t.op('pe', lambda e: e.matmul(
                                psm[:, 2 * cc:2 * cc + 2], lhsT=wa[s_][:, kc, cl * P:(cl + 1) * P], rhs=scv[:, kc, :],
                                start=(kc == 0), stop=(kc == KC - 1)), r=('wa%d' % s_, 'scv'), w=('ps7',))
            psm3 = psm[:, 0:96].rearrange("p (c j) -> p c j", j=2)
            part3 = part[:].rearrange("p (c j) -> p c j", j=2)
            for j in range(2):
                t.op('dve', lambda e: e.tensor_tensor(out=part3[:, :, j], in0=psm3[:, :, j], in1=adab[:, :], op=ALU.add),
                     r=('ps7', 'adab'), w=('part',))
            t.dma('sp', msend.ap(), part[:], 'dram:msend', src=('part',))
            t.collective(lambda e: e.collective_compute("AllGather", ALU.bypass, replica_groups=[[0, 1, 2, 3], [4, 5, 6, 7]],
                                                        ins=[msend.ap().opt()], outs=[mgath.ap().opt()]),
                         'dram:mgath', src=('dram:msend',))
            mg = mgath.ap().rearrange("(r p) f -> p r f", p=P)
            for L_ in range(2):
                md = modsL[L_]['mod']
                t.dma('sp', md[:].rearrange("p (r c) j -> p r (c j)", r=4), mg[:, :, L_ * 48:(L_ + 1) * 48], 'mod', src=('dram:mgath',))
            for L_ in range(2):
                md = modsL[L_]['mod']
                for j in range(2):
                    t.op('dve', lambda e: e.scalar_tensor_tensor(out=modsL[L_]['se1'][:, :, j], in0=md[:, 16:32, j], scalar=1.0,
                                                                 in1=nw[:, L_ * 2, :], op0=ALU.add, op1=ALU.mult),
                         r=('mod', 'nw'), w=('se1',))
                    t.op('dve', lambda e: e.scalar_tensor_tensor(out=modsL[L_]['se2'][:, :, j], in0=md[:, 64:80, j], scalar=1.0,
                                                                 in1=nw[:, L_ * 2 + 1, :], op0=ALU.add, op1=ALU.mult),
                         r=('mod', 'nw'), w=('se2',))
            t.barrier()

    for layer_ in layers:
        emit_layer(layer_)
    t.barrier()
    top.close()
    t.es.close()
    return nc


def _rope_tables():
    tt = np.arange(SEQ)
    row = (tt // 64).astype(np.float32)
    col = (tt % 64).astype(np.float32)
    inv = (np.float32(10000.0) ** (-np.arange(32, dtype=np.float32) / np.float32(32))).astype(np.float32)
    ang_r = row[:, None] * inv
    ang_c = col[:, None] * inv
    ang = np.concatenate([ang_r, ang_r, ang_c, ang_c], axis=-1).astype(np.float32)
    cos = np.cos(ang).astype(np.float32)
    sin = np.sin(ang).astype(np.float32)
    sgn = np.ones(128, np.float32)
    sgn[0:32] = -1.0
    sgn[64:96] = -1.0
    return cos.T.copy(), (sin * sgn[None, :]).T.copy()


def _rot_matrix():
    R = np.zeros((128, 128), np.float32)
    for base in (0, 64):
        for d in range(32):
            R[base + d + 32, base + d] = 1.0
            R[base + d, base + d + 32] = 1.0
    return R


def _band_tables(qt):
    p = np.arange(128)[:, None]
    f = np.arange(128)[None, :]
    m = np.zeros((128, 384), np.float32)
    m[:, 0:128] = np.where(p <= f, 0.0, NEG)
    m[:, 256:384] = np.where(p >= f, 0.0, NEG)
    full = np.full((128, 384), NEG, np.float32)
    out = np.stack([m, m if qt > 0 else full, m if qt < 3 else full], axis=1)
    return np.ascontiguousarray(out.astype(np.float32))


def _natten_bias(rpb, qt):
    R0 = 16 * qt
    H = rpb.shape[0]
    out = np.full((H, 128, 29, 128), NEG, np.float32)

    def tset(p):
        if p in (0, 1):
            return list(range(0, 6)), (0 if p == 0 else 6), 0
        if p in (6, 7):
            return list(range(-1, 5)), (17 if p == 6 else 23), -1
        return list(range(0, 5)), 12, 0
    w = np.arange(64)
    cq = np.arange(64)
    cs = np.clip(cq - 8, 0, 48)
    col_valid = (w[None, :] >= cs[:, None]) & (w[None, :] < cs[:, None] + 16)
    ci = np.clip(w[None, :] - cq[:, None] + 15, 0, 30)
    for p in (0, 1, 2, 6, 7):
        ts, base, t0 = tset(p)
        r0 = R0 + 2 * p
        for tt in ts:
            slot = base + (tt - t0)
            for a in range(2):
                rq = r0 + a
                rs = min(max(rq - 4, 0), 56)
                for b in range(2):
                    rk = r0 - 4 + 2 * tt + b
                    if rk < 0 or rk >= 64 or rk < rs or rk >= rs + 8:
                        continue
                    ri = rk - rq + 7
                    vals = rpb[:, ri, :][:, ci]
                    vals = np.where(col_valid[None], vals, NEG)
                    out[:, b * 64:(b + 1) * 64, slot, a * 64:(a + 1) * 64] = np.transpose(vals, (0, 2, 1))
    return out


_NC_CACHE = {}


def _get_nc(layer):
    if layer not in _NC_CACHE:
        _NC_CACHE[layer] = build(layer)
    return _NC_CACHE[layer]


def _fm(a):
    return np.ascontiguousarray(a.T)


def layer_inputs(layer, core, x, ctx, c, c_ctx, ada_w, ada_b, norm_w, mlp_w1, mlp_w2, w_in, w_out, extra):
    b, qt = core // 4, core % 4
    m = {}
    m["cvec"] = np.ascontiguousarray(np.stack([c[b], c_ctx], axis=1).astype(np.float32))
    m["ada_w"] = ada_w[layer]
    m["ada_b"] = np.ascontiguousarray(ada_b[layer].reshape(96, 128).T)
    m["normw"] = np.ascontiguousarray(norm_w[layer].reshape(2, KC, 128).transpose(2, 0, 1))
    m["w_in"] = w_in
    m["w_out"] = w_out
    m["w1"] = mlp_w1[layer]
    m["w2"] = mlp_w2[layer]
    if layer == 0:
        shift = 1024 * qt - 128
        m["xT"] = _fm(np.roll(x[b], -shift, axis=0))
        m["cT"] = _fm(np.roll(ctx[b], -64 * qt, axis=0))
        cosT, sinT = extra["rope"]
        m["cosT"] = np.ascontiguousarray(np.roll(cosT, -shift, axis=1))
        m["sinT"] = np.ascontiguousarray(np.roll(sinT, -shift, axis=1))
        m["rotm"] = extra["rotm"]
        m["bandb"] = _band_tables(qt)
        m["qkw"] = np.ascontiguousarray(np.stack([extra["qn"], extra["kn"]], axis=1).astype(np.float32))
        m["sinkb"] = np.ascontiguousarray(np.broadcast_to(extra["sink"][None, :], (128, 8)).astype(np.float32))
    else:
        xe = np.zeros((1536, D), np.float32)
        lo = 1024 * qt - 256
        hi = lo + 1536
        slo, shi = max(lo, 0), min(hi, SEQ)
        xe[slo - lo:shi - lo] = x[b][slo:shi]
        m["xT"] = _fm(xe)
        m["cT"] = _fm(ctx[b])
        m["biasT"] = _natten_bias(extra["rpb"], qt)
        m["fnw"] = np.ascontiguousarray(extra["fnw"].reshape(KC, 128).T)
    return m


FUSED = True


def kernel(x, c, ctx, c_ctx, ada_w, ada_b, norm_w, mlp_w1, mlp_w2, ev_w_in, ev_w_out,
           ev_q_norm, ev_k_norm, ev_sink, od_w_in, od_w_out, od_rpb, final_norm_w):
    f = lambda a: np.asarray(a, dtype=np.float32)
    x, c, ctx, c_ctx = f(x), f(c), f(ctx), f(c_ctx)
    ada_w, ada_b, norm_w, mlp_w1, mlp_w2 = f(ada_w), f(ada_b), f(norm_w), f(mlp_w1), f(mlp_w2)
    cores = list(range(8))
    cosT, sinT = _rope_tables()
    ex0 = dict(rope=(cosT, sinT), rotm=_rot_matrix(), qn=f(ev_q_norm)[0], kn=f(ev_k_norm)[0], sink=f(ev_sink)[0])
    ex1 = dict(rpb=f(od_rpb)[0], fnw=f(final_norm_w))
    out = np.empty_like(x)
    if FUSED:
        in_maps = []
        for k in cores:
            b, qt = k // 4, k % 4
            m0 = layer_inputs(0, k, x, ctx, c, c_ctx, ada_w, ada_b, norm_w, mlp_w1, mlp_w2, f(ev_w_in)[0], f(ev_w_out)[0], ex0)
            m = {"cvec": m0.pop("cvec")}
            for kk in ("ada_w", "ada_b", "normw"):
                m0.pop(kk)
            for kk, v in m0.items():
                m[kk + "_0"] = v
            for L_ in range(2):
                m["ada_wq_%d" % L_] = np.ascontiguousarray(ada_w[L_][:, 3072 * qt:3072 * (qt + 1)])
            m["ada_bq"] = np.ascontiguousarray(
                ada_b.reshape(2, 96, 128)[:, 24 * qt:24 * (qt + 1), :].transpose(2, 0, 1).reshape(128, 48))
            m["normw2"] = np.ascontiguousarray(norm_w.reshape(2, 2, KC, 128).transpose(3, 0, 1, 2))
            m["w_in_1"] = f(od_w_in)[0]
            m["w_out_1"] = f(od_w_out)[0]
            m["w1_1"] = mlp_w1[1]
            m["w2_1"] = mlp_w2[1]
            m["biasT_1"] = _natten_bias(ex1["rpb"], qt)
            m["fnw_1"] = np.ascontiguousarray(ex1["fnw"].reshape(KC, 128).T)
            sel = np.zeros((128, 8), np.float32)
            if qt > 0:
                sel[:, qt - 1] = 1.0
            if qt < 3:
                sel[:, 4 + qt + 1] = 1.0
            m["selv_1"] = sel
            in_maps.append(m)
        r = run_bass_kernel_spmd(_get_nc('fused'), in_maps, core_ids=cores).results
        for k in cores:
            b, qt = k // 4, k % 4
            out[b, 1024 * qt:1024 * (qt + 1)] = r[k]["outT"].T
        return out
    in0 = [layer_inputs(0, k, x, ctx, c, c_ctx, ada_w, ada_b, norm_w, mlp_w1, mlp_w2, f(ev_w_in)[0], f(ev_w_out)[0], ex0)
           for k in cores]
    r0 = run_bass_kernel_spmd(_get_nc(0), in0, core_ids=cores).results
    x1 = np.empty_like(x)
    ctx1 = np.empty_like(ctx)
    for k in cores:
        b, qt = k // 4, k % 4
        x1[b, 1024 * qt:1024 * (qt + 1)] = r0[k]["x1T"].T
        ctx1[b, 64 * qt:64 * (qt + 1)] = r0[k]["c1T"].T
    in1 = [layer_inputs(1, k, x1, ctx1, c, c_ctx, ada_w, ada_b, norm_w, mlp_w1, mlp_w2, f(od_w_in)[0], f(od_w_out)[0], ex1)
           for k in cores]
    r1 = run_bass_kernel_spmd(_get_nc(1), in1, core_ids=cores).results
    for k in cores:
        b, qt = k // 4, k % 4
        out[b, 1024 * qt:1024 * (qt + 1)] = r1[k]["outT"].T
    return out
```
